# Optimizing a Trainium2 kernel written in Bass

```python
import jax, jax.numpy as jnp
from jax import lax
import numpy as np

D_MODEL = 1024
BATCH = 2
SEQ = 8192
DEPTH = 2

EXPAND = 2
D_INNER = EXPAND * D_MODEL
EPS = 1e-6
NEG = -1e30

A_WIDTH = D_INNER // 2
POOL_SIZES = (2, 4, 8, 16)
A_GROUPS = len(POOL_SIZES)
A_GROUP_DIM = A_WIDTH // A_GROUPS
B_WIDTH = D_INNER - A_WIDTH
B_GROUPS = 4
B_GROUP_DIM = B_WIDTH // B_GROUPS
CHUNK = 128
EVEN_IN = 2 * A_WIDTH + 3 * B_WIDTH

DILATED = ((128, 1), (512, 4), (2048, 16))
N_DIL = len(DILATED)
HEAD_DIM = 128
C_SLOTS = 8
C_HEADS = C_SLOTS * N_DIL
C_WIDTH = C_SLOTS * HEAD_DIM
D_WIDTH = D_INNER - C_WIDTH
CONV_W = 3
ATTN_BLOCK = 128
QKV_WIDTH = 3 * C_HEADS * HEAD_DIM
ODD_IN = QKV_WIDTH + C_WIDTH + 4 * D_WIDTH

N_EVEN = (DEPTH + 1) // 2
N_ODD = DEPTH // 2

kernel_name = "hybrid_pool_gmlp_dilattn_shortconv"


def rmsnorm(x, g):
    xf = x.astype(jnp.float32)
    y = xf * lax.rsqrt(jnp.mean(xf * xf, axis=-1, keepdims=True) + EPS)
    return (y * g.astype(jnp.float32)).astype(x.dtype)


def alibi_slopes(n):
    return jnp.asarray(2.0 ** (-8.0 * (np.arange(n) + 1) / n), dtype=jnp.float32)


def multiscale_pool(a, pool_w, pool_scale):
    Bn, S, _ = a.shape
    ag = a.reshape(Bn, S, A_GROUPS, A_GROUP_DIM).astype(jnp.float32)
    cs = jnp.pad(jnp.cumsum(ag, axis=1), ((0, 0), (1, 0), (0, 0), (0, 0)))
    t = jnp.arange(S)
    means = []
    for g, w in enumerate(POOL_SIZES):
        lo = jnp.maximum(t + 1 - w, 0)
        cnt = (t + 1 - lo).astype(jnp.float32)
        means.append((cs[:, 1:, g] - cs[:, lo, g]) / cnt[None, :, None])
    pooled = (jnp.stack(means, axis=2) - ag).astype(a.dtype)
    mixed = jnp.einsum('bsgc,gcd->bsgd', pooled, pool_w)
    return mixed.reshape(Bn, S, A_WIDTH) * pool_scale


def chunk_spatial_gate(u, v, ws, bs):
    Bn, S, _ = v.shape
    nc = S // CHUNK
    vg = v.reshape(Bn, nc, CHUNK, B_GROUPS, B_GROUP_DIM)
    causal = jnp.tril(jnp.ones((CHUNK, CHUNK), dtype=bool))
    w = jnp.where(causal[None], ws, jnp.zeros_like(ws))
    mixed = jnp.einsum('gts,bnsgc->bntgc', w, vg) + bs.T[None, None, :, :, None]
    return u * mixed.reshape(Bn, S, B_WIDTH)


def dilated_group_attention(q, k, v, window, dilation, slopes):
    Bn, S, H, Dh = q.shape
    unit = dilation * ATTN_BLOCK
    Sp = -(-S // unit) * unit
    pad = Sp - S
    L = Sp // dilation
    nb = L // ATTN_BLOCK
    span = window // dilation

    def to_sub(x):
        x = jnp.pad(x, ((0, 0), (0, pad), (0, 0), (0, 0)))
        return x.reshape(Bn, L, dilation, H, Dh).transpose(0, 2, 1, 3, 4)

    def band(x):
        x = jnp.pad(x, ((0, 0), (0, 0), (ATTN_BLOCK, 0), (0, 0), (0, 0)))
        x = x.reshape(Bn, dilation, nb + 1, ATTN_BLOCK, H, Dh)
        return jnp.concatenate([x[:, :, :-1], x[:, :, 1:]], axis=3)

    qb = to_sub(q).reshape(Bn, dilation, nb, ATTN_BLOCK, H, Dh)
    kb = band(to_sub(k))
    vb = band(to_sub(v))
    s = jnp.einsum('bdnqhe,bdnkhe->bdnhqk', qb, kb,
                   preferred_element_type=jnp.float32) * (Dh ** -0.5)
    qi = jnp.arange(ATTN_BLOCK)[:, None] + ATTN_BLOCK
    ki = jnp.arange(2 * ATTN_BLOCK)[None, :]
    steps = qi - ki
    band_ok = (steps >= 0) & (steps <= span)
    blk = jnp.arange(nb)[:, None, None]
    valid = band_ok[None] & ((blk > 0) | (ki >= ATTN_BLOCK)[None])
    dist = (steps * dilation).astype(jnp.float32)
    s = s - slopes[:, None, None] * dist
    s = jnp.where(valid[:, None], s, NEG)
    lse = jax.nn.logsumexp(s, axis=-1)
    p = jnp.exp(s - lse[..., None])
    o = jnp.einsum('bdnhqk,bdnkhe->bdnqhe', p.astype(v.dtype), vb)
    o = o.reshape(Bn, dilation, L, H, Dh).transpose(0, 2, 1, 3, 4).reshape(Bn, Sp, H, Dh)[:, :S]
    lse = lse.transpose(0, 1, 2, 4, 3).reshape(Bn, dilation, L, H)
    lse = lse.transpose(0, 2, 1, 3).reshape(Bn, Sp, H)[:, :S]
    return o, lse


def short_gated_conv(gb, gc, xt, conv_w):
    S = xt.shape[1]
    z = jnp.pad(gc * xt, ((0, 0), (CONV_W - 1, 0), (0, 0)))
    conv = conv_w[0] * z[:, 0:S]
    for j in range(1, CONV_W):
        conv = conv + conv_w[j] * z[:, j:j + S]
    return gb * conv


def even_layer(x, norm_g, w_in, pool_w, pool_scale, ws, bs, w_out):
    h = rmsnorm(x, norm_g)
    z = h @ w_in
    a, g_a, u, v, g_b = jnp.split(
        z, [A_WIDTH, 2 * A_WIDTH, 2 * A_WIDTH + B_WIDTH, 2 * A_WIDTH + 2 * B_WIDTH], axis=-1)
    y_a = multiscale_pool(a, pool_w, pool_scale) * jax.nn.silu(g_a)
    y_b = chunk_spatial_gate(u, v, ws, bs) * jax.nn.silu(g_b)
    return x + jnp.concatenate([y_a, y_b], axis=-1) @ w_out


def odd_layer(x, norm_g, w_in, conv_w, w_out):
    Bn, S, _ = x.shape
    h = rmsnorm(x, norm_g)
    z = h @ w_in
    o1 = QKV_WIDTH
    o2 = o1 + C_WIDTH
    qkv, g_c, d_b, d_c, d_x, g_d = jnp.split(
        z, [o1, o2, o2 + D_WIDTH, o2 + 2 * D_WIDTH, o2 + 3 * D_WIDTH], axis=-1)
    qkv = qkv.reshape(Bn, S, 3, N_DIL, C_SLOTS, HEAD_DIM)
    slopes = alibi_slopes(C_HEADS).reshape(N_DIL, C_SLOTS)
    outs, lses = [], []
    for gi, (window, dil) in enumerate(DILATED):
        o, l = dilated_group_attention(qkv[:, :, 0, gi], qkv[:, :, 1, gi], qkv[:, :, 2, gi],
                                       window, dil, slopes[gi])
        outs.append(o)
        lses.append(l)
    alpha = jax.nn.softmax(jnp.stack(lses, axis=0), axis=0)
    y_c = jnp.einsum('gbsh,gbshe->bshe', alpha.astype(x.dtype), jnp.stack(outs, axis=0))
    y_c = y_c.reshape(Bn, S, C_WIDTH) * jax.nn.silu(g_c)
    y_d = short_gated_conv(d_b, d_c, d_x, conv_w) * jax.nn.silu(g_d)
    return x + jnp.concatenate([y_c, y_d], axis=-1) @ w_out


def setup_inputs(seed: int = 0) -> dict:
    key = jax.random.key(seed)
    ks = jax.random.split(key, 16)
    nrm = jax.random.normal
    f32 = jnp.float32
    return {
        "x": nrm(ks[0], (BATCH, SEQ, D_MODEL), f32),
        "even_norm": 1.0 + 0.05 * nrm(ks[1], (N_EVEN, D_MODEL), f32),
        "even_w_in": nrm(ks[2], (N_EVEN, D_MODEL, EVEN_IN), f32) * D_MODEL ** -0.5,
        "even_pool_w": nrm(ks[3], (N_EVEN, A_GROUPS, A_GROUP_DIM, A_GROUP_DIM), f32) * A_GROUP_DIM ** -0.5,
        "even_pool_scale": 1.0 + 0.1 * nrm(ks[4], (N_EVEN, A_WIDTH), f32),
        "even_ws": nrm(ks[5], (N_EVEN, B_GROUPS, CHUNK, CHUNK), f32) * CHUNK ** -0.5,
        "even_bs": 1.0 + 0.1 * nrm(ks[6], (N_EVEN, B_GROUPS, CHUNK), f32),
        "even_w_out": nrm(ks[7], (N_EVEN, D_INNER, D_MODEL), f32) * D_INNER ** -0.5,
        "odd_norm": 1.0 + 0.05 * nrm(ks[8], (N_ODD, D_MODEL), f32),
        "odd_w_in": nrm(ks[9], (N_ODD, D_MODEL, ODD_IN), f32) * D_MODEL ** -0.5,
        "odd_conv_w": nrm(ks[10], (N_ODD, CONV_W, D_WIDTH), f32) * CONV_W ** -0.5,
        "odd_w_out": nrm(ks[11], (N_ODD, D_INNER, D_MODEL), f32) * D_INNER ** -0.5,
        "final_norm": 1.0 + 0.05 * nrm(ks[12], (D_MODEL,), f32),
    }


def reference(x, even_norm, even_w_in, even_pool_w, even_pool_scale, even_ws, even_bs,
              even_w_out, odd_norm, odd_w_in, odd_conv_w, odd_w_out, final_norm):
    h = x
    for layer in range(DEPTH):
        i = layer // 2
        if layer % 2 == 0:
            h = even_layer(h, even_norm[i], even_w_in[i], even_pool_w[i], even_pool_scale[i],
                           even_ws[i], even_bs[i], even_w_out[i])
        else:
            h = odd_layer(h, odd_norm[i], odd_w_in[i], odd_conv_w[i], odd_w_out[i])
    return rmsnorm(h, final_norm)
```

```python
import numpy as np
from contextlib import ExitStack
import concourse.bass as bass
import concourse.mybir as mybir
from concourse.bass_utils import run_bass_kernel_spmd

F32 = mybir.dt.float32
BF16 = mybir.dt.bfloat16
AF = mybir.ActivationFunctionType
ALU = mybir.AluOpType

D = 1024
SEQ = 8192
NB = 2
OWN = 2048
HALO = 2048
PRE = 16
TH = PRE + HALO + OWN
TA = 256
EPS = 1e-6
POOLS = (2, 4, 8, 16)
DILS = (1, 4, 16)
NDS = 16


class Sem:
    __slots__ = ("h", "i")

    def __init__(self, h, i):
        self.h = h
        self.i = i


class Buf:
    __slots__ = ("t", "writer", "readers")

    def __init__(self, t=None):
        self.t = t
        self.writer = None
        self.readers = []

    def __getitem__(self, k):
        return self.t[k]


class Eng:
    def __init__(self, S, name, e, is_pe=False):
        self.e = e
        self.name = name
        self.sem = S.newsem("s_" + name)
        self.cnt = 0
        self.known = {}
        self.is_pe = is_pe


class Sched:
    def __init__(self, nc):
        self.nc = nc
        self.nsem = 0
        self.pe = Eng(self, "pe", nc.tensor, True)
        self.act = Eng(self, "act", nc.scalar)
        self.dve = Eng(self, "dve", nc.vector)
        self.pool = Eng(self, "pool", nc.gpsimd)
        self.sp = Eng(self, "sp", nc.sync)
        self.engs = [self.pe, self.act, self.dve, self.pool, self.sp]
        self.dsems = [self.newsem(f"d{i}") for i in range(2 * NDS)]
        self.dcnt = [0] * (2 * NDS)
        self.dnext = [0, 0]
        self.nbank = 0
        self.banks = []

    def newsem(self, name):
        s = Sem(self.nc.alloc_semaphore(name), self.nsem)
        self.nsem += 1
        return s

    def _wait(self, eng, tok):
        sem, val = tok
        if eng.known.get(sem.i, 0) >= val:
            return
        eng.e.wait_ge(sem.h, val)
        eng.known[sem.i] = val

    def _deps(self, eng, reads, writes):
        for b in reads:
            if b.writer is not None:
                if b.writer[0] is eng.sem and eng.is_pe:
                    continue
                self._wait(eng, b.writer)
        for b in writes:
            if b.writer is not None and not (b.writer[0] is eng.sem and eng.is_pe):
                self._wait(eng, b.writer)
            for r in b.readers:
                if not (r[0] is eng.sem and eng.is_pe):
                    self._wait(eng, r)

    def _record(self, tok, reads, writes):
        for b in reads:
            b.readers.append(tok)
        for b in writes:
            b.writer = tok
            b.readers = []

    def op(self, eng, fn, reads=(), writes=(), mark=True):
        self._deps(eng, reads, writes)
        ins = fn()
        tok = (eng.sem, eng.cnt + 1)
        if mark:
            ins.then_inc(eng.sem.h, 1)
            eng.cnt += 1
        self._record(tok, reads, writes)
        return ins

    def dma(self, q, out, in_, reads=(), writes=()):
        self._deps(q, reads, writes)
        ring = 1 if q is self.pool else 0
        i = ring * NDS + self.dnext[ring]
        self.dnext[ring] = (self.dnext[ring] + 1) % NDS
        if self.dcnt[i] > 0:
            self._wait(q, (self.dsems[i], self.dcnt[i]))
        ins = q.e.dma_start(out=out, in_=in_)
        self.dcnt[i] += 16
        ins.then_inc(self.dsems[i].h, 16)
        self._record((self.dsems[i], self.dcnt[i]), reads, writes)

    def barrier(self):
        for e in self.engs:
            for f in self.engs:
                if f is not e and f.cnt > 0:
                    self._wait(e, (f.sem, f.cnt))
            for i in range(2 * NDS):
                if self.dcnt[i] > 0:
                    self._wait(e, (self.dsems[i], self.dcnt[i]))

    def bank(self):
        b = self.banks[self.nbank % len(self.banks)]
        self.nbank += 1
        return b


def build_program(phases="AB"):
    nc = bass.Bass("TRN2", target_bir_lowering=False)
    S = Sched(nc)
    pe, act, dve, pool, sp = S.pe, S.act, S.dve, S.pool, S.sp
    fused = phases == "AB"

    def dram(name, shape, dt, kind):
        return nc.dram_tensor(name, list(shape), dt, kind=kind).ap()

    mid_kind_out = "Internal" if fused else "ExternalOutput"
    mid_kind_in = "Internal" if fused else "ExternalInput"
    if "A" in phases:
        xT = dram("xT", [D, TH], F32, "ExternalInput")
        e_win = dram("e_win", [D, 5120], F32, "ExternalInput")
        e_pw = dram("e_pw", [4, 2, 128, 256], F32, "ExternalInput")
        e_wsT = dram("e_wsT", [128, 4, 128], F32, "ExternalInput")
        e_tril = dram("e_tril", [128, 4, 128], F32, "ExternalInput")
        e_bs = dram("e_bs", [1, 512], F32, "ExternalInput")
        e_wout = dram("e_wout", [2048, D], F32, "ExternalInput")
        vecs = dram("vecs", [128, 32], F32, "ExternalInput")
        invc = dram("invc", [128, 128], F32, "ExternalInput")
        h1T_d = dram("h1T", [D, OWN], F32, mid_kind_out)
        hnT_d = dram("hnT", [D, HALO + OWN], BF16, mid_kind_out)
    else:
        h1T_d = dram("h1T", [D, OWN], F32, mid_kind_in)
        hnT_d = dram("hnT", [D, HALO + OWN], BF16, mid_kind_in)
    if "B" in phases:
        o_wqkv = dram("o_wqkv", [24, D, 384], F32, "ExternalInput")
        o_wgc = dram("o_wgc", [8, D, 128], F32, "ExternalInput")
        o_wd = dram("o_wd", [8, D, 512], F32, "ExternalInput")
        o_wout = dram("o_wout", [2048, D], F32, "ExternalInput")
        o_cw = dram("o_cw", [128, 24], F32, "ExternalInput")
        o_mask = dram("o_mask", [24, 2, 128, 256], F32, "ExternalInput")
        o_ident = dram("o_ident", [128, 128], F32, "ExternalInput")
        vecsB = dram("vecsB", [128, 32], F32, "ExternalInput")
        outT = dram("outT", [D, OWN], F32, "ExternalOutput")

    S.banks = [Buf(nc.alloc_psum_tensor(f"bank{i}", [128, 512], F32)) for i in range(8)]

    epsb = Buf(nc.alloc_sbuf_tensor("epsb", [128, 1], F32))
    S.op(dve, lambda: nc.vector.memset(epsb[:, :], EPS), writes=[epsb])

    def rstd_from(bk, rs_buf, w):
        S.op(act, lambda: nc.scalar.activation(out=rs_buf[:, 0:w], in_=bk[:, 0:w], func=AF.Sqrt, bias=epsb[:, 0:1],
                                               scale=1.0 / D), reads=[bk, epsb], writes=[rs_buf])
        S.op(dve, lambda: nc.vector.reciprocal(out=rs_buf[:, 0:w], in_=rs_buf[:, 0:w]), reads=[rs_buf], writes=[rs_buf])

    if "A" in phases:
        with ExitStack() as es:
            def sb(name, shape, dt):
                return Buf(es.enter_context(nc.sbuf_tensor(name, list(shape), dt)))

            WI = [sb(f"WI{i}", [128, 8, 1024], BF16) for i in range(5)]
            WO = sb("WO", [128, 16, D], BF16)
            PW = sb("PW", [128, 4, 2, 256], BF16)
            wsT_f = sb("wsT_f", [128, 4, 128], F32)
            tril_f = sb("tril_f", [128, 4, 128], F32)
            wsTm = sb("wsTm", [128, 4, 128], BF16)
            bsr = sb("bsr", [1, 512], BF16)
            ones_row = sb("ones_row", [1, 128], BF16)
            ones_f = sb("ones_f", [128, 128], F32)
            vec = sb("vec", [128, 32], F32)
            invc_s = sb("invc_s", [128, 128], F32)
            xt = [sb(f"xt{i}", [128, 8, TA], F32) for i in range(3)]
            sq = [sb(f"sq{i}", [128, TA], F32) for i in range(2)]
            rs = sb("rs", [128, TA], F32)
            rs2 = sb("rs2", [128, TA], F32)
            hT = [sb(f"hT{i}", [128, 8, TA], BF16) for i in range(2)]
            aT = [sb(f"aT{i}", [128, PRE + TA], F32) for i in range(8)]
            Tm = [sb(f"Tm{i}", [128, PRE + TA], F32) for i in range(2)]
            pooled = [sb(f"pooled{i}", [128, TA], BF16) for i in range(8)]
            accN = sb("accN", [128, TA], F32)
            accO = sb("accO", [128, TA], F32)
            tmp16 = sb("tmp16", [128, 16], F32)
            sga = [sb(f"sga{i}", [128, TA], F32) for i in range(2)]
            sgb = [sb(f"sgb{i}", [128, TA], F32) for i in range(2)]
            ugb = [sb(f"ugb{i}", [128, TA], F32) for i in range(2)]
            vtm = [sb(f"vtm{i}", [128, 1024], BF16) for i in range(TA // 128)]
            yT = [sb(f"yT{i}", [128, 16, TA], BF16) for i in range(2)]
            hn = [sb(f"hn{i}", [128, 8, TA], BF16) for i in range(1)]

            S.dma(sp, vec[:, :], vecs, writes=[vec])
            S.dma(sp, invc_s[:, :], invc, writes=[invc_s])
            S.dma(sp, wsT_f[:, :, :], e_wsT, writes=[wsT_f])
            S.dma(sp, tril_f[:, :, :], e_tril, writes=[tril_f])
            S.op(dve, lambda: nc.vector.memset(ones_f[:, :], 1.0), writes=[ones_f])
            S.op(dve, lambda: nc.vector.memset(ones_row[:, :], 1.0), writes=[ones_row])
            S.op(dve, lambda: nc.vector.tensor_tensor(out=wsTm[:, :, :], in0=wsT_f[:, :, :], in1=tril_f[:, :, :],
                                                     op=ALU.mult), reads=[wsT_f, tril_f], writes=[wsTm])
            S.dma(pool, bsr[:, :], e_bs, writes=[bsr])
            win_v = e_win.rearrange("(k p) c -> p k c", p=128)
            S.dma(pool, WI[0][:, :, :], win_v[:, :, 0:1024], writes=[WI[0]])
            S.dma(pool, PW[:, :, :, :], e_pw.rearrange("g cc p d -> p g cc d"), writes=[PW])
            for i in (1, 3, 2, 4):
                S.dma(pool, WI[i][:, :, :], win_v[:, :, i * 1024:(i + 1) * 1024], writes=[WI[i]])
            S.dma(pool, WO[:, :, :], e_wout.rearrange("(k p) c -> p k c", p=128), writes=[WO])

            xT_v = xT.rearrange("(k p) t -> p k t", p=128)
            h1T_v = h1T_d.rearrange("(k p) t -> p k t", p=128)
            hnT_v = hnT_d.rearrange("(k p) t -> p k t", p=128)
            gE = lambda k: vec[:, k:k + 1]
            gO = lambda k: vec[:, 8 + k:9 + k]
            psc = lambda k: vec[:, 16 + k:17 + k]

            ntile = 1 + (HALO + OWN) // TA

            def tile_geom(j):
                if j == 0:
                    return 0, PRE
                return PRE + (j - 1) * TA, TA

            def sumsq1(src_buf, w, accb):
                for k in range(8):
                    q_ = sq[k % 2]
                    S.op(act, lambda: nc.scalar.activation(out=q_[:, 0:w], in_=src_buf[:, k, 0:w], func=AF.Square),
                         reads=[src_buf], writes=[q_])
                    if k == 1:
                        S.op(pool, lambda: nc.gpsimd.tensor_tensor(out=accb[:, 0:w], in0=sq[0][:, 0:w], in1=sq[1][:, 0:w], op=ALU.add),
                             reads=[sq[0], sq[1]], writes=[accb])
                    elif k > 1:
                        S.op(pool, lambda: nc.gpsimd.tensor_tensor(out=accb[:, 0:w], in0=accb[:, 0:w], in1=q_[:, 0:w], op=ALU.add),
                             reads=[accb, q_], writes=[accb])

            def sumsq2(accb, w):
                bk = S.bank()
                S.op(pe, lambda: nc.tensor.matmul(bk[:, 0:w], lhsT=ones_f[:, :], rhs=accb[:, 0:w], start=True, stop=True),
                     reads=[ones_f, accb], writes=[bk], mark=True)
                return bk

            def stageN1(j):
                col0, w = tile_geom(j)
                x_ = xt[j % 3]
                S.dma(sp, x_[:, :, 0:w], xT_v[:, :, col0:col0 + w], writes=[x_])
                sumsq1(x_, w, accN)

            def stageN2(j):
                col0, w = tile_geom(j)
                x_ = xt[j % 3]
                bk = sumsq2(accN, w)
                rstd_from(bk, rs, w)
                h_ = hT[j % 2]
                for k in range(8):
                    S.op(dve, lambda: nc.vector.scalar_tensor_tensor(out=h_[:, k, 0:w], in0=x_[:, k, 0:w], scalar=gE(k),
                                                                    in1=rs[:, 0:w], op0=ALU.mult, op1=ALU.mult),
                         reads=[x_, rs, vec], writes=[h_])

            def proj_fm(Wb, c, h_, w):
                bk = S.bank()
                for k in range(8):
                    S.op(pe, lambda: nc.tensor.matmul(bk[:, 0:w], lhsT=Wb[:, k, c * 128:(c + 1) * 128], rhs=h_[:, k, 0:w],
                                                      start=(k == 0), stop=(k == 7)),
                         reads=[Wb, h_], writes=[bk], mark=(k == 7))
                return bk

            def stagePa(j):
                col0, w = tile_geom(j)
                h_ = hT[j % 2]
                for c in range(8):
                    bk = proj_fm(WI[0], c, h_, w)
                    dst0 = 0 if j == 0 else PRE
                    S.op(act, lambda: nc.scalar.activation(out=aT[c][:, dst0:dst0 + w], in_=bk[:, 0:w], func=AF.Copy),
                         reads=[bk], writes=[aT[c]])

            def stagePb(j):
                col0, w = tile_geom(j)
                h_ = hT[j % 2]
                y_ = yT[j % 2]
                W_ = PRE + w
                for gi in range(4):
                    wpool = POOLS[gi]
                    for cc in range(2):
                        c = 2 * gi + cc
                        A = aT[c]
                        cur = A
                        lo = 0
                        sh = 1
                        for step in range(gi + 1):
                            nxt = Tm[step % 2]
                            S.op(dve, lambda: nc.vector.tensor_tensor(out=nxt[:, lo + sh:W_], in0=cur[:, lo + sh:W_],
                                                                     in1=cur[:, lo:W_ - sh], op=ALU.add),
                                 reads=[cur], writes=[nxt])
                            cur = nxt
                            lo += sh
                            sh *= 2
                        pl = pooled[c]
                        S.op(dve, lambda: nc.vector.scalar_tensor_tensor(out=pl[:, 0:w], in0=cur[:, PRE:W_], scalar=1.0 / wpool,
                                                                        in1=A[:, PRE:W_], op0=ALU.mult, op1=ALU.subtract),
                             reads=[cur, A], writes=[pl])
                        if j in (1, 1 + HALO // TA):
                            pos = 0 if j == 1 else 1
                            o0 = (gi * 2 + pos) * 16
                            S.op(dve, lambda: nc.vector.tensor_tensor(out=tmp16[:, :], in0=cur[:, PRE:PRE + 16],
                                                                     in1=invc_s[:, o0:o0 + 16], op=ALU.mult),
                                 reads=[cur, invc_s], writes=[tmp16])
                            S.op(dve, lambda: nc.vector.tensor_tensor(out=pl[:, 0:16], in0=tmp16[:, :],
                                                                     in1=A[:, PRE:PRE + 16], op=ALU.subtract),
                                 reads=[tmp16, A, pl], writes=[pl])
                        S.op(dve, lambda: nc.vector.tensor_copy(out=A[:, 0:PRE], in_=A[:, w:w + PRE]),
                             reads=[A], writes=[A])
                for tc in range(w // 128):
                    for half in range(2):
                        bk = S.bank()
                        for k in range(8):
                            S.op(pe, lambda: nc.tensor.matmul(bk[:, :], lhsT=h_[:, k, tc * 128:(tc + 1) * 128],
                                                              rhs=WI[3][:, k, half * 512:(half + 1) * 512],
                                                              start=(k == 0), stop=(k == 7)),
                                 reads=[WI[3], h_], writes=[bk], mark=(k == 7))
                        S.op(act, lambda: nc.scalar.activation(out=vtm[tc][:, half * 512:(half + 1) * 512], in_=bk[:, :],
                                                               func=AF.Copy), reads=[bk], writes=[vtm[tc]])
                for cc in range(8):
                    g = cc // 2
                    bu = proj_fm(WI[2], cc, h_, w)
                    bg = proj_fm(WI[4], cc, h_, w)
                    bm = S.bank()
                    for tc in range(w // 128):
                        S.op(pe, lambda: nc.tensor.matmul(bm[:, tc * 128:(tc + 1) * 128], lhsT=vtm[tc][:, cc * 128:(cc + 1) * 128],
                                                          rhs=wsTm[:, g, :], start=True, stop=False),
                             reads=[vtm[tc], wsTm], writes=[bm], mark=False)
                        S.op(pe, lambda: nc.tensor.matmul(bm[:, tc * 128:(tc + 1) * 128], lhsT=ones_row[0:1, :],
                                                          rhs=bsr[0:1, g * 128:(g + 1) * 128], start=False, stop=True),
                             reads=[ones_row, bsr], writes=[bm], mark=(tc == w // 128 - 1))
                    sg = sgb[cc % 2]
                    ug = ugb[cc % 2]
                    S.op(act, lambda: nc.scalar.activation(out=sg[:, 0:w], in_=bg[:, 0:w], func=AF.Silu),
                         reads=[bg], writes=[sg])
                    S.op(dve, lambda: nc.vector.tensor_tensor(out=ug[:, 0:w], in0=bu[:, 0:w], in1=sg[:, 0:w], op=ALU.mult),
                         reads=[bu, sg], writes=[ug])
                    S.op(dve, lambda: nc.vector.tensor_tensor(out=y_[:, 8 + cc, 0:w], in0=bm[:, 0:w], in1=ug[:, 0:w], op=ALU.mult),
                         reads=[bm, ug], writes=[y_])
                for gi in range(4):
                    for dch in range(2):
                        co = 2 * gi + dch
                        bm = S.bank()
                        for cc in range(2):
                            S.op(pe, lambda: nc.tensor.matmul(bm[:, 0:w], lhsT=PW[:, gi, cc, dch * 128:(dch + 1) * 128],
                                                              rhs=pooled[2 * gi + cc][:, 0:w],
                                                              start=(cc == 0), stop=(cc == 1)),
                                 reads=[PW, pooled[2 * gi + cc]], writes=[bm], mark=(cc == 1))
                        bg = proj_fm(WI[1], co, h_, w)
                        sg = sga[co % 2]
                        S.op(act, lambda: nc.scalar.activation(out=sg[:, 0:w], in_=bg[:, 0:w], func=AF.Silu),
                             reads=[bg], writes=[sg])
                        S.op(dve, lambda: nc.vector.scalar_tensor_tensor(out=y_[:, co, 0:w], in0=bm[:, 0:w], scalar=psc(co),
                                                                        in1=sg[:, 0:w], op0=ALU.mult, op1=ALU.mult),
                             reads=[bm, sg, vec], writes=[y_])

            def stageO1(j):
                col0, w = tile_geom(j)
                x_ = xt[j % 3]
                y_ = yT[j % 2]
                for oc in range(8):
                    bk = S.bank()
                    for kc in range(16):
                        S.op(pe, lambda: nc.tensor.matmul(bk[:, 0:w], lhsT=WO[:, kc, oc * 128:(oc + 1) * 128], rhs=y_[:, kc, 0:w],
                                                          start=(kc == 0), stop=(kc == 15)),
                             reads=[WO, y_], writes=[bk], mark=(kc == 15))
                    S.op(dve, lambda: nc.vector.tensor_tensor(out=x_[:, oc, 0:w], in0=bk[:, 0:w], in1=x_[:, oc, 0:w], op=ALU.add),
                         reads=[bk, x_], writes=[x_])
                sumsq1(x_, w, accO)

            def stageO2(j):
                col0, w = tile_geom(j)
                x_ = xt[j % 3]
                bk = sumsq2(accO, w)
                rstd_from(bk, rs2, w)
                n_ = hn[0]
                for k in range(8):
                    S.op(dve, lambda: nc.vector.scalar_tensor_tensor(out=n_[:, k, 0:w], in0=x_[:, k, 0:w], scalar=gO(k),
                                                                    in1=rs2[:, 0:w], op0=ALU.mult, op1=ALU.mult),
                         reads=[x_, rs2, vec], writes=[n_])
                t0 = (j - 1) * TA
                S.dma(sp, hnT_v[:, :, t0:t0 + w], n_[:, :, 0:w], reads=[n_])
                if t0 >= HALO:
                    S.dma(sp, h1T_v[:, :, t0 - HALO:t0 - HALO + w], x_[:, :, 0:w], reads=[x_])

            stageN1(0)
            stageN2(0)
            stagePa(0)
            stageN1(1)
            stageN2(1)
            for j in range(1, ntile):
                stagePa(j)
                if j > 1:
                    stageO2(j - 1)
                if j + 1 < ntile:
                    stageN1(j + 1)
                stagePb(j)
                if j + 1 < ntile:
                    stageN2(j + 1)
                stageO1(j)
            stageO2(ntile - 1)
            S.barrier()

    if "B" in phases:
        SCALE = 128.0 ** -0.5
        h1T_v = h1T_d.rearrange("(k p) t -> p k t", p=128)
        hnT_v = hnT_d.rearrange("(k p) t -> p k t", p=128)
        outT_v = outT.rearrange("(k p) t -> p k t", p=128)
        with ExitStack() as esB:
            def sbB(name, shape, dt):
                return Buf(esB.enter_context(nc.sbuf_tensor(name, list(shape), dt)))

            yT2 = sbB("yT2", [128, 16, OWN], BF16)
            vecB = sbB("vecB", [128, 32], F32)
            cw = sbB("cw", [128, 24], F32)
            ones_b = sbB("ones_b", [128, 128], BF16)
            ones_f2 = sbB("ones_f2", [128, 128], F32)
            ident = sbB("ident", [128, 128], BF16)
            S.dma(pool, ident[:, :], o_ident, writes=[ident])
            S.dma(sp, vecB[:, :], vecsB, writes=[vecB])
            S.dma(sp, cw[:, :], o_cw, writes=[cw])
            S.op(dve, lambda: nc.vector.memset(ones_b[:, :], 1.0), writes=[ones_b])
            S.op(dve, lambda: nc.vector.memset(ones_f2[:, :], 1.0), writes=[ones_f2])
            gF = lambda k: vecB[:, 24 + k:25 + k]

            with ExitStack() as esH:
                def sbH(name, shape, dt):
                    return Buf(esH.enter_context(nc.sbuf_tensor(name, list(shape), dt)))

                hnS_b = sbH("hnS", [128, 8, HALO + OWN], BF16)
                hnS = hnS_b.t
                hnK = [Buf(hnS) for _ in range(8)]
                wsl = [sbH(f"wsl{i}", [128, 8, 512], BF16) for i in range(2)]
                widx = [0]
                for k in range(8):
                    S.dma(sp, hnS[:, k, :], hnT_v[:, k, :], writes=[hnK[k]])

                def next_w(src_ap, ncols):
                    wb = wsl[widx[0] % 2]
                    widx[0] += 1
                    S.dma(pool, wb[:, :, 0:ncols], src_ap.rearrange("(k p) c -> p k c", p=128), writes=[wb])
                    return wb

                def proj(wb, c0, rhs_fn, w, out_fn=None):
                    bk = S.bank()
                    for k in range(8):
                        o_ap = bk[:, 0:w] if out_fn is None else out_fn(bk)
                        S.op(pe, lambda: nc.tensor.matmul(o_ap, lhsT=wb[:, k, c0:c0 + 128], rhs=rhs_fn(k),
                                                          start=(k == 0), stop=(k == 7)),
                             reads=[wb, hnK[k]], writes=[bk], mark=(k == 7))
                    return bk

                with ExitStack() as es1:
                    def sb1(name, shape, dt):
                        return Buf(es1.enter_context(nc.sbuf_tensor(name, list(shape), dt)))

                    QT = sb1("QT", [128, OWN], BF16)
                    KT = sb1("KT", [128, 4096], BF16)
                    Vt = sb1("Vt", [128, 32, 128], BF16)
                    VT = sb1("VT", [128, 4096], BF16)
                    U = sb1("U", [128, OWN], F32)
                    R = sb1("R", [128, OWN], F32)
                    Eb = [sb1(f"Eb{i}", [128, 512], F32) for i in range(3)]
                    PTb = [sb1(f"PTb{i}", [128, 512], BF16) for i in range(3)]
                    Wm = [sb1(f"Wm{i}", [128, 2, 256], F32) for i in range(2)]
                    tht = [sb1(f"tht{i}", [128, 512], F32) for i in range(2)]
                    pcount = [0]

                    for s_ in range(8):
                        for g in range(3):
                            hidx = s_ * 3 + g
                            d = DILS[g]
                            wb = next_w(o_wqkv[hidx], 384)
                            wm = Wm[hidx % 2]
                            S.dma(sp, wm[:, :, :], o_mask[hidx].rearrange("f k c -> k f c"), writes=[wm])
                            base = HALO - 128 * d

                            ntok = (d + 16) * 128
                            ntile_k = (ntok + 511) // 512

                            def nat(k, m):
                                st = base + 512 * m
                                w_ = min(512, ntok - 512 * m)
                                return hnS[:, k, st:st + w_], w_

                            def perm_dst(T, m, w_):
                                if d == 1:
                                    return T[:, 512 * m:512 * m + w_]
                                if d == 4:
                                    return T[:, 512 * m:512 * (m + 1)].rearrange("p (r i) -> p i r", r=4)
                                sb_, mm = m // 4, m % 4
                                return T[:, sb_ * 2048:(sb_ + 1) * 2048].rearrange("p (r i) -> p i r", r=16)[:, 32 * mm:32 * mm + 32, :]

                            def nat_src(bk, w_):
                                if d == 1:
                                    return bk[:, 0:w_]
                                return bk[:, 0:w_].rearrange("p (i r) -> p i r", r=d)

                            m_own0 = (128 * d) // 512 if d > 1 else None
                            for tt in range(4):
                                bk = proj(wb, 0, lambda k: hnS[:, k, HALO + tt * 512:HALO + (tt + 1) * 512], 512)
                                if d == 1:
                                    dst = QT[:, tt * 512:(tt + 1) * 512]
                                elif d == 4:
                                    dst = QT[:, tt * 512:(tt + 1) * 512].rearrange("p (r i) -> p i r", r=4)
                                else:
                                    dst = QT[:, :].rearrange("p (r i) -> p i r", r=16)[:, 32 * tt:32 * tt + 32, :]
                                S.op(act, lambda: nc.scalar.activation(out=dst, in_=nat_src(bk, 512), func=AF.Copy),
                                     reads=[bk], writes=[QT])
                            for m in range(ntile_k):
                                w_ = min(512, ntok - 512 * m)
                                bk = proj(wb, 128, lambda k: nat(k, m)[0], w_)
                                S.op(act, lambda: nc.scalar.activation(out=perm_dst(KT, m, w_), in_=nat_src(bk, w_), func=AF.Copy),
                                     reads=[bk], writes=[KT])
                                bk = proj(wb, 256, lambda k: nat(k, m)[0], w_)
                                S.op(dve, lambda: nc.vector.tensor_copy(out=VT[:, 512 * m:512 * m + w_], in_=bk[:, 0:w_]),
                                     reads=[bk], writes=[VT])
                            nblk_all = d + 16
                            bi = 0
                            while bi < nblk_all:
                                nb_ = min(4, nblk_all - bi)
                                bk = S.bank()
                                bkb = bk[:, :].bitcast(BF16)
                                for j in range(nb_):
                                    b_ = bi + j
                                    if d == 1:
                                        src = VT[:, b_ * 128:(b_ + 1) * 128]
                                    else:
                                        sb_, r = b_ // d, b_ % d
                                        src = VT[:, sb_ * 128 * d:(sb_ + 1) * 128 * d].rearrange("p (i r) -> p r i", r=d)[:, r, :]
                                    S.op(pe, lambda: nc.tensor.transpose(out=bkb[:, j * 128:(j + 1) * 128], in_=src, identity=ident[:, :]),
                                         reads=[VT, ident], writes=[bk], mark=(j == nb_ - 1))
                                S.op(dve, lambda: nc.vector.tensor_copy(out=Vt[:, bi:bi + nb_, :],
                                                                        in_=bkb[:, 0:nb_ * 128].rearrange("p (j e) -> p j e", j=nb_)),
                                     reads=[bk], writes=[Vt])
                                bi += nb_
                            pairs = [(quad, pair) for quad in range(4) for pair in range(2)]
                            st = {}

                            def att1(pi):
                                quad, pair = pairs[pi]
                                bS = S.bank()
                                blks = []
                                for j in range(2):
                                    qi = quad * 4 + pair * 2 + j
                                    sb_ = qi // d + 1
                                    r = qi % d
                                    bp = (sb_ - 1) * d + r
                                    bc = sb_ * d + r
                                    blks.append((qi, sb_, bp, bc))
                                    S.op(pe, lambda: nc.tensor.matmul(bS[:, j * 256:j * 256 + 128], lhsT=KT[:, bp * 128:(bp + 1) * 128],
                                                                      rhs=QT[:, qi * 128:(qi + 1) * 128], start=True, stop=True),
                                         reads=[KT, QT], writes=[bS], mark=False)
                                    S.op(pe, lambda: nc.tensor.matmul(bS[:, j * 256 + 128:j * 256 + 256], lhsT=KT[:, bc * 128:(bc + 1) * 128],
                                                                      rhs=QT[:, qi * 128:(qi + 1) * 128], start=True, stop=True),
                                         reads=[KT, QT], writes=[bS], mark=(j == 1))
                                E = Eb[pcount[0] % 3]
                                PT = PTb[pcount[0] % 3]
                                pcount[0] += 1
                                S.op(act, lambda: nc.scalar.activation(out=E[:, :], in_=bS[:, :], func=AF.Exp, scale=SCALE),
                                     reads=[bS], writes=[E])
                                f0 = 0 if blks[0][1] == 1 else 1
                                f1 = 0 if blks[1][1] == 1 else 1
                                if f0 == f1:
                                    S.op(dve, lambda: nc.vector.tensor_tensor(
                                        out=PT[:, :].rearrange("p (j c) -> p j c", j=2), in0=E[:, :].rearrange("p (j c) -> p j c", j=2),
                                        in1=wm[:, f0, :].unsqueeze(1).broadcast_to([128, 2, 256]), op=ALU.mult),
                                         reads=[E, wm], writes=[PT])
                                else:
                                    S.op(dve, lambda: nc.vector.tensor_tensor(out=PT[:, 0:256], in0=E[:, 0:256], in1=wm[:, f0, :],
                                                                             op=ALU.mult), reads=[E, wm], writes=[PT])
                                    S.op(dve, lambda: nc.vector.tensor_tensor(out=PT[:, 256:512], in0=E[:, 256:512], in1=wm[:, f1, :],
                                                                             op=ALU.mult), reads=[E, wm], writes=[PT])
                                st[pi] = (blks, PT)

                            def att2(pi):
                                quad, pair = pairs[pi]
                                blks, PT = st.pop(pi)
                                if pair == 0:
                                    st["bO"] = S.bank()
                                    st["bR"] = S.bank()
                                bO, bR = st["bO"], st["bR"]
                                for j in range(2):
                                    qi, sb_, bp, bc = blks[j]
                                    qc = (pair * 2 + j) * 128
                                    S.op(pe, lambda: nc.tensor.matmul(bO[:, qc:qc + 128], lhsT=Vt[:, bp, :], rhs=PT[:, j * 256:j * 256 + 128],
                                                                      start=True, stop=False), reads=[Vt, PT], writes=[bO], mark=False)
                                    S.op(pe, lambda: nc.tensor.matmul(bO[:, qc:qc + 128], lhsT=Vt[:, bc, :], rhs=PT[:, j * 256 + 128:j * 256 + 256],
                                                                      start=False, stop=True), reads=[Vt, PT], writes=[bO], mark=False)
                                    S.op(pe, lambda: nc.tensor.matmul(bR[:, qc:qc + 128], lhsT=ones_b[:, :], rhs=PT[:, j * 256:j * 256 + 128],
                                                                      start=True, stop=False), reads=[ones_b, PT], writes=[bR], mark=False)
                                    S.op(pe, lambda: nc.tensor.matmul(bR[:, qc:qc + 128], lhsT=ones_b[:, :], rhs=PT[:, j * 256 + 128:j * 256 + 256],
                                                                      start=False, stop=True), reads=[ones_b, PT], writes=[bR], mark=(j == 1))
                                if pair == 1:
                                    if d == 1:
                                        uo = lambda T: T[:, quad * 512:(quad + 1) * 512]
                                        bi_ = lambda b: b[:, :]
                                    elif d == 4:
                                        uo = lambda T: T[:, quad * 512:(quad + 1) * 512].rearrange("p (i r) -> p r i", r=4)
                                        bi_ = lambda b: b[:, :].rearrange("p (r i) -> p r i", r=4)
                                    else:
                                        uo = lambda T: T[:, :].rearrange("p (i r) -> p r i", r=16)[:, 4 * quad:4 * quad + 4, :]
                                        bi_ = lambda b: b[:, :].rearrange("p (r i) -> p r i", r=4)
                                    if g == 0:
                                        S.op(act, lambda: nc.scalar.activation(out=uo(U), in_=bi_(bO), func=AF.Copy), reads=[bO], writes=[U])
                                        S.op(act, lambda: nc.scalar.activation(out=uo(R), in_=bi_(bR), func=AF.Copy), reads=[bR], writes=[R])
                                    else:
                                        S.op(dve, lambda: nc.vector.tensor_tensor(out=uo(U), in0=bi_(bO), in1=uo(U), op=ALU.add),
                                             reads=[bO, U], writes=[U])
                                        S.op(dve, lambda: nc.vector.tensor_tensor(out=uo(R), in0=bi_(bR), in1=uo(R), op=ALU.add),
                                             reads=[bR, R], writes=[R])

                            att1(0)
                            att1(1)
                            for pi in range(8):
                                if pi + 2 < 8:
                                    att1(pi + 2)
                                att2(pi)
                        wb = next_w(o_wgc[s_], 128)
                        for tt in range(4):
                            bk = proj(wb, 0, lambda k: hnS[:, k, HALO + tt * 512:HALO + (tt + 1) * 512], 512)
                            t_ = tht[tt % 2]
                            cs = slice(tt * 512, (tt + 1) * 512)
                            S.op(act, lambda: nc.scalar.activation(out=t_[:, :], in_=bk[:, :], func=AF.Tanh, scale=0.5), reads=[bk], writes=[t_])
                            S.op(dve, lambda: nc.vector.scalar_tensor_tensor(out=t_[:, :], in0=t_[:, :], scalar=1.0, in1=bk[:, :],
                                                                            op0=ALU.add, op1=ALU.mult), reads=[bk, t_], writes=[t_])
                            S.op(dve, lambda: nc.vector.reciprocal(out=R[:, cs], in_=R[:, cs]), reads=[R], writes=[R])
                            S.op(dve, lambda: nc.vector.tensor_tensor(out=U[:, cs], in0=U[:, cs], in1=R[:, cs], op=ALU.mult),
                                 reads=[U, R], writes=[U])
                            S.op(dve, lambda: nc.vector.scalar_tensor_tensor(out=yT2[:, s_, cs], in0=U[:, cs], scalar=0.5, in1=t_[:, :],
                                                                            op0=ALU.mult, op1=ALU.mult), reads=[U, t_], writes=[yT2])
                    S.barrier()

                with ExitStack() as es2:
                    def sb2(name, shape, dt):
                        return Buf(es2.enter_context(nc.sbuf_tensor(name, list(shape), dt)))

                    dc = sb2("dc", [128, 2 + OWN], F32)
                    zz = sb2("zz", [128, 2 + OWN], F32)
                    acc = sb2("acc", [128, OWN], F32)
                    th2 = sb2("th2", [128, OWN], F32)
                    own = lambda tt: (lambda k: hnS[:, k, HALO + tt * 512:HALO + (tt + 1) * 512])
                    hal2 = lambda k: hnS[:, k, HALO - 2:HALO]
                    for c in range(8):
                        wb = next_w(o_wd[c], 512)
                        bk = proj(wb, 128, hal2, 2)
                        S.op(act, lambda: nc.scalar.activation(out=dc[:, 0:2], in_=bk[:, 0:2], func=AF.Copy), reads=[bk], writes=[dc])
                        for tt in range(4):
                            bk = proj(wb, 128, own(tt), 512)
                            S.op(act, lambda: nc.scalar.activation(out=dc[:, 2 + tt * 512:2 + (tt + 1) * 512], in_=bk[:, :], func=AF.Copy),
                                 reads=[bk], writes=[dc])
                        bk = proj(wb, 256, hal2, 2)
                        S.op(dve, lambda: nc.vector.tensor_tensor(out=zz[:, 0:2], in0=bk[:, 0:2], in1=dc[:, 0:2], op=ALU.mult),
                             reads=[bk, dc], writes=[zz])
                        for tt in range(4):
                            bk = proj(wb, 256, own(tt), 512)
                            S.op(dve, lambda: nc.vector.tensor_tensor(out=zz[:, 2 + tt * 512:2 + (tt + 1) * 512], in0=bk[:, :],
                                                                     in1=dc[:, 2 + tt * 512:2 + (tt + 1) * 512], op=ALU.mult),
                                 reads=[bk, dc], writes=[zz])
                        S.op(dve, lambda: nc.vector.tensor_scalar(out=acc[:, :], in0=zz[:, 0:OWN], scalar1=cw[:, 3 * c:3 * c + 1], scalar2=None,
                                                                 op0=ALU.mult), reads=[zz, cw], writes=[acc])
                        S.op(dve, lambda: nc.vector.scalar_tensor_tensor(out=acc[:, :], in0=zz[:, 1:OWN + 1], scalar=cw[:, 3 * c + 1:3 * c + 2],
                                                                        in1=acc[:, :], op0=ALU.mult, op1=ALU.add), reads=[zz, cw, acc], writes=[acc])
                        S.op(dve, lambda: nc.vector.scalar_tensor_tensor(out=acc[:, :], in0=zz[:, 2:OWN + 2], scalar=cw[:, 3 * c + 2:3 * c + 3],
                                                                        in1=acc[:, :], op0=ALU.mult, op1=ALU.add), reads=[zz, cw, acc], writes=[acc])
                        for tt in range(4):
                            bk = proj(wb, 384, own(tt), 512)
                            tsl = th2[:, tt * 512:(tt + 1) * 512]
                            S.op(act, lambda: nc.scalar.activation(out=tsl, in_=bk[:, :], func=AF.Tanh, scale=0.5), reads=[bk], writes=[th2])
                            S.op(dve, lambda: nc.vector.scalar_tensor_tensor(out=tsl, in0=tsl, scalar=1.0, in1=bk[:, :], op0=ALU.add, op1=ALU.mult),
                                 reads=[bk, th2], writes=[th2])
                        for tt in range(4):
                            bk = proj(wb, 0, own(tt), 512)
                            asl = acc[:, tt * 512:(tt + 1) * 512]
                            S.op(dve, lambda: nc.vector.tensor_tensor(out=asl, in0=bk[:, :], in1=asl, op=ALU.mult), reads=[bk, acc], writes=[acc])
                        S.op(dve, lambda: nc.vector.scalar_tensor_tensor(out=yT2[:, 8 + c, :], in0=acc[:, :], scalar=0.5, in1=th2[:, :],
                                                                        op0=ALU.mult, op1=ALU.mult), reads=[acc, th2], writes=[yT2])
                    S.barrier()

            with ExitStack() as es3:
                def sb3(name, shape, dt):
                    return Buf(es3.enter_context(nc.sbuf_tensor(name, list(shape), dt)))

                WO2 = [sb3(f"WO2_{i}", [128, 16, 128], BF16) for i in range(8)]
                h1t = [sb3(f"h1t{i}", [128, 8, 512], F32) for i in range(2)]
                sq3 = [sb3(f"sq3{i}", [128, 512], F32) for i in range(2)]
                rs3 = sb3("rs3", [128, 512], F32)
                acc3 = [sb3(f"acc3{i}", [128, 512], F32) for i in range(2)]
                wo_v = o_wout.rearrange("(k p) c -> p k c", p=128)
                for oc in range(8):
                    S.dma(pool, WO2[oc][:, :, :], wo_v[:, :, oc * 128:(oc + 1) * 128], writes=[WO2[oc]])

                def b3a(tt):
                    h_ = h1t[tt % 2]
                    S.dma(sp, h_[:, :, :], h1T_v[:, :, tt * 512:(tt + 1) * 512], writes=[h_])
                    for oc in range(8):
                        bk = S.bank()
                        for kc in range(16):
                            S.op(pe, lambda: nc.tensor.matmul(bk[:, :], lhsT=WO2[oc][:, kc, :],
                                                              rhs=yT2[:, kc, tt * 512:(tt + 1) * 512], start=(kc == 0), stop=(kc == 15)),
                                 reads=[WO2[oc], yT2], writes=[bk], mark=(kc == 15))
                        S.op(dve, lambda: nc.vector.tensor_tensor(out=h_[:, oc, :], in0=bk[:, :], in1=h_[:, oc, :], op=ALU.add),
                             reads=[bk, h_], writes=[h_])
                    a3 = acc3[tt % 2]
                    for k in range(8):
                        q_ = sq3[k % 2]
                        S.op(act, lambda: nc.scalar.activation(out=q_[:, :], in_=h_[:, k, :], func=AF.Square), reads=[h_], writes=[q_])
                        if k == 1:
                            S.op(pool, lambda: nc.gpsimd.tensor_tensor(out=a3[:, :], in0=sq3[0][:, :], in1=sq3[1][:, :], op=ALU.add),
                                 reads=[sq3[0], sq3[1]], writes=[a3])
                        elif k > 1:
                            S.op(pool, lambda: nc.gpsimd.tensor_tensor(out=a3[:, :], in0=a3[:, :], in1=q_[:, :], op=ALU.add),
                                 reads=[a3, q_], writes=[a3])

                def b3b(tt):
                    h_ = h1t[tt % 2]
                    a3 = acc3[tt % 2]
                    bk = S.bank()
                    S.op(pe, lambda: nc.tensor.matmul(bk[:, :], lhsT=ones_f2[:, :], rhs=a3[:, :], start=True, stop=True),
                         reads=[ones_f2, a3], writes=[bk], mark=True)
                    rstd_from(bk, rs3, 512)
                    for k in range(8):
                        S.op(dve, lambda: nc.vector.scalar_tensor_tensor(out=h_[:, k, :], in0=h_[:, k, :], scalar=gF(k), in1=rs3[:, :],
                                                                        op0=ALU.mult, op1=ALU.mult), reads=[h_, rs3, vecB], writes=[h_])
                    S.dma(sp, outT_v[:, :, tt * 512:(tt + 1) * 512], h_[:, :, :], reads=[h_])

                b3a(0)
                for tt in range(4):
                    if tt + 1 < 4:
                        b3a(tt + 1)
                    b3b(tt)

    S.barrier()
    return nc


def host_inputs_A(inp):
    x = np.asarray(inp["x"], dtype=np.float32)
    maps = []
    e_win = np.ascontiguousarray(inp["even_w_in"][0])
    pw = np.asarray(inp["even_pool_w"][0]).reshape(4, 2, 128, 256)
    wsT = np.ascontiguousarray(np.transpose(np.asarray(inp["even_ws"][0]), (2, 0, 1)))
    tril = np.ascontiguousarray(np.broadcast_to((np.arange(128)[:, None] <= np.arange(128)[None, :]).astype(np.float32)[:, None, :],
                                                (128, 4, 128)))
    bs = np.asarray(inp["even_bs"][0]).reshape(1, 512)
    e_wout = np.ascontiguousarray(inp["even_w_out"][0])
    vecs = np.zeros((128, 32), np.float32)
    vecs[:, 0:8] = np.asarray(inp["even_norm"][0]).reshape(8, 128).T
    vecs[:, 8:16] = np.asarray(inp["odd_norm"][0]).reshape(8, 128).T
    vecs[:, 16:24] = np.asarray(inp["even_pool_scale"][0]).reshape(8, 128).T
    vecs[:, 24:32] = np.asarray(inp["final_norm"]).reshape(8, 128).T
    for c in range(8):
        b, q = c // 4, c % 4
        t0 = OWN * q - HALO - PRE
        xt = np.zeros((TH, D), np.float32)
        lo = max(t0, 0)
        xt[lo - t0:, :] = x[b, lo:t0 + TH, :]
        ic = np.zeros((4, 2, 16), np.float32)
        for g, w in enumerate(POOLS):
            ic[g, :, :] = 1.0 / w
            cnt = np.minimum(np.arange(16) + 1, w).astype(np.float32)
            if q == 1:
                ic[g, 0, :] = 1.0 / cnt
            if q == 0:
                ic[g, 1, :] = 1.0 / cnt
        maps.append({
            "xT": np.ascontiguousarray(xt.T), "e_win": e_win, "e_pw": np.ascontiguousarray(pw), "e_wsT": wsT, "e_tril": tril,
            "e_bs": np.ascontiguousarray(bs), "e_wout": e_wout, "vecs": vecs,
            "invc": np.ascontiguousarray(np.broadcast_to(ic.reshape(1, 128), (128, 128))),
        })
    return maps


def host_inputs_B(inp):
    w = np.asarray(inp["odd_w_in"][0], dtype=np.float32)
    wqkv = np.zeros((24, D, 384), np.float32)
    for s_ in range(8):
        for g in range(3):
            for qi in range(3):
                c0 = ((qi * 3 + g) * 8 + s_) * 128
                wqkv[s_ * 3 + g, :, qi * 128:(qi + 1) * 128] = w[:, c0:c0 + 128]
    wgc = np.stack([w[:, 9216 + s_ * 128:9216 + (s_ + 1) * 128] for s_ in range(8)], 0)
    wd = np.stack([np.concatenate([w[:, 10240 + j * 1024 + c * 128:10240 + j * 1024 + (c + 1) * 128] for j in range(4)], 1)
                   for c in range(8)], 0)
    cwv = np.asarray(inp["odd_conv_w"][0], dtype=np.float32)
    cw = np.zeros((128, 24), np.float32)
    for c in range(8):
        for j in range(3):
            cw[:, c * 3 + j] = cwv[j, c * 128:(c + 1) * 128]
    vecs = np.zeros((128, 32), np.float32)
    vecs[:, 24:32] = np.asarray(inp["final_norm"]).reshape(8, 128).T
    kk = np.arange(128)[:, None].astype(np.float64)
    qq = np.arange(128)[None, :].astype(np.float64)
    slopes = 2.0 ** (-8.0 * (np.arange(24) + 1) / 24)
    masks = []
    for hv in (0.0, 1.0):
        m = np.zeros((24, 2, 128, 256), np.float32)
        for s_ in range(8):
            for g in range(3):
                sl = float(np.float32(slopes[g * 8 + s_])) * DILS[g]
                wprev = (kk >= qq) * np.exp(-sl * (qq + 128 - kk))
                wcur = (kk <= qq) * np.exp(-sl * (qq - kk))
                first = np.concatenate([wprev * hv, wcur], 1)
                rest = np.concatenate([wprev, wcur], 1)
                m[s_ * 3 + g, 0] = first
                m[s_ * 3 + g, 1] = rest
        masks.append(m)
    wout = np.ascontiguousarray(inp["odd_w_out"][0], dtype=np.float32)
    maps = []
    for c in range(8):
        q = c % 4
        maps.append({"o_wqkv": wqkv, "o_wgc": np.ascontiguousarray(wgc), "o_wd": np.ascontiguousarray(wd), "o_wout": wout,
                     "o_cw": cw, "o_mask": masks[0 if q == 0 else 1], "vecsB": vecs, "o_ident": np.eye(128, dtype=np.float32)})
    return maps


_NC_CACHE = {}


def kernel(**inputs):
    inp = {k: np.asarray(v) for k, v in inputs.items()}
    if "AB" not in _NC_CACHE:
        _NC_CACHE["AB"] = build_program("AB")
    nc = _NC_CACHE["AB"]
    mA = host_inputs_A(inp)
    mB = host_inputs_B(inp)
    maps = [dict(a, **b) for a, b in zip(mA, mB)]
    res = run_bass_kernel_spmd(nc, maps, core_ids=list(range(8)))
    out = np.zeros((NB, SEQ, D), np.float32)
    for c in range(8):
        b, q = c // 4, c % 4
        out[b, q * OWN:(q + 1) * OWN, :] = np.asarray(res.results[c]["outT"]).T
    return out
```

```python
import numpy as np
from contextlib import ExitStack
import concourse.bass as bass
import concourse.mybir as mybir
from concourse.bass_utils import run_bass_kernel_spmd

F32 = mybir.dt.float32
BF16 = mybir.dt.bfloat16
AF = mybir.ActivationFunctionType
ALU = mybir.AluOpType

D = 1024
SEQ = 8192
NB = 2
OWN = 2048
HALO = 2048
PRE = 16
TH = PRE + HALO + OWN
TA = 256
EPS = 1e-6
POOLS = (2, 4, 8, 16)
DILS = (1, 4, 16)
NDS = 16


class Sem:
    __slots__ = ("h", "i")

    def __init__(self, h, i):
        self.h = h
        self.i = i


class Buf:
    __slots__ = ("t", "writer", "readers")

    def __init__(self, t=None):
        self.t = t
        self.writer = None
        self.readers = []

    def __getitem__(self, k):
        return self.t[k]


class Eng:
    def __init__(self, S, name, e, is_pe=False):
        self.e = e
        self.name = name
        self.sem = S.newsem("s_" + name)
        self.cnt = 0
        self.known = {}
        self.is_pe = is_pe


class Sched:
    def __init__(self, nc):
        self.nc = nc
        self.nsem = 0
        self.pe = Eng(self, "pe", nc.tensor, True)
        self.act = Eng(self, "act", nc.scalar)
        self.dve = Eng(self, "dve", nc.vector)
        self.pool = Eng(self, "pool", nc.gpsimd)
        self.sp = Eng(self, "sp", nc.sync)
        self.engs = [self.pe, self.act, self.dve, self.pool, self.sp]
        self.dsems = [self.newsem(f"d{i}") for i in range(2 * NDS)]
        self.dcnt = [0] * (2 * NDS)
        self.dnext = [0, 0]
        self.nbank = 0
        self.banks = []

    def newsem(self, name):
        s = Sem(self.nc.alloc_semaphore(name), self.nsem)
        self.nsem += 1
        return s

    def _wait(self, eng, tok):
        sem, val = tok
        if eng.known.get(sem.i, 0) >= val:
            return
        eng.e.wait_ge(sem.h, val)
        eng.known[sem.i] = val

    def _deps(self, eng, reads, writes):
        for b in reads:
            if b.writer is not None:
                if b.writer[0] is eng.sem and eng.is_pe:
                    continue
                self._wait(eng, b.writer)
        for b in writes:
            if b.writer is not None and not (b.writer[0] is eng.sem and eng.is_pe):
                self._wait(eng, b.writer)
            for r in b.readers:
                if not (r[0] is eng.sem and eng.is_pe):
                    self._wait(eng, r)

    def _record(self, tok, reads, writes):
        for b in reads:
            b.readers.append(tok)
        for b in writes:
            b.writer = tok
            b.readers = []

    def op(self, eng, fn, reads=(), writes=(), mark=True):
        self._deps(eng, reads, writes)
        ins = fn()
        tok = (eng.sem, eng.cnt + 1)
        if mark:
            ins.then_inc(eng.sem.h, 1)
            eng.cnt += 1
        self._record(tok, reads, writes)
        return ins

    def dma(self, q, out, in_, reads=(), writes=()):
        self._deps(q, reads, writes)
        ring = 1 if q is self.pool else 0
        i = ring * NDS + self.dnext[ring]
        self.dnext[ring] = (self.dnext[ring] + 1) % NDS
        if self.dcnt[i] > 0:
            self._wait(q, (self.dsems[i], self.dcnt[i]))
        ins = q.e.dma_start(out=out, in_=in_)
        self.dcnt[i] += 16
        ins.then_inc(self.dsems[i].h, 16)
        self._record((self.dsems[i], self.dcnt[i]), reads, writes)

    def barrier(self):
        for e in self.engs:
            for f in self.engs:
                if f is not e and f.cnt > 0:
                    self._wait(e, (f.sem, f.cnt))
            for i in range(2 * NDS):
                if self.dcnt[i] > 0:
                    self._wait(e, (self.dsems[i], self.dcnt[i]))

    def bank(self):
        b = self.banks[self.nbank % len(self.banks)]
        self.nbank += 1
        return b


def build_program(phases="AB"):
    nc = bass.Bass("TRN2", target_bir_lowering=False)
    S = Sched(nc)
    pe, act, dve, pool, sp = S.pe, S.act, S.dve, S.pool, S.sp
    fused = phases == "AB"

    def dram(name, shape, dt, kind):
        return nc.dram_tensor(name, list(shape), dt, kind=kind).ap()

    mid_kind_out = "Internal" if fused else "ExternalOutput"
    mid_kind_in = "Internal" if fused else "ExternalInput"
    if "A" in phases:
        xT = dram("xT", [D, TH], F32, "ExternalInput")
        e_win = dram("e_win", [D, 5120], F32, "ExternalInput")
        e_pw = dram("e_pw", [4, 2, 128, 256], F32, "ExternalInput")
        e_wsT = dram("e_wsT", [128, 4, 128], F32, "ExternalInput")
        e_tril = dram("e_tril", [128, 4, 128], F32, "ExternalInput")
        e_bs = dram("e_bs", [1, 512], F32, "ExternalInput")
        e_wout = dram("e_wout", [2048, D], F32, "ExternalInput")
        vecs = dram("vecs", [128, 32], F32, "ExternalInput")
        invc = dram("invc", [128, 128], F32, "ExternalInput")
        h1T_d = dram("h1T", [D, OWN], F32, mid_kind_out)
        hnT_d = dram("hnT", [D, HALO + OWN], BF16, mid_kind_out)
    else:
        h1T_d = dram("h1T", [D, OWN], F32, mid_kind_in)
        hnT_d = dram("hnT", [D, HALO + OWN], BF16, mid_kind_in)
    if "B" in phases:
        o_wqkv = dram("o_wqkv", [24, D, 384], F32, "ExternalInput")
        o_wgc = dram("o_wgc", [8, D, 128], F32, "ExternalInput")
        o_wd = dram("o_wd", [8, D, 512], F32, "ExternalInput")
        o_wout = dram("o_wout", [2048, D], F32, "ExternalInput")
        o_cw = dram("o_cw", [128, 24], F32, "ExternalInput")
        o_mask = dram("o_mask", [24, 2, 128, 256], F32, "ExternalInput")
        o_ident = dram("o_ident", [128, 128], F32, "ExternalInput")
        vecsB = dram("vecsB", [128, 32], F32, "ExternalInput")
        outT = dram("outT", [D, OWN], F32, "ExternalOutput")

    S.banks = [Buf(nc.alloc_psum_tensor(f"bank{i}", [128, 512], F32)) for i in range(8)]

    epsb = Buf(nc.alloc_sbuf_tensor("epsb", [128, 1], F32))
    S.op(dve, lambda: nc.vector.memset(epsb[:, :], EPS), writes=[epsb])

    def rstd_from(bk, rs_buf, w):
        S.op(act, lambda: nc.scalar.activation(out=rs_buf[:, 0:w], in_=bk[:, 0:w], func=AF.Sqrt, bias=epsb[:, 0:1],
                                               scale=1.0 / D), reads=[bk, epsb], writes=[rs_buf])
        S.op(dve, lambda: nc.vector.reciprocal(out=rs_buf[:, 0:w], in_=rs_buf[:, 0:w]), reads=[rs_buf], writes=[rs_buf])

    if "A" in phases:
        with ExitStack() as es:
            def sb(name, shape, dt):
                return Buf(es.enter_context(nc.sbuf_tensor(name, list(shape), dt)))

            WI = [sb(f"WI{i}", [128, 8, 1024], BF16) for i in range(5)]
            WO = sb("WO", [128, 16, D], BF16)
            PW = sb("PW", [128, 4, 2, 256], BF16)
            wsT_f = sb("wsT_f", [128, 4, 128], F32)
            tril_f = sb("tril_f", [128, 4, 128], F32)
            wsTm = sb("wsTm", [128, 4, 128], BF16)
            bsr = sb("bsr", [1, 512], BF16)
            ones_row = sb("ones_row", [1, 128], BF16)
            ones_f = sb("ones_f", [128, 128], F32)
            vec = sb("vec", [128, 32], F32)
            invc_s = sb("invc_s", [128, 128], F32)
            xt = [sb(f"xt{i}", [128, 8, TA], F32) for i in range(3)]
            sq = [sb(f"sq{i}", [128, TA], F32) for i in range(2)]
            rs = sb("rs", [128, TA], F32)
            rs2 = sb("rs2", [128, TA], F32)
            hT = [sb(f"hT{i}", [128, 8, TA], BF16) for i in range(2)]
            aT = [sb(f"aT{i}", [128, PRE + TA], F32) for i in range(8)]
            Tm = [sb(f"Tm{i}", [128, PRE + TA], F32) for i in range(2)]
            pooled = [sb(f"pooled{i}", [128, TA], BF16) for i in range(8)]
            accN = sb("accN", [128, TA], F32)
            accO = sb("accO", [128, TA], F32)
            tmp16 = sb("tmp16", [128, 16], F32)
            sga = [sb(f"sga{i}", [128, TA], F32) for i in range(2)]
            sgb = [sb(f"sgb{i}", [128, TA], F32) for i in range(2)]
            ugb = [sb(f"ugb{i}", [128, TA], F32) for i in range(2)]
            vtm = [sb(f"vtm{i}", [128, 1024], BF16) for i in range(TA // 128)]
            yT = [sb(f"yT{i}", [128, 16, TA], BF16) for i in range(2)]
            hn = [sb(f"hn{i}", [128, 8, TA], BF16) for i in range(1)]

            S.dma(sp, vec[:, :], vecs, writes=[vec])
            S.dma(sp, invc_s[:, :], invc, writes=[invc_s])
            S.dma(sp, wsT_f[:, :, :], e_wsT, writes=[wsT_f])
            S.dma(sp, tril_f[:, :, :], e_tril, writes=[tril_f])
            S.op(dve, lambda: nc.vector.memset(ones_f[:, :], 1.0), writes=[ones_f])
            S.op(dve, lambda: nc.vector.memset(ones_row[:, :], 1.0), writes=[ones_row])
            S.op(dve, lambda: nc.vector.tensor_tensor(out=wsTm[:, :, :], in0=wsT_f[:, :, :], in1=tril_f[:, :, :],
                                                     op=ALU.mult), reads=[wsT_f, tril_f], writes=[wsTm])
            S.dma(pool, bsr[:, :], e_bs, writes=[bsr])
            win_v = e_win.rearrange("(k p) c -> p k c", p=128)
            S.dma(pool, WI[0][:, :, :], win_v[:, :, 0:1024], writes=[WI[0]])
            S.dma(pool, PW[:, :, :, :], e_pw.rearrange("g cc p d -> p g cc d"), writes=[PW])
            for i in (1, 3, 2, 4):
                S.dma(pool, WI[i][:, :, :], win_v[:, :, i * 1024:(i + 1) * 1024], writes=[WI[i]])
            S.dma(pool, WO[:, :, :], e_wout.rearrange("(k p) c -> p k c", p=128), writes=[WO])

            xT_v = xT.rearrange("(k p) t -> p k t", p=128)
            h1T_v = h1T_d.rearrange("(k p) t -> p k t", p=128)
            hnT_v = hnT_d.rearrange("(k p) t -> p k t", p=128)
            gE = lambda k: vec[:, k:k + 1]
            gO = lambda k: vec[:, 8 + k:9 + k]
            psc = lambda k: vec[:, 16 + k:17 + k]

            ntile = 1 + (HALO + OWN) // TA

            def tile_geom(j):
                if j == 0:
                    return 0, PRE
                return PRE + (j - 1) * TA, TA

            def sumsq1(src_buf, w, accb):
                for k in range(8):
                    q_ = sq[k % 2]
                    S.op(act, lambda: nc.scalar.activation(out=q_[:, 0:w], in_=src_buf[:, k, 0:w], func=AF.Square),
                         reads=[src_buf], writes=[q_])
                    if k == 1:
                        S.op(pool, lambda: nc.gpsimd.tensor_tensor(out=accb[:, 0:w], in0=sq[0][:, 0:w], in1=sq[1][:, 0:w], op=ALU.add),
                             reads=[sq[0], sq[1]], writes=[accb])
                    elif k > 1:
                        S.op(pool, lambda: nc.gpsimd.tensor_tensor(out=accb[:, 0:w], in0=accb[:, 0:w], in1=q_[:, 0:w], op=ALU.add),
                             reads=[accb, q_], writes=[accb])

            def sumsq2(accb, w):
                bk = S.bank()
                S.op(pe, lambda: nc.tensor.matmul(bk[:, 0:w], lhsT=ones_f[:, :], rhs=accb[:, 0:w], start=True, stop=True),
                     reads=[ones_f, accb], writes=[bk], mark=True)
                return bk

            def stageN0(j):
                col0, w = tile_geom(j)
                x_ = xt[j % 3]
                S.dma(sp, x_[:, :, 0:w], xT_v[:, :, col0:col0 + w], writes=[x_])

            def stageN1(j):
                col0, w = tile_geom(j)
                x_ = xt[j % 3]
                sumsq1(x_, w, accN)

            def stageN2(j):
                col0, w = tile_geom(j)
                x_ = xt[j % 3]
                bk = sumsq2(accN, w)
                rstd_from(bk, rs, w)
                h_ = hT[j % 2]
                for k in range(8):
                    S.op(dve, lambda: nc.vector.scalar_tensor_tensor(out=h_[:, k, 0:w], in0=x_[:, k, 0:w], scalar=gE(k),
                                                                    in1=rs[:, 0:w], op0=ALU.mult, op1=ALU.mult),
                         reads=[x_, rs, vec], writes=[h_])

            def proj_fm(Wb, c, h_, w):
                bk = S.bank()
                for k in range(8):
                    S.op(pe, lambda: nc.tensor.matmul(bk[:, 0:w], lhsT=Wb[:, k, c * 128:(c + 1) * 128], rhs=h_[:, k, 0:w],
                                                      start=(k == 0), stop=(k == 7)),
                         reads=[Wb, h_], writes=[bk], mark=(k == 7))
                return bk

            def stagePa(j):
                col0, w = tile_geom(j)
                h_ = hT[j % 2]
                for c in range(8):
                    bk = proj_fm(WI[0], c, h_, w)
                    dst0 = 0 if j == 0 else PRE
                    S.op(act, lambda: nc.scalar.activation(out=aT[c][:, dst0:dst0 + w], in_=bk[:, 0:w], func=AF.Copy),
                         reads=[bk], writes=[aT[c]])

            def stagePb(j):
                col0, w = tile_geom(j)
                h_ = hT[j % 2]
                y_ = yT[j % 2]
                W_ = PRE + w
                for gi in range(4):
                    wpool = POOLS[gi]
                    for cc in range(2):
                        c = 2 * gi + cc
                        A = aT[c]
                        cur = A
                        lo = 0
                        sh = 1
                        for step in range(gi + 1):
                            nxt = Tm[step % 2]
                            S.op(dve, lambda: nc.vector.tensor_tensor(out=nxt[:, lo + sh:W_], in0=cur[:, lo + sh:W_],
                                                                     in1=cur[:, lo:W_ - sh], op=ALU.add),
                                 reads=[cur], writes=[nxt])
                            cur = nxt
                            lo += sh
                            sh *= 2
                        pl = pooled[c]
                        S.op(dve, lambda: nc.vector.scalar_tensor_tensor(out=pl[:, 0:w], in0=cur[:, PRE:W_], scalar=1.0 / wpool,
                                                                        in1=A[:, PRE:W_], op0=ALU.mult, op1=ALU.subtract),
                             reads=[cur, A], writes=[pl])
                        if j in (1, 1 + HALO // TA):
                            pos = 0 if j == 1 else 1
                            o0 = (gi * 2 + pos) * 16
                            S.op(dve, lambda: nc.vector.tensor_tensor(out=tmp16[:, :], in0=cur[:, PRE:PRE + 16],
                                                                     in1=invc_s[:, o0:o0 + 16], op=ALU.mult),
                                 reads=[cur, invc_s], writes=[tmp16])
                            S.op(dve, lambda: nc.vector.tensor_tensor(out=pl[:, 0:16], in0=tmp16[:, :],
                                                                     in1=A[:, PRE:PRE + 16], op=ALU.subtract),
                                 reads=[tmp16, A, pl], writes=[pl])
                        S.op(dve, lambda: nc.vector.tensor_copy(out=A[:, 0:PRE], in_=A[:, w:w + PRE]),
                             reads=[A], writes=[A])
                for tc in range(w // 128):
                    for half in range(2):
                        bk = S.bank()
                        for k in range(8):
                            S.op(pe, lambda: nc.tensor.matmul(bk[:, :], lhsT=h_[:, k, tc * 128:(tc + 1) * 128],
                                                              rhs=WI[3][:, k, half * 512:(half + 1) * 512],
                                                              start=(k == 0), stop=(k == 7)),
                                 reads=[WI[3], h_], writes=[bk], mark=(k == 7))
                        S.op(act, lambda: nc.scalar.activation(out=vtm[tc][:, half * 512:(half + 1) * 512], in_=bk[:, :],
                                                               func=AF.Copy), reads=[bk], writes=[vtm[tc]])
            def stagePb2(j):
                col0, w = tile_geom(j)
                h_ = hT[j % 2]
                y_ = yT[j % 2]
                for cc in range(8):
                    g = cc // 2
                    bu = proj_fm(WI[2], cc, h_, w)
                    bg = proj_fm(WI[4], cc, h_, w)
                    bm = S.bank()
                    for tc in range(w // 128):
                        S.op(pe, lambda: nc.tensor.matmul(bm[:, tc * 128:(tc + 1) * 128], lhsT=vtm[tc][:, cc * 128:(cc + 1) * 128],
                                                          rhs=wsTm[:, g, :], start=True, stop=False),
                             reads=[vtm[tc], wsTm], writes=[bm], mark=False)
                        S.op(pe, lambda: nc.tensor.matmul(bm[:, tc * 128:(tc + 1) * 128], lhsT=ones_row[0:1, :],
                                                          rhs=bsr[0:1, g * 128:(g + 1) * 128], start=False, stop=True),
                             reads=[ones_row, bsr], writes=[bm], mark=(tc == w // 128 - 1))
                    sg = sgb[cc % 2]
                    ug = ugb[cc % 2]
                    S.op(act, lambda: nc.scalar.activation(out=sg[:, 0:w], in_=bg[:, 0:w], func=AF.Silu),
                         reads=[bg], writes=[sg])
                    S.op(dve, lambda: nc.vector.tensor_tensor(out=ug[:, 0:w], in0=bu[:, 0:w], in1=sg[:, 0:w], op=ALU.mult),
                         reads=[bu, sg], writes=[ug])
                    S.op(dve, lambda: nc.vector.tensor_tensor(out=y_[:, 8 + cc, 0:w], in0=bm[:, 0:w], in1=ug[:, 0:w], op=ALU.mult),
                         reads=[bm, ug], writes=[y_])
                for gi in range(4):
                    for dch in range(2):
                        co = 2 * gi + dch
                        bm = S.bank()
                        for cc in range(2):
                            S.op(pe, lambda: nc.tensor.matmul(bm[:, 0:w], lhsT=PW[:, gi, cc, dch * 128:(dch + 1) * 128],
                                                              rhs=pooled[2 * gi + cc][:, 0:w],
                                                              start=(cc == 0), stop=(cc == 1)),
                                 reads=[PW, pooled[2 * gi + cc]], writes=[bm], mark=(cc == 1))
                        bg = proj_fm(WI[1], co, h_, w)
                        sg = sga[co % 2]
                        S.op(act, lambda: nc.scalar.activation(out=sg[:, 0:w], in_=bg[:, 0:w], func=AF.Silu),
                             reads=[bg], writes=[sg])
                        S.op(dve, lambda: nc.vector.scalar_tensor_tensor(out=y_[:, co, 0:w], in0=bm[:, 0:w], scalar=psc(co),
                                                                        in1=sg[:, 0:w], op0=ALU.mult, op1=ALU.mult),
                             reads=[bm, sg, vec], writes=[y_])

            def stageO1(j):
                col0, w = tile_geom(j)
                x_ = xt[j % 3]
                y_ = yT[j % 2]
                for oc in range(8):
                    bk = S.bank()
                    for kc in range(16):
                        S.op(pe, lambda: nc.tensor.matmul(bk[:, 0:w], lhsT=WO[:, kc, oc * 128:(oc + 1) * 128], rhs=y_[:, kc, 0:w],
                                                          start=(kc == 0), stop=(kc == 15)),
                             reads=[WO, y_], writes=[bk], mark=(kc == 15))
                    S.op(dve, lambda: nc.vector.tensor_tensor(out=x_[:, oc, 0:w], in0=bk[:, 0:w], in1=x_[:, oc, 0:w], op=ALU.add),
                         reads=[bk, x_], writes=[x_])
                sumsq1(x_, w, accO)

            def stageO2(j):
                col0, w = tile_geom(j)
                x_ = xt[j % 3]
                bk = sumsq2(accO, w)
                rstd_from(bk, rs2, w)
                n_ = hn[0]
                for k in range(8):
                    S.op(dve, lambda: nc.vector.scalar_tensor_tensor(out=n_[:, k, 0:w], in0=x_[:, k, 0:w], scalar=gO(k),
                                                                    in1=rs2[:, 0:w], op0=ALU.mult, op1=ALU.mult),
                         reads=[x_, rs2, vec], writes=[n_])
                t0 = (j - 1) * TA
                S.dma(sp, hnT_v[:, :, t0:t0 + w], n_[:, :, 0:w], reads=[n_])
                if t0 >= HALO:
                    S.dma(sp, h1T_v[:, :, t0 - HALO:t0 - HALO + w], x_[:, :, 0:w], reads=[x_])

            stageN0(0)
            stageN0(1)
            stageN1(0)
            stageN2(0)
            stagePa(0)
            stageN1(1)
            stageN2(1)
            for j in range(1, ntile):
                if j + 1 < ntile:
                    stageN0(j + 1)
                stagePa(j)
                if j > 1:
                    stageO2(j - 1)
                stagePb(j)
                if j + 1 < ntile:
                    stageN1(j + 1)
                stagePb2(j)
                if j + 1 < ntile:
                    stageN2(j + 1)
                stageO1(j)
            stageO2(ntile - 1)
            S.barrier()

    if "B" in phases:
        SCALE = 128.0 ** -0.5
        h1T_v = h1T_d.rearrange("(k p) t -> p k t", p=128)
        hnT_v = hnT_d.rearrange("(k p) t -> p k t", p=128)
        outT_v = outT.rearrange("(k p) t -> p k t", p=128)
        with ExitStack() as esB:
            def sbB(name, shape, dt):
                return Buf(esB.enter_context(nc.sbuf_tensor(name, list(shape), dt)))

            yT2 = sbB("yT2", [128, 16, OWN], BF16)
            vecB = sbB("vecB", [128, 32], F32)
            cw = sbB("cw", [128, 24], F32)
            ones_b = sbB("ones_b", [128, 128], BF16)
            ones_f2 = sbB("ones_f2", [128, 128], F32)
            ident = sbB("ident", [128, 128], BF16)
            S.dma(pool, ident[:, :], o_ident, writes=[ident])
            S.dma(sp, vecB[:, :], vecsB, writes=[vecB])
            S.dma(sp, cw[:, :], o_cw, writes=[cw])
            S.op(dve, lambda: nc.vector.memset(ones_b[:, :], 1.0), writes=[ones_b])
            S.op(dve, lambda: nc.vector.memset(ones_f2[:, :], 1.0), writes=[ones_f2])
            gF = lambda k: vecB[:, 24 + k:25 + k]

            with ExitStack() as esH:
                def sbH(name, shape, dt):
                    return Buf(esH.enter_context(nc.sbuf_tensor(name, list(shape), dt)))

                hnS_b = sbH("hnS", [128, 8, HALO + OWN], BF16)
                hnS = hnS_b.t
                hnK = [Buf(hnS) for _ in range(8)]
                wsl = [sbH(f"wsl{i}", [128, 8, 512], BF16) for i in range(2)]
                widx = [0]
                for k in range(8):
                    S.dma(sp, hnS[:, k, :], hnT_v[:, k, :], writes=[hnK[k]])

                def next_w(src_ap, ncols):
                    wb = wsl[widx[0] % 2]
                    widx[0] += 1
                    S.dma(pool, wb[:, :, 0:ncols], src_ap.rearrange("(k p) c -> p k c", p=128), writes=[wb])
                    return wb

                def proj(wb, c0, rhs_fn, w, out_fn=None):
                    bk = S.bank()
                    for k in range(8):
                        o_ap = bk[:, 0:w] if out_fn is None else out_fn(bk)
                        S.op(pe, lambda: nc.tensor.matmul(o_ap, lhsT=wb[:, k, c0:c0 + 128], rhs=rhs_fn(k),
                                                          start=(k == 0), stop=(k == 7)),
                             reads=[wb, hnK[k]], writes=[bk], mark=(k == 7))
                    return bk

                with ExitStack() as es1:
                    def sb1(name, shape, dt):
                        return Buf(es1.enter_context(nc.sbuf_tensor(name, list(shape), dt)))

                    QT = sb1("QT", [128, OWN], BF16)
                    KT = sb1("KT", [128, 4096], BF16)
                    Vt = sb1("Vt", [128, 32, 128], BF16)
                    VT = sb1("VT", [128, 4096], BF16)
                    U = sb1("U", [128, OWN], F32)
                    R = sb1("R", [128, OWN], F32)
                    Eb = [sb1(f"Eb{i}", [128, 512], F32) for i in range(3)]
                    PTb = [sb1(f"PTb{i}", [128, 512], BF16) for i in range(3)]
                    Wm = [sb1(f"Wm{i}", [128, 2, 256], F32) for i in range(2)]
                    tht = [sb1(f"tht{i}", [128, 512], F32) for i in range(2)]
                    pcount = [0]

                    for s_ in range(8):
                        for g in range(3):
                            hidx = s_ * 3 + g
                            d = DILS[g]
                            wb = next_w(o_wqkv[hidx], 384)
                            wm = Wm[hidx % 2]
                            S.dma(sp, wm[:, :, :], o_mask[hidx].rearrange("f k c -> k f c"), writes=[wm])
                            base = HALO - 128 * d

                            ntok = (d + 16) * 128
                            ntile_k = (ntok + 511) // 512

                            def nat(k, m):
                                st = base + 512 * m
                                w_ = min(512, ntok - 512 * m)
                                return hnS[:, k, st:st + w_], w_

                            def perm_dst(T, m, w_):
                                if d == 1:
                                    return T[:, 512 * m:512 * m + w_]
                                if d == 4:
                                    return T[:, 512 * m:512 * (m + 1)].rearrange("p (r i) -> p i r", r=4)
                                sb_, mm = m // 4, m % 4
                                return T[:, sb_ * 2048:(sb_ + 1) * 2048].rearrange("p (r i) -> p i r", r=16)[:, 32 * mm:32 * mm + 32, :]

                            def nat_src(bk, w_):
                                if d == 1:
                                    return bk[:, 0:w_]
                                return bk[:, 0:w_].rearrange("p (i r) -> p i r", r=d)

                            m_own0 = (128 * d) // 512 if d > 1 else None
                            for tt in range(4):
                                bk = proj(wb, 0, lambda k: hnS[:, k, HALO + tt * 512:HALO + (tt + 1) * 512], 512)
                                if d == 1:
                                    dst = QT[:, tt * 512:(tt + 1) * 512]
                                elif d == 4:
                                    dst = QT[:, tt * 512:(tt + 1) * 512].rearrange("p (r i) -> p i r", r=4)
                                else:
                                    dst = QT[:, :].rearrange("p (r i) -> p i r", r=16)[:, 32 * tt:32 * tt + 32, :]
                                S.op(act, lambda: nc.scalar.activation(out=dst, in_=nat_src(bk, 512), func=AF.Copy),
                                     reads=[bk], writes=[QT])
                            for m in range(ntile_k):
                                w_ = min(512, ntok - 512 * m)
                                bk = proj(wb, 128, lambda k: nat(k, m)[0], w_)
                                S.op(act, lambda: nc.scalar.activation(out=perm_dst(KT, m, w_), in_=nat_src(bk, w_), func=AF.Copy),
                                     reads=[bk], writes=[KT])
                                bk = proj(wb, 256, lambda k: nat(k, m)[0], w_)
                                S.op(dve, lambda: nc.vector.tensor_copy(out=VT[:, 512 * m:512 * m + w_], in_=bk[:, 0:w_]),
                                     reads=[bk], writes=[VT])
                            nblk_all = d + 16
                            bi = 0
                            while bi < nblk_all:
                                nb_ = min(4, nblk_all - bi)
                                bk = S.bank()
                                bkb = bk[:, :].bitcast(BF16)
                                for j in range(nb_):
                                    b_ = bi + j
                                    if d == 1:
                                        src = VT[:, b_ * 128:(b_ + 1) * 128]
                                    else:
                                        sb_, r = b_ // d, b_ % d
                                        src = VT[:, sb_ * 128 * d:(sb_ + 1) * 128 * d].rearrange("p (i r) -> p r i", r=d)[:, r, :]
                                    S.op(pe, lambda: nc.tensor.transpose(out=bkb[:, j * 128:(j + 1) * 128], in_=src, identity=ident[:, :]),
                                         reads=[VT, ident], writes=[bk], mark=(j == nb_ - 1))
                                S.op(dve, lambda: nc.vector.tensor_copy(out=Vt[:, bi:bi + nb_, :],
                                                                        in_=bkb[:, 0:nb_ * 128].rearrange("p (j e) -> p j e", j=nb_)),
                                     reads=[bk], writes=[Vt])
                                bi += nb_
                            pairs = [(quad, pair) for quad in range(4) for pair in range(2)]
                            st = {}

                            def att1(pi):
                                quad, pair = pairs[pi]
                                bS = S.bank()
                                blks = []
                                for j in range(2):
                                    qi = quad * 4 + pair * 2 + j
                                    sb_ = qi // d + 1
                                    r = qi % d
                                    bp = (sb_ - 1) * d + r
                                    bc = sb_ * d + r
                                    blks.append((qi, sb_, bp, bc))
                                    S.op(pe, lambda: nc.tensor.matmul(bS[:, j * 256:j * 256 + 128], lhsT=KT[:, bp * 128:(bp + 1) * 128],
                                                                      rhs=QT[:, qi * 128:(qi + 1) * 128], start=True, stop=True),
                                         reads=[KT, QT], writes=[bS], mark=False)
                                    S.op(pe, lambda: nc.tensor.matmul(bS[:, j * 256 + 128:j * 256 + 256], lhsT=KT[:, bc * 128:(bc + 1) * 128],
                                                                      rhs=QT[:, qi * 128:(qi + 1) * 128], start=True, stop=True),
                                         reads=[KT, QT], writes=[bS], mark=(j == 1))
                                E = Eb[pcount[0] % 3]
                                PT = PTb[pcount[0] % 3]
                                pcount[0] += 1
                                S.op(act, lambda: nc.scalar.activation(out=E[:, :], in_=bS[:, :], func=AF.Exp, scale=SCALE),
                                     reads=[bS], writes=[E])
                                f0 = 0 if blks[0][1] == 1 else 1
                                f1 = 0 if blks[1][1] == 1 else 1
                                if f0 == f1:
                                    S.op(dve, lambda: nc.vector.tensor_tensor(
                                        out=PT[:, :].rearrange("p (j c) -> p j c", j=2), in0=E[:, :].rearrange("p (j c) -> p j c", j=2),
                                        in1=wm[:, f0, :].unsqueeze(1).broadcast_to([128, 2, 256]), op=ALU.mult),
                                         reads=[E, wm], writes=[PT])
                                else:
                                    S.op(dve, lambda: nc.vector.tensor_tensor(out=PT[:, 0:256], in0=E[:, 0:256], in1=wm[:, f0, :],
                                                                             op=ALU.mult), reads=[E, wm], writes=[PT])
                                    S.op(dve, lambda: nc.vector.tensor_tensor(out=PT[:, 256:512], in0=E[:, 256:512], in1=wm[:, f1, :],
                                                                             op=ALU.mult), reads=[E, wm], writes=[PT])
                                st[pi] = (blks, PT)

                            def att2(pi):
                                quad, pair = pairs[pi]
                                blks, PT = st.pop(pi)
                                if pair == 0:
                                    st["bO"] = S.bank()
                                    st["bR"] = S.bank()
                                bO, bR = st["bO"], st["bR"]
                                for j in range(2):
                                    qi, sb_, bp, bc = blks[j]
                                    qc = (pair * 2 + j) * 128
                                    S.op(pe, lambda: nc.tensor.matmul(bO[:, qc:qc + 128], lhsT=Vt[:, bp, :], rhs=PT[:, j * 256:j * 256 + 128],
                                                                      start=True, stop=False), reads=[Vt, PT], writes=[bO], mark=False)
                                    S.op(pe, lambda: nc.tensor.matmul(bO[:, qc:qc + 128], lhsT=Vt[:, bc, :], rhs=PT[:, j * 256 + 128:j * 256 + 256],
                                                                      start=False, stop=True), reads=[Vt, PT], writes=[bO], mark=False)
                                    S.op(pe, lambda: nc.tensor.matmul(bR[:, qc:qc + 128], lhsT=ones_b[:, :], rhs=PT[:, j * 256:j * 256 + 128],
                                                                      start=True, stop=False), reads=[ones_b, PT], writes=[bR], mark=False)
                                    S.op(pe, lambda: nc.tensor.matmul(bR[:, qc:qc + 128], lhsT=ones_b[:, :], rhs=PT[:, j * 256 + 128:j * 256 + 256],
                                                                      start=False, stop=True), reads=[ones_b, PT], writes=[bR], mark=(j == 1))
                                if pair == 1:
                                    if d == 1:
                                        uo = lambda T: T[:, quad * 512:(quad + 1) * 512]
                                        bi_ = lambda b: b[:, :]
                                    elif d == 4:
                                        uo = lambda T: T[:, quad * 512:(quad + 1) * 512].rearrange("p (i r) -> p r i", r=4)
                                        bi_ = lambda b: b[:, :].rearrange("p (r i) -> p r i", r=4)
                                    else:
                                        uo = lambda T: T[:, :].rearrange("p (i r) -> p r i", r=16)[:, 4 * quad:4 * quad + 4, :]
                                        bi_ = lambda b: b[:, :].rearrange("p (r i) -> p r i", r=4)
                                    if g == 0:
                                        S.op(act, lambda: nc.scalar.activation(out=uo(U), in_=bi_(bO), func=AF.Copy), reads=[bO], writes=[U])
                                        S.op(act, lambda: nc.scalar.activation(out=uo(R), in_=bi_(bR), func=AF.Copy), reads=[bR], writes=[R])
                                    else:
                                        S.op(dve, lambda: nc.vector.tensor_tensor(out=uo(U), in0=bi_(bO), in1=uo(U), op=ALU.add),
                                             reads=[bO, U], writes=[U])
                                        S.op(dve, lambda: nc.vector.tensor_tensor(out=uo(R), in0=bi_(bR), in1=uo(R), op=ALU.add),
                                             reads=[bR, R], writes=[R])

                            att1(0)
                            att1(1)
                            for pi in range(8):
                                if pi + 2 < 8:
                                    att1(pi + 2)
                                att2(pi)
                        wb = next_w(o_wgc[s_], 128)
                        for tt in range(4):
                            bk = proj(wb, 0, lambda k: hnS[:, k, HALO + tt * 512:HALO + (tt + 1) * 512], 512)
                            t_ = tht[tt % 2]
                            cs = slice(tt * 512, (tt + 1) * 512)
                            S.op(act, lambda: nc.scalar.activation(out=t_[:, :], in_=bk[:, :], func=AF.Tanh, scale=0.5), reads=[bk], writes=[t_])
                            S.op(dve, lambda: nc.vector.scalar_tensor_tensor(out=t_[:, :], in0=t_[:, :], scalar=1.0, in1=bk[:, :],
                                                                            op0=ALU.add, op1=ALU.mult), reads=[bk, t_], writes=[t_])
                            S.op(dve, lambda: nc.vector.reciprocal(out=R[:, cs], in_=R[:, cs]), reads=[R], writes=[R])
                            S.op(dve, lambda: nc.vector.tensor_tensor(out=U[:, cs], in0=U[:, cs], in1=R[:, cs], op=ALU.mult),
                                 reads=[U, R], writes=[U])
                            S.op(dve, lambda: nc.vector.scalar_tensor_tensor(out=yT2[:, s_, cs], in0=U[:, cs], scalar=0.5, in1=t_[:, :],
                                                                            op0=ALU.mult, op1=ALU.mult), reads=[U, t_], writes=[yT2])
                    S.barrier()

                with ExitStack() as es2:
                    def sb2(name, shape, dt):
                        return Buf(es2.enter_context(nc.sbuf_tensor(name, list(shape), dt)))

                    dc = sb2("dc", [128, 2 + OWN], F32)
                    zz = sb2("zz", [128, 2 + OWN], F32)
                    acc = sb2("acc", [128, OWN], F32)
                    th2 = sb2("th2", [128, OWN], F32)
                    own = lambda tt: (lambda k: hnS[:, k, HALO + tt * 512:HALO + (tt + 1) * 512])
                    hal2 = lambda k: hnS[:, k, HALO - 2:HALO]
                    for c in range(8):
                        wb = next_w(o_wd[c], 512)
                        bk = proj(wb, 128, hal2, 2)
                        S.op(act, lambda: nc.scalar.activation(out=dc[:, 0:2], in_=bk[:, 0:2], func=AF.Copy), reads=[bk], writes=[dc])
                        for tt in range(4):
                            bk = proj(wb, 128, own(tt), 512)
                            S.op(act, lambda: nc.scalar.activation(out=dc[:, 2 + tt * 512:2 + (tt + 1) * 512], in_=bk[:, :], func=AF.Copy),
                                 reads=[bk], writes=[dc])
                        bk = proj(wb, 256, hal2, 2)
                        S.op(dve, lambda: nc.vector.tensor_tensor(out=zz[:, 0:2], in0=bk[:, 0:2], in1=dc[:, 0:2], op=ALU.mult),
                             reads=[bk, dc], writes=[zz])
                        for tt in range(4):
                            bk = proj(wb, 256, own(tt), 512)
                            S.op(dve, lambda: nc.vector.tensor_tensor(out=zz[:, 2 + tt * 512:2 + (tt + 1) * 512], in0=bk[:, :],
                                                                     in1=dc[:, 2 + tt * 512:2 + (tt + 1) * 512], op=ALU.mult),
                                 reads=[bk, dc], writes=[zz])
                        S.op(dve, lambda: nc.vector.tensor_scalar(out=acc[:, :], in0=zz[:, 0:OWN], scalar1=cw[:, 3 * c:3 * c + 1], scalar2=None,
                                                                 op0=ALU.mult), reads=[zz, cw], writes=[acc])
                        S.op(dve, lambda: nc.vector.scalar_tensor_tensor(out=acc[:, :], in0=zz[:, 1:OWN + 1], scalar=cw[:, 3 * c + 1:3 * c + 2],
                                                                        in1=acc[:, :], op0=ALU.mult, op1=ALU.add), reads=[zz, cw, acc], writes=[acc])
                        S.op(dve, lambda: nc.vector.scalar_tensor_tensor(out=acc[:, :], in0=zz[:, 2:OWN + 2], scalar=cw[:, 3 * c + 2:3 * c + 3],
                                                                        in1=acc[:, :], op0=ALU.mult, op1=ALU.add), reads=[zz, cw, acc], writes=[acc])
                        for tt in range(4):
                            bk = proj(wb, 384, own(tt), 512)
                            tsl = th2[:, tt * 512:(tt + 1) * 512]
                            S.op(act, lambda: nc.scalar.activation(out=tsl, in_=bk[:, :], func=AF.Tanh, scale=0.5), reads=[bk], writes=[th2])
                            S.op(dve, lambda: nc.vector.scalar_tensor_tensor(out=tsl, in0=tsl, scalar=1.0, in1=bk[:, :], op0=ALU.add, op1=ALU.mult),
                                 reads=[bk, th2], writes=[th2])
                        for tt in range(4):
                            bk = proj(wb, 0, own(tt), 512)
                            asl = acc[:, tt * 512:(tt + 1) * 512]
                            S.op(dve, lambda: nc.vector.tensor_tensor(out=asl, in0=bk[:, :], in1=asl, op=ALU.mult), reads=[bk, acc], writes=[acc])
                        S.op(dve, lambda: nc.vector.scalar_tensor_tensor(out=yT2[:, 8 + c, :], in0=acc[:, :], scalar=0.5, in1=th2[:, :],
                                                                        op0=ALU.mult, op1=ALU.mult), reads=[acc, th2], writes=[yT2])
                    S.barrier()

            with ExitStack() as es3:
                def sb3(name, shape, dt):
                    return Buf(es3.enter_context(nc.sbuf_tensor(name, list(shape), dt)))

                WO2 = [sb3(f"WO2_{i}", [128, 16, 128], BF16) for i in range(8)]
                h1t = [sb3(f"h1t{i}", [128, 8, 512], F32) for i in range(2)]
                sq3 = [sb3(f"sq3{i}", [128, 512], F32) for i in range(2)]
                rs3 = sb3("rs3", [128, 512], F32)
                acc3 = [sb3(f"acc3{i}", [128, 512], F32) for i in range(2)]
                wo_v = o_wout.rearrange("(k p) c -> p k c", p=128)
                for oc in range(8):
                    S.dma(pool, WO2[oc][:, :, :], wo_v[:, :, oc * 128:(oc + 1) * 128], writes=[WO2[oc]])

                def b3a(tt):
                    h_ = h1t[tt % 2]
                    S.dma(sp, h_[:, :, :], h1T_v[:, :, tt * 512:(tt + 1) * 512], writes=[h_])
                    for oc in range(8):
                        bk = S.bank()
                        for kc in range(16):
                            S.op(pe, lambda: nc.tensor.matmul(bk[:, :], lhsT=WO2[oc][:, kc, :],
                                                              rhs=yT2[:, kc, tt * 512:(tt + 1) * 512], start=(kc == 0), stop=(kc == 15)),
                                 reads=[WO2[oc], yT2], writes=[bk], mark=(kc == 15))
                        S.op(dve, lambda: nc.vector.tensor_tensor(out=h_[:, oc, :], in0=bk[:, :], in1=h_[:, oc, :], op=ALU.add),
                             reads=[bk, h_], writes=[h_])
                    a3 = acc3[tt % 2]
                    for k in range(8):
                        q_ = sq3[k % 2]
                        S.op(act, lambda: nc.scalar.activation(out=q_[:, :], in_=h_[:, k, :], func=AF.Square), reads=[h_], writes=[q_])
                        if k == 1:
                            S.op(pool, lambda: nc.gpsimd.tensor_tensor(out=a3[:, :], in0=sq3[0][:, :], in1=sq3[1][:, :], op=ALU.add),
                                 reads=[sq3[0], sq3[1]], writes=[a3])
                        elif k > 1:
                            S.op(pool, lambda: nc.gpsimd.tensor_tensor(out=a3[:, :], in0=a3[:, :], in1=q_[:, :], op=ALU.add),
                                 reads=[a3, q_], writes=[a3])

                def b3b(tt):
                    h_ = h1t[tt % 2]
                    a3 = acc3[tt % 2]
                    bk = S.bank()
                    S.op(pe, lambda: nc.tensor.matmul(bk[:, :], lhsT=ones_f2[:, :], rhs=a3[:, :], start=True, stop=True),
                         reads=[ones_f2, a3], writes=[bk], mark=True)
                    rstd_from(bk, rs3, 512)
                    for k in range(8):
                        S.op(dve, lambda: nc.vector.scalar_tensor_tensor(out=h_[:, k, :], in0=h_[:, k, :], scalar=gF(k), in1=rs3[:, :],
                                                                        op0=ALU.mult, op1=ALU.mult), reads=[h_, rs3, vecB], writes=[h_])
                    S.dma(sp, outT_v[:, :, tt * 512:(tt + 1) * 512], h_[:, :, :], reads=[h_])

                b3a(0)
                for tt in range(4):
                    if tt + 1 < 4:
                        b3a(tt + 1)
                    b3b(tt)

    S.barrier()
    return nc


def host_inputs_A(inp):
    x = np.asarray(inp["x"], dtype=np.float32)
    maps = []
    e_win = np.ascontiguousarray(inp["even_w_in"][0])
    pw = np.asarray(inp["even_pool_w"][0]).reshape(4, 2, 128, 256)
    wsT = np.ascontiguousarray(np.transpose(np.asarray(inp["even_ws"][0]), (2, 0, 1)))
    tril = np.ascontiguousarray(np.broadcast_to((np.arange(128)[:, None] <= np.arange(128)[None, :]).astype(np.float32)[:, None, :],
                                                (128, 4, 128)))
    bs = np.asarray(inp["even_bs"][0]).reshape(1, 512)
    e_wout = np.ascontiguousarray(inp["even_w_out"][0])
    vecs = np.zeros((128, 32), np.float32)
    vecs[:, 0:8] = np.asarray(inp["even_norm"][0]).reshape(8, 128).T
    vecs[:, 8:16] = np.asarray(inp["odd_norm"][0]).reshape(8, 128).T
    vecs[:, 16:24] = np.asarray(inp["even_pool_scale"][0]).reshape(8, 128).T
    vecs[:, 24:32] = np.asarray(inp["final_norm"]).reshape(8, 128).T
    for c in range(8):
        b, q = c // 4, c % 4
        t0 = OWN * q - HALO - PRE
        xt = np.zeros((TH, D), np.float32)
        lo = max(t0, 0)
        xt[lo - t0:, :] = x[b, lo:t0 + TH, :]
        ic = np.zeros((4, 2, 16), np.float32)
        for g, w in enumerate(POOLS):
            ic[g, :, :] = 1.0 / w
            cnt = np.minimum(np.arange(16) + 1, w).astype(np.float32)
            if q == 1:
                ic[g, 0, :] = 1.0 / cnt
            if q == 0:
                ic[g, 1, :] = 1.0 / cnt
        maps.append({
            "xT": np.ascontiguousarray(xt.T), "e_win": e_win, "e_pw": np.ascontiguousarray(pw), "e_wsT": wsT, "e_tril": tril,
            "e_bs": np.ascontiguousarray(bs), "e_wout": e_wout, "vecs": vecs,
            "invc": np.ascontiguousarray(np.broadcast_to(ic.reshape(1, 128), (128, 128))),
        })
    return maps


def host_inputs_B(inp):
    w = np.asarray(inp["odd_w_in"][0], dtype=np.float32)
    wqkv = np.zeros((24, D, 384), np.float32)
    for s_ in range(8):
        for g in range(3):
            for qi in range(3):
                c0 = ((qi * 3 + g) * 8 + s_) * 128
                wqkv[s_ * 3 + g, :, qi * 128:(qi + 1) * 128] = w[:, c0:c0 + 128]
    wgc = np.stack([w[:, 9216 + s_ * 128:9216 + (s_ + 1) * 128] for s_ in range(8)], 0)
    wd = np.stack([np.concatenate([w[:, 10240 + j * 1024 + c * 128:10240 + j * 1024 + (c + 1) * 128] for j in range(4)], 1)
                   for c in range(8)], 0)
    cwv = np.asarray(inp["odd_conv_w"][0], dtype=np.float32)
    cw = np.zeros((128, 24), np.float32)
    for c in range(8):
        for j in range(3):
            cw[:, c * 3 + j] = cwv[j, c * 128:(c + 1) * 128]
    vecs = np.zeros((128, 32), np.float32)
    vecs[:, 24:32] = np.asarray(inp["final_norm"]).reshape(8, 128).T
    kk = np.arange(128)[:, None].astype(np.float64)
    qq = np.arange(128)[None, :].astype(np.float64)
    slopes = 2.0 ** (-8.0 * (np.arange(24) + 1) / 24)
    masks = []
    for hv in (0.0, 1.0):
        m = np.zeros((24, 2, 128, 256), np.float32)
        for s_ in range(8):
            for g in range(3):
                sl = float(np.float32(slopes[g * 8 + s_])) * DILS[g]
                wprev = (kk >= qq) * np.exp(-sl * (qq + 128 - kk))
                wcur = (kk <= qq) * np.exp(-sl * (qq - kk))
                first = np.concatenate([wprev * hv, wcur], 1)
                rest = np.concatenate([wprev, wcur], 1)
                m[s_ * 3 + g, 0] = first
                m[s_ * 3 + g, 1] = rest
        masks.append(m)
    wout = np.ascontiguousarray(inp["odd_w_out"][0], dtype=np.float32)
    maps = []
    for c in range(8):
        q = c % 4
        maps.append({"o_wqkv": wqkv, "o_wgc": np.ascontiguousarray(wgc), "o_wd": np.ascontiguousarray(wd), "o_wout": wout,
                     "o_cw": cw, "o_mask": masks[0 if q == 0 else 1], "vecsB": vecs, "o_ident": np.eye(128, dtype=np.float32)})
    return maps


_NC_CACHE = {}


def kernel(**inputs):
    inp = {k: np.asarray(v) for k, v in inputs.items()}
    if "AB" not in _NC_CACHE:
        _NC_CACHE["AB"] = build_program("AB")
    nc = _NC_CACHE["AB"]
    mA = host_inputs_A(inp)
    mB = host_inputs_B(inp)
    maps = [dict(a, **b) for a, b in zip(mA, mB)]
    res = run_bass_kernel_spmd(nc, maps, core_ids=list(range(8)))
    out = np.zeros((NB, SEQ, D), np.float32)
    for c in range(8):
        b, q = c // 4, c % 4
        out[b, q * OWN:(q + 1) * OWN, :] = np.asarray(res.results[c]["outT"]).T
    return out
```

```python
import numpy as np
from contextlib import ExitStack
import concourse.bass as bass
import concourse.mybir as mybir
from concourse.bass_utils import run_bass_kernel_spmd

F32 = mybir.dt.float32
BF16 = mybir.dt.bfloat16
AF = mybir.ActivationFunctionType
ALU = mybir.AluOpType

D = 1024
SEQ = 8192
NB = 2
OWN = 2048
HALO = 2048
PRE = 16
TH = PRE + HALO + OWN
TA = 256
EPS = 1e-6
POOLS = (2, 4, 8, 16)
DILS = (1, 4, 16)
NDS = 16


class Sem:
    __slots__ = ("h", "i")

    def __init__(self, h, i):
        self.h = h
        self.i = i


class Buf:
    __slots__ = ("t", "writer", "readers")

    def __init__(self, t=None):
        self.t = t
        self.writer = None
        self.readers = []

    def __getitem__(self, k):
        return self.t[k]


class Eng:
    def __init__(self, S, name, e, is_pe=False):
        self.e = e
        self.name = name
        self.sem = S.newsem("s_" + name)
        self.cnt = 0
        self.known = {}
        self.is_pe = is_pe


class Sched:
    def __init__(self, nc):
        self.nc = nc
        self.nsem = 0
        self.pe = Eng(self, "pe", nc.tensor, True)
        self.act = Eng(self, "act", nc.scalar)
        self.dve = Eng(self, "dve", nc.vector)
        self.pool = Eng(self, "pool", nc.gpsimd)
        self.sp = Eng(self, "sp", nc.sync)
        self.engs = [self.pe, self.act, self.dve, self.pool, self.sp]
        self.dsems = [self.newsem(f"d{i}") for i in range(2 * NDS)]
        self.dcnt = [0] * (2 * NDS)
        self.dnext = [0, 0]
        self.nbank = 0
        self.banks = []

    def newsem(self, name):
        s = Sem(self.nc.alloc_semaphore(name), self.nsem)
        self.nsem += 1
        return s

    def _wait(self, eng, tok):
        sem, val = tok
        if eng.known.get(sem.i, 0) >= val:
            return
        eng.e.wait_ge(sem.h, val)
        eng.known[sem.i] = val

    def _deps(self, eng, reads, writes):
        for b in reads:
            if b.writer is not None:
                if b.writer[0] is eng.sem and eng.is_pe:
                    continue
                self._wait(eng, b.writer)
        for b in writes:
            if b.writer is not None and not (b.writer[0] is eng.sem and eng.is_pe):
                self._wait(eng, b.writer)
            for r in b.readers:
                if not (r[0] is eng.sem and eng.is_pe):
                    self._wait(eng, r)

    def _record(self, tok, reads, writes):
        for b in reads:
            b.readers.append(tok)
        for b in writes:
            b.writer = tok
            b.readers = []

    def op(self, eng, fn, reads=(), writes=(), mark=True):
        self._deps(eng, reads, writes)
        ins = fn()
        tok = (eng.sem, eng.cnt + 1)
        if mark:
            ins.then_inc(eng.sem.h, 1)
            eng.cnt += 1
        self._record(tok, reads, writes)
        return ins

    def dma(self, q, out, in_, reads=(), writes=()):
        self._deps(q, reads, writes)
        ring = 1 if q is self.pool else 0
        i = ring * NDS + self.dnext[ring]
        self.dnext[ring] = (self.dnext[ring] + 1) % NDS
        if self.dcnt[i] > 0:
            self._wait(q, (self.dsems[i], self.dcnt[i]))
        ins = q.e.dma_start(out=out, in_=in_)
        self.dcnt[i] += 16
        ins.then_inc(self.dsems[i].h, 16)
        self._record((self.dsems[i], self.dcnt[i]), reads, writes)

    def barrier(self):
        for e in self.engs:
            for f in self.engs:
                if f is not e and f.cnt > 0:
                    self._wait(e, (f.sem, f.cnt))
            for i in range(2 * NDS):
                if self.dcnt[i] > 0:
                    self._wait(e, (self.dsems[i], self.dcnt[i]))

    def bank(self):
        b = self.banks[self.nbank % len(self.banks)]
        self.nbank += 1
        return b


def build_program(phases="AB"):
    nc = bass.Bass("TRN2", target_bir_lowering=False)
    S = Sched(nc)
    pe, act, dve, pool, sp = S.pe, S.act, S.dve, S.pool, S.sp
    fused = phases == "AB"

    def dram(name, shape, dt, kind):
        return nc.dram_tensor(name, list(shape), dt, kind=kind).ap()

    mid_kind_out = "Internal" if fused else "ExternalOutput"
    mid_kind_in = "Internal" if fused else "ExternalInput"
    if "A" in phases:
        xT = dram("xT", [D, TH], F32, "ExternalInput")
        e_win = dram("e_win", [D, 5120], F32, "ExternalInput")
        e_pw = dram("e_pw", [4, 2, 128, 256], F32, "ExternalInput")
        e_wsT = dram("e_wsT", [128, 4, 128], F32, "ExternalInput")
        e_tril = dram("e_tril", [128, 4, 128], F32, "ExternalInput")
        e_bs = dram("e_bs", [1, 512], F32, "ExternalInput")
        e_wout = dram("e_wout", [2048, D], F32, "ExternalInput")
        vecs = dram("vecs", [128, 32], F32, "ExternalInput")
        invc = dram("invc", [128, 128], F32, "ExternalInput")
        h1T_d = dram("h1T", [D, OWN], F32, mid_kind_out)
        hnT_d = dram("hnT", [D, HALO + OWN], BF16, mid_kind_out)
    else:
        h1T_d = dram("h1T", [D, OWN], F32, mid_kind_in)
        hnT_d = dram("hnT", [D, HALO + OWN], BF16, mid_kind_in)
    if "B" in phases:
        o_wqkv = dram("o_wqkv", [24, D, 384], F32, "ExternalInput")
        o_wgc = dram("o_wgc", [8, D, 128], F32, "ExternalInput")
        o_wd = dram("o_wd", [8, D, 512], F32, "ExternalInput")
        o_wout = dram("o_wout", [2048, D], F32, "ExternalInput")
        o_cw = dram("o_cw", [128, 24], F32, "ExternalInput")
        o_mask = dram("o_mask", [24, 2, 128, 256], F32, "ExternalInput")
        o_ident = dram("o_ident", [128, 128], F32, "ExternalInput")
        vecsB = dram("vecsB", [128, 32], F32, "ExternalInput")
        outT = dram("outT", [D, OWN], F32, "ExternalOutput")

    S.banks = [Buf(nc.alloc_psum_tensor(f"bank{i}", [128, 512], F32)) for i in range(8)]

    epsb = Buf(nc.alloc_sbuf_tensor("epsb", [128, 1], F32))
    S.op(dve, lambda: nc.vector.memset(epsb[:, :], EPS), writes=[epsb])

    def rstd_from(bk, rs_buf, w):
        S.op(act, lambda: nc.scalar.activation(out=rs_buf[:, 0:w], in_=bk[:, 0:w], func=AF.Sqrt, bias=epsb[:, 0:1],
                                               scale=1.0 / D), reads=[bk, epsb], writes=[rs_buf])
        S.op(dve, lambda: nc.vector.reciprocal(out=rs_buf[:, 0:w], in_=rs_buf[:, 0:w]), reads=[rs_buf], writes=[rs_buf])

    if "A" in phases:
        with ExitStack() as es:
            def sb(name, shape, dt):
                return Buf(es.enter_context(nc.sbuf_tensor(name, list(shape), dt)))

            WI = [sb(f"WI{i}", [128, 8, 1024], BF16) for i in range(5)]
            WO = sb("WO", [128, 16, D], BF16)
            PW = sb("PW", [128, 4, 2, 256], BF16)
            wsT_f = sb("wsT_f", [128, 4, 128], F32)
            tril_f = sb("tril_f", [128, 4, 128], F32)
            wsTm = sb("wsTm", [128, 4, 128], BF16)
            bsr = sb("bsr", [1, 512], BF16)
            ones_row = sb("ones_row", [1, 128], BF16)
            ones_f = sb("ones_f", [128, 128], F32)
            vec = sb("vec", [128, 32], F32)
            invc_s = sb("invc_s", [128, 128], F32)
            xt = [sb(f"xt{i}", [128, 8, TA], F32) for i in range(3)]
            sq = [sb(f"sq{i}", [128, TA], F32) for i in range(2)]
            rs = sb("rs", [128, TA], F32)
            rs2 = sb("rs2", [128, TA], F32)
            hT = [sb(f"hT{i}", [128, 8, TA], BF16) for i in range(2)]
            aT = [sb(f"aT{i}", [128, PRE + TA], F32) for i in range(8)]
            Tm = [sb(f"Tm{i}", [128, PRE + TA], F32) for i in range(2)]
            pooled = [sb(f"pooled{i}", [128, TA], BF16) for i in range(8)]
            accN = sb("accN", [128, TA], F32)
            accO = sb("accO", [128, TA], F32)
            tmp16 = sb("tmp16", [128, 16], F32)
            sga = [sb(f"sga{i}", [128, TA], F32) for i in range(2)]
            sgb = [sb(f"sgb{i}", [128, TA], F32) for i in range(2)]
            ugb = [sb(f"ugb{i}", [128, TA], F32) for i in range(2)]
            vtm = [sb(f"vtm{i}", [128, 1024], BF16) for i in range(TA // 128)]
            yT = [sb(f"yT{i}", [128, 16, TA], BF16) for i in range(2)]
            hn = [sb(f"hn{i}", [128, 8, TA], BF16) for i in range(1)]

            S.dma(sp, vec[:, :], vecs, writes=[vec])
            S.dma(sp, invc_s[:, :], invc, writes=[invc_s])
            S.dma(sp, wsT_f[:, :, :], e_wsT, writes=[wsT_f])
            S.dma(sp, tril_f[:, :, :], e_tril, writes=[tril_f])
            S.op(dve, lambda: nc.vector.memset(ones_f[:, :], 1.0), writes=[ones_f])
            S.op(dve, lambda: nc.vector.memset(ones_row[:, :], 1.0), writes=[ones_row])
            S.op(dve, lambda: nc.vector.tensor_tensor(out=wsTm[:, :, :], in0=wsT_f[:, :, :], in1=tril_f[:, :, :],
                                                     op=ALU.mult), reads=[wsT_f, tril_f], writes=[wsTm])
            S.dma(pool, bsr[:, :], e_bs, writes=[bsr])
            win_v = e_win.rearrange("(k p) c -> p k c", p=128)
            S.dma(pool, WI[0][:, :, :], win_v[:, :, 0:1024], writes=[WI[0]])
            S.dma(pool, PW[:, :, :, :], e_pw.rearrange("g cc p d -> p g cc d"), writes=[PW])
            for i in (1, 3, 2, 4):
                S.dma(pool, WI[i][:, :, :], win_v[:, :, i * 1024:(i + 1) * 1024], writes=[WI[i]])
            S.dma(pool, WO[:, :, :], e_wout.rearrange("(k p) c -> p k c", p=128), writes=[WO])

            xT_v = xT.rearrange("(k p) t -> p k t", p=128)
            h1T_v = h1T_d.rearrange("(k p) t -> p k t", p=128)
            hnT_v = hnT_d.rearrange("(k p) t -> p k t", p=128)
            gE = lambda k: vec[:, k:k + 1]
            gO = lambda k: vec[:, 8 + k:9 + k]
            psc = lambda k: vec[:, 16 + k:17 + k]

            ntile = 1 + (HALO + OWN) // TA

            def tile_geom(j):
                if j == 0:
                    return 0, PRE
                return PRE + (j - 1) * TA, TA

            def sumsq1(src_buf, w, accb):
                for k in range(8):
                    q_ = sq[k % 2]
                    S.op(act, lambda: nc.scalar.activation(out=q_[:, 0:w], in_=src_buf[:, k, 0:w], func=AF.Square),
                         reads=[src_buf], writes=[q_])
                    if k == 1:
                        S.op(pool, lambda: nc.gpsimd.tensor_tensor(out=accb[:, 0:w], in0=sq[0][:, 0:w], in1=sq[1][:, 0:w], op=ALU.add),
                             reads=[sq[0], sq[1]], writes=[accb])
                    elif k > 1:
                        S.op(pool, lambda: nc.gpsimd.tensor_tensor(out=accb[:, 0:w], in0=accb[:, 0:w], in1=q_[:, 0:w], op=ALU.add),
                             reads=[accb, q_], writes=[accb])

            def sumsq2(accb, w):
                bk = S.bank()
                S.op(pe, lambda: nc.tensor.matmul(bk[:, 0:w], lhsT=ones_f[:, :], rhs=accb[:, 0:w], start=True, stop=True),
                     reads=[ones_f, accb], writes=[bk], mark=True)
                return bk

            def stageN0(j):
                col0, w = tile_geom(j)
                x_ = xt[j % 3]
                S.dma(sp, x_[:, :, 0:w], xT_v[:, :, col0:col0 + w], writes=[x_])

            def stageN1(j):
                col0, w = tile_geom(j)
                x_ = xt[j % 3]
                sumsq1(x_, w, accN)

            def stageN2(j):
                col0, w = tile_geom(j)
                x_ = xt[j % 3]
                bk = sumsq2(accN, w)
                rstd_from(bk, rs, w)
                h_ = hT[j % 2]
                for k in range(8):
                    S.op(dve, lambda: nc.vector.scalar_tensor_tensor(out=h_[:, k, 0:w], in0=x_[:, k, 0:w], scalar=gE(k),
                                                                    in1=rs[:, 0:w], op0=ALU.mult, op1=ALU.mult),
                         reads=[x_, rs, vec], writes=[h_])

            def proj_fm(Wb, c, h_, w):
                bk = S.bank()
                for k in range(8):
                    S.op(pe, lambda: nc.tensor.matmul(bk[:, 0:w], lhsT=Wb[:, k, c * 128:(c + 1) * 128], rhs=h_[:, k, 0:w],
                                                      start=(k == 0), stop=(k == 7)),
                         reads=[Wb, h_], writes=[bk], mark=(k == 7))
                return bk

            def stagePa(j):
                col0, w = tile_geom(j)
                h_ = hT[j % 2]
                for c in range(8):
                    bk = proj_fm(WI[0], c, h_, w)
                    dst0 = 0 if j == 0 else PRE
                    S.op(act, lambda: nc.scalar.activation(out=aT[c][:, dst0:dst0 + w], in_=bk[:, 0:w], func=AF.Copy),
                         reads=[bk], writes=[aT[c]])

            def pool_group(j, gi):
                col0, w = tile_geom(j)
                W_ = PRE + w
                wpool = POOLS[gi]
                for cc in range(2):
                    c = 2 * gi + cc
                    A = aT[c]
                    cur = A
                    lo = 0
                    sh = 1
                    for step in range(gi + 1):
                        nxt = Tm[step % 2]
                        S.op(dve, lambda: nc.vector.tensor_tensor(out=nxt[:, lo + sh:W_], in0=cur[:, lo + sh:W_],
                                                                 in1=cur[:, lo:W_ - sh], op=ALU.add),
                             reads=[cur], writes=[nxt])
                        cur = nxt
                        lo += sh
                        sh *= 2
                    pl = pooled[c]
                    S.op(dve, lambda: nc.vector.scalar_tensor_tensor(out=pl[:, 0:w], in0=cur[:, PRE:W_], scalar=1.0 / wpool,
                                                                    in1=A[:, PRE:W_], op0=ALU.mult, op1=ALU.subtract),
                         reads=[cur, A], writes=[pl])
                    if j in (1, 1 + HALO // TA):
                        pos = 0 if j == 1 else 1
                        o0 = (gi * 2 + pos) * 16
                        S.op(dve, lambda: nc.vector.tensor_tensor(out=tmp16[:, :], in0=cur[:, PRE:PRE + 16],
                                                                 in1=invc_s[:, o0:o0 + 16], op=ALU.mult),
                             reads=[cur, invc_s], writes=[tmp16])
                        S.op(dve, lambda: nc.vector.tensor_tensor(out=pl[:, 0:16], in0=tmp16[:, :],
                                                                 in1=A[:, PRE:PRE + 16], op=ALU.subtract),
                             reads=[tmp16, A, pl], writes=[pl])
                    S.op(pool, lambda: nc.gpsimd.tensor_copy(out=A[:, 0:PRE], in_=A[:, w:w + PRE]),
                         reads=[A], writes=[A])

            def v_proj(j):
                col0, w = tile_geom(j)
                h_ = hT[j % 2]
                for tc in range(w // 128):
                    for half in range(2):
                        bk = S.bank()
                        for k in range(8):
                            S.op(pe, lambda: nc.tensor.matmul(bk[:, :], lhsT=h_[:, k, tc * 128:(tc + 1) * 128],
                                                              rhs=WI[3][:, k, half * 512:(half + 1) * 512],
                                                              start=(k == 0), stop=(k == 7)),
                                 reads=[WI[3], h_], writes=[bk], mark=(k == 7))
                        S.op(act, lambda: nc.scalar.activation(out=vtm[tc][:, half * 512:(half + 1) * 512], in_=bk[:, :],
                                                               func=AF.Copy), reads=[bk], writes=[vtm[tc]])

            def gmlp_cc(j, cc):
                col0, w = tile_geom(j)
                h_ = hT[j % 2]
                y_ = yT[j % 2]
                g = cc // 2
                bu = proj_fm(WI[2], cc, h_, w)
                bg = proj_fm(WI[4], cc, h_, w)
                bm = S.bank()
                for tc in range(w // 128):
                    S.op(pe, lambda: nc.tensor.matmul(bm[:, tc * 128:(tc + 1) * 128], lhsT=vtm[tc][:, cc * 128:(cc + 1) * 128],
                                                      rhs=wsTm[:, g, :], start=True, stop=False),
                         reads=[vtm[tc], wsTm], writes=[bm], mark=False)
                    S.op(pe, lambda: nc.tensor.matmul(bm[:, tc * 128:(tc + 1) * 128], lhsT=ones_row[0:1, :],
                                                      rhs=bsr[0:1, g * 128:(g + 1) * 128], start=False, stop=True),
                         reads=[ones_row, bsr], writes=[bm], mark=(tc == w // 128 - 1))
                sg = sgb[cc % 2]
                ug = ugb[cc % 2]
                S.op(act, lambda: nc.scalar.activation(out=sg[:, 0:w], in_=bg[:, 0:w], func=AF.Silu),
                     reads=[bg], writes=[sg])
                S.op(dve, lambda: nc.vector.tensor_tensor(out=ug[:, 0:w], in0=bu[:, 0:w], in1=sg[:, 0:w], op=ALU.mult),
                     reads=[bu, sg], writes=[ug])
                S.op(dve, lambda: nc.vector.tensor_tensor(out=y_[:, 8 + cc, 0:w], in0=bm[:, 0:w], in1=ug[:, 0:w], op=ALU.mult),
                     reads=[bm, ug], writes=[y_])

            def pool_mix(j, gi):
                col0, w = tile_geom(j)
                h_ = hT[j % 2]
                y_ = yT[j % 2]
                for dch in range(2):
                    co = 2 * gi + dch
                    bm = S.bank()
                    for cc in range(2):
                        S.op(pe, lambda: nc.tensor.matmul(bm[:, 0:w], lhsT=PW[:, gi, cc, dch * 128:(dch + 1) * 128],
                                                          rhs=pooled[2 * gi + cc][:, 0:w],
                                                          start=(cc == 0), stop=(cc == 1)),
                             reads=[PW, pooled[2 * gi + cc]], writes=[bm], mark=(cc == 1))
                    bg = proj_fm(WI[1], co, h_, w)
                    sg = sga[co % 2]
                    S.op(act, lambda: nc.scalar.activation(out=sg[:, 0:w], in_=bg[:, 0:w], func=AF.Silu),
                         reads=[bg], writes=[sg])
                    S.op(dve, lambda: nc.vector.scalar_tensor_tensor(out=y_[:, co, 0:w], in0=bm[:, 0:w], scalar=psc(co),
                                                                    in1=sg[:, 0:w], op0=ALU.mult, op1=ALU.mult),
                         reads=[bm, sg, vec], writes=[y_])

            def stagePb(j):
                v_proj(j)

            def stagePb2(j):
                for gi in range(4):
                    gmlp_cc(j, 2 * gi)
                    gmlp_cc(j, 2 * gi + 1)
                    pool_group(j, gi)
                    if gi >= 1:
                        pool_mix(j, gi - 1)
                pool_mix(j, 3)

            def stageO1(j):
                col0, w = tile_geom(j)
                x_ = xt[j % 3]
                y_ = yT[j % 2]
                for oc in range(8):
                    bk = S.bank()
                    for kc in range(16):
                        S.op(pe, lambda: nc.tensor.matmul(bk[:, 0:w], lhsT=WO[:, kc, oc * 128:(oc + 1) * 128], rhs=y_[:, kc, 0:w],
                                                          start=(kc == 0), stop=(kc == 15)),
                             reads=[WO, y_], writes=[bk], mark=(kc == 15))
                    S.op(dve, lambda: nc.vector.tensor_tensor(out=x_[:, oc, 0:w], in0=bk[:, 0:w], in1=x_[:, oc, 0:w], op=ALU.add),
                         reads=[bk, x_], writes=[x_])
                sumsq1(x_, w, accO)

            def stageO2(j):
                col0, w = tile_geom(j)
                x_ = xt[j % 3]
                bk = sumsq2(accO, w)
                rstd_from(bk, rs2, w)
                n_ = hn[0]
                for k in range(8):
                    S.op(dve, lambda: nc.vector.scalar_tensor_tensor(out=n_[:, k, 0:w], in0=x_[:, k, 0:w], scalar=gO(k),
                                                                    in1=rs2[:, 0:w], op0=ALU.mult, op1=ALU.mult),
                         reads=[x_, rs2, vec], writes=[n_])
                t0 = (j - 1) * TA
                S.dma(sp, hnT_v[:, :, t0:t0 + w], n_[:, :, 0:w], reads=[n_])
                if t0 >= HALO:
                    S.dma(sp, h1T_v[:, :, t0 - HALO:t0 - HALO + w], x_[:, :, 0:w], reads=[x_])

            stageN0(0)
            stageN0(1)
            stageN1(0)
            stageN2(0)
            stagePa(0)
            stageN1(1)
            stageN2(1)
            for j in range(1, ntile):
                if j + 1 < ntile:
                    stageN0(j + 1)
                stagePa(j)
                if j > 1:
                    stageO2(j - 1)
                stagePb(j)
                if j + 1 < ntile:
                    stageN1(j + 1)
                stagePb2(j)
                if j + 1 < ntile:
                    stageN2(j + 1)
                stageO1(j)
            stageO2(ntile - 1)
            S.barrier()

    if "B" in phases:
        SCALE = 128.0 ** -0.5
        h1T_v = h1T_d.rearrange("(k p) t -> p k t", p=128)
        hnT_v = hnT_d.rearrange("(k p) t -> p k t", p=128)
        outT_v = outT.rearrange("(k p) t -> p k t", p=128)
        with ExitStack() as esB:
            def sbB(name, shape, dt):
                return Buf(esB.enter_context(nc.sbuf_tensor(name, list(shape), dt)))

            yT2 = sbB("yT2", [128, 16, OWN], BF16)
            vecB = sbB("vecB", [128, 32], F32)
            cw = sbB("cw", [128, 24], F32)
            ones_b = sbB("ones_b", [128, 128], BF16)
            ones_f2 = sbB("ones_f2", [128, 128], F32)
            ident = sbB("ident", [128, 128], BF16)
            S.dma(pool, ident[:, :], o_ident, writes=[ident])
            S.dma(sp, vecB[:, :], vecsB, writes=[vecB])
            S.dma(sp, cw[:, :], o_cw, writes=[cw])
            S.op(dve, lambda: nc.vector.memset(ones_b[:, :], 1.0), writes=[ones_b])
            S.op(dve, lambda: nc.vector.memset(ones_f2[:, :], 1.0), writes=[ones_f2])
            gF = lambda k: vecB[:, 24 + k:25 + k]

            with ExitStack() as esH:
                def sbH(name, shape, dt):
                    return Buf(esH.enter_context(nc.sbuf_tensor(name, list(shape), dt)))

                hnS_b = sbH("hnS", [128, 8, HALO + OWN], BF16)
                hnS = hnS_b.t
                hnK = [Buf(hnS) for _ in range(8)]
                wsl = [sbH(f"wsl{i}", [128, 8, 512], BF16) for i in range(2)]
                widx = [0]
                for k in range(8):
                    S.dma(sp, hnS[:, k, :], hnT_v[:, k, :], writes=[hnK[k]])

                def next_w(src_ap, ncols):
                    wb = wsl[widx[0] % 2]
                    widx[0] += 1
                    S.dma(pool, wb[:, :, 0:ncols], src_ap.rearrange("(k p) c -> p k c", p=128), writes=[wb])
                    return wb

                def proj(wb, c0, rhs_fn, w, out_fn=None):
                    bk = S.bank()
                    for k in range(8):
                        o_ap = bk[:, 0:w] if out_fn is None else out_fn(bk)
                        S.op(pe, lambda: nc.tensor.matmul(o_ap, lhsT=wb[:, k, c0:c0 + 128], rhs=rhs_fn(k),
                                                          start=(k == 0), stop=(k == 7)),
                             reads=[wb, hnK[k]], writes=[bk], mark=(k == 7))
                    return bk

                with ExitStack() as es1:
                    def sb1(name, shape, dt):
                        return Buf(es1.enter_context(nc.sbuf_tensor(name, list(shape), dt)))

                    QT = sb1("QT", [128, OWN], BF16)
                    KT = sb1("KT", [128, 4096], BF16)
                    Vt = sb1("Vt", [128, 32, 128], BF16)
                    VT = sb1("VT", [128, 4096], BF16)
                    U = sb1("U", [128, OWN], F32)
                    R = sb1("R", [128, OWN], F32)
                    Eb = [sb1(f"Eb{i}", [128, 512], F32) for i in range(3)]
                    PTb = [sb1(f"PTb{i}", [128, 512], BF16) for i in range(3)]
                    Wm = [sb1(f"Wm{i}", [128, 2, 256], F32) for i in range(2)]
                    tht = [sb1(f"tht{i}", [128, 512], F32) for i in range(2)]
                    pcount = [0]

                    for s_ in range(8):
                        for g in range(3):
                            hidx = s_ * 3 + g
                            d = DILS[g]
                            wb = next_w(o_wqkv[hidx], 384)
                            wm = Wm[hidx % 2]
                            S.dma(sp, wm[:, :, :], o_mask[hidx].rearrange("f k c -> k f c"), writes=[wm])
                            base = HALO - 128 * d

                            ntok = (d + 16) * 128
                            ntile_k = (ntok + 511) // 512

                            def nat(k, m):
                                st = base + 512 * m
                                w_ = min(512, ntok - 512 * m)
                                return hnS[:, k, st:st + w_], w_

                            def perm_dst(T, m, w_):
                                if d == 1:
                                    return T[:, 512 * m:512 * m + w_]
                                if d == 4:
                                    return T[:, 512 * m:512 * (m + 1)].rearrange("p (r i) -> p i r", r=4)
                                sb_, mm = m // 4, m % 4
                                return T[:, sb_ * 2048:(sb_ + 1) * 2048].rearrange("p (r i) -> p i r", r=16)[:, 32 * mm:32 * mm + 32, :]

                            def nat_src(bk, w_):
                                if d == 1:
                                    return bk[:, 0:w_]
                                return bk[:, 0:w_].rearrange("p (i r) -> p i r", r=d)

                            m_own0 = (128 * d) // 512 if d > 1 else None
                            for tt in range(4):
                                bk = proj(wb, 0, lambda k: hnS[:, k, HALO + tt * 512:HALO + (tt + 1) * 512], 512)
                                if d == 1:
                                    dst = QT[:, tt * 512:(tt + 1) * 512]
                                elif d == 4:
                                    dst = QT[:, tt * 512:(tt + 1) * 512].rearrange("p (r i) -> p i r", r=4)
                                else:
                                    dst = QT[:, :].rearrange("p (r i) -> p i r", r=16)[:, 32 * tt:32 * tt + 32, :]
                                S.op(act, lambda: nc.scalar.activation(out=dst, in_=nat_src(bk, 512), func=AF.Copy),
                                     reads=[bk], writes=[QT])
                            for m in range(ntile_k):
                                w_ = min(512, ntok - 512 * m)
                                bk = proj(wb, 128, lambda k: nat(k, m)[0], w_)
                                S.op(act, lambda: nc.scalar.activation(out=perm_dst(KT, m, w_), in_=nat_src(bk, w_), func=AF.Copy),
                                     reads=[bk], writes=[KT])
                                bk = proj(wb, 256, lambda k: nat(k, m)[0], w_)
                                S.op(dve, lambda: nc.vector.tensor_copy(out=VT[:, 512 * m:512 * m + w_], in_=bk[:, 0:w_]),
                                     reads=[bk], writes=[VT])
                            nblk_all = d + 16
                            bi = 0
                            while bi < nblk_all:
                                nb_ = min(4, nblk_all - bi)
                                bk = S.bank()
                                bkb = bk[:, :].bitcast(BF16)
                                for j in range(nb_):
                                    b_ = bi + j
                                    if d == 1:
                                        src = VT[:, b_ * 128:(b_ + 1) * 128]
                                    else:
                                        sb_, r = b_ // d, b_ % d
                                        src = VT[:, sb_ * 128 * d:(sb_ + 1) * 128 * d].rearrange("p (i r) -> p r i", r=d)[:, r, :]
                                    S.op(pe, lambda: nc.tensor.transpose(out=bkb[:, j * 128:(j + 1) * 128], in_=src, identity=ident[:, :]),
                                         reads=[VT, ident], writes=[bk], mark=(j == nb_ - 1))
                                S.op(dve, lambda: nc.vector.tensor_copy(out=Vt[:, bi:bi + nb_, :],
                                                                        in_=bkb[:, 0:nb_ * 128].rearrange("p (j e) -> p j e", j=nb_)),
                                     reads=[bk], writes=[Vt])
                                bi += nb_
                            pairs = [(quad, pair) for quad in range(4) for pair in range(2)]
                            st = {}

                            def att1(pi):
                                quad, pair = pairs[pi]
                                bS = S.bank()
                                blks = []
                                for j in range(2):
                                    qi = quad * 4 + pair * 2 + j
                                    sb_ = qi // d + 1
                                    r = qi % d
                                    bp = (sb_ - 1) * d + r
                                    bc = sb_ * d + r
                                    blks.append((qi, sb_, bp, bc))
                                    S.op(pe, lambda: nc.tensor.matmul(bS[:, j * 256:j * 256 + 128], lhsT=KT[:, bp * 128:(bp + 1) * 128],
                                                                      rhs=QT[:, qi * 128:(qi + 1) * 128], start=True, stop=True),
                                         reads=[KT, QT], writes=[bS], mark=False)
                                    S.op(pe, lambda: nc.tensor.matmul(bS[:, j * 256 + 128:j * 256 + 256], lhsT=KT[:, bc * 128:(bc + 1) * 128],
                                                                      rhs=QT[:, qi * 128:(qi + 1) * 128], start=True, stop=True),
                                         reads=[KT, QT], writes=[bS], mark=(j == 1))
                                E = Eb[pcount[0] % 3]
                                PT = PTb[pcount[0] % 3]
                                pcount[0] += 1
                                S.op(act, lambda: nc.scalar.activation(out=E[:, :], in_=bS[:, :], func=AF.Exp, scale=SCALE),
                                     reads=[bS], writes=[E])
                                f0 = 0 if blks[0][1] == 1 else 1
                                f1 = 0 if blks[1][1] == 1 else 1
                                if f0 == f1:
                                    S.op(dve, lambda: nc.vector.tensor_tensor(
                                        out=PT[:, :].rearrange("p (j c) -> p j c", j=2), in0=E[:, :].rearrange("p (j c) -> p j c", j=2),
                                        in1=wm[:, f0, :].unsqueeze(1).broadcast_to([128, 2, 256]), op=ALU.mult),
                                         reads=[E, wm], writes=[PT])
                                else:
                                    S.op(dve, lambda: nc.vector.tensor_tensor(out=PT[:, 0:256], in0=E[:, 0:256], in1=wm[:, f0, :],
                                                                             op=ALU.mult), reads=[E, wm], writes=[PT])
                                    S.op(dve, lambda: nc.vector.tensor_tensor(out=PT[:, 256:512], in0=E[:, 256:512], in1=wm[:, f1, :],
                                                                             op=ALU.mult), reads=[E, wm], writes=[PT])
                                st[pi] = (blks, PT)

                            def att2(pi):
                                quad, pair = pairs[pi]
                                blks, PT = st.pop(pi)
                                if pair == 0:
                                    st["bO"] = S.bank()
                                    st["bR"] = S.bank()
                                bO, bR = st["bO"], st["bR"]
                                for j in range(2):
                                    qi, sb_, bp, bc = blks[j]
                                    qc = (pair * 2 + j) * 128
                                    S.op(pe, lambda: nc.tensor.matmul(bO[:, qc:qc + 128], lhsT=Vt[:, bp, :], rhs=PT[:, j * 256:j * 256 + 128],
                                                                      start=True, stop=False), reads=[Vt, PT], writes=[bO], mark=False)
                                    S.op(pe, lambda: nc.tensor.matmul(bO[:, qc:qc + 128], lhsT=Vt[:, bc, :], rhs=PT[:, j * 256 + 128:j * 256 + 256],
                                                                      start=False, stop=True), reads=[Vt, PT], writes=[bO], mark=False)
                                    S.op(pe, lambda: nc.tensor.matmul(bR[:, qc:qc + 128], lhsT=ones_b[:, :], rhs=PT[:, j * 256:j * 256 + 128],
                                                                      start=True, stop=False), reads=[ones_b, PT], writes=[bR], mark=False)
                                    S.op(pe, lambda: nc.tensor.matmul(bR[:, qc:qc + 128], lhsT=ones_b[:, :], rhs=PT[:, j * 256 + 128:j * 256 + 256],
                                                                      start=False, stop=True), reads=[ones_b, PT], writes=[bR], mark=(j == 1))
                                if pair == 1:
                                    if d == 1:
                                        uo = lambda T: T[:, quad * 512:(quad + 1) * 512]
                                        bi_ = lambda b: b[:, :]
                                    elif d == 4:
                                        uo = lambda T: T[:, quad * 512:(quad + 1) * 512].rearrange("p (i r) -> p r i", r=4)
                                        bi_ = lambda b: b[:, :].rearrange("p (r i) -> p r i", r=4)
                                    else:
                                        uo = lambda T: T[:, :].rearrange("p (i r) -> p r i", r=16)[:, 4 * quad:4 * quad + 4, :]
                                        bi_ = lambda b: b[:, :].rearrange("p (r i) -> p r i", r=4)
                                    if g == 0:
                                        S.op(act, lambda: nc.scalar.activation(out=uo(U), in_=bi_(bO), func=AF.Copy), reads=[bO], writes=[U])
                                        S.op(act, lambda: nc.scalar.activation(out=uo(R), in_=bi_(bR), func=AF.Copy), reads=[bR], writes=[R])
                                    else:
                                        S.op(dve, lambda: nc.vector.tensor_tensor(out=uo(U), in0=bi_(bO), in1=uo(U), op=ALU.add),
                                             reads=[bO, U], writes=[U])
                                        S.op(dve, lambda: nc.vector.tensor_tensor(out=uo(R), in0=bi_(bR), in1=uo(R), op=ALU.add),
                                             reads=[bR, R], writes=[R])

                            att1(0)
                            att1(1)
                            for pi in range(8):
                                if pi + 2 < 8:
                                    att1(pi + 2)
                                att2(pi)
                        wb = next_w(o_wgc[s_], 128)
                        for tt in range(4):
                            bk = proj(wb, 0, lambda k: hnS[:, k, HALO + tt * 512:HALO + (tt + 1) * 512], 512)
                            t_ = tht[tt % 2]
                            cs = slice(tt * 512, (tt + 1) * 512)
                            S.op(act, lambda: nc.scalar.activation(out=t_[:, :], in_=bk[:, :], func=AF.Tanh, scale=0.5), reads=[bk], writes=[t_])
                            S.op(dve, lambda: nc.vector.scalar_tensor_tensor(out=t_[:, :], in0=t_[:, :], scalar=1.0, in1=bk[:, :],
                                                                            op0=ALU.add, op1=ALU.mult), reads=[bk, t_], writes=[t_])
                            S.op(dve, lambda: nc.vector.reciprocal(out=R[:, cs], in_=R[:, cs]), reads=[R], writes=[R])
                            S.op(dve, lambda: nc.vector.tensor_tensor(out=U[:, cs], in0=U[:, cs], in1=R[:, cs], op=ALU.mult),
                                 reads=[U, R], writes=[U])
                            S.op(dve, lambda: nc.vector.scalar_tensor_tensor(out=yT2[:, s_, cs], in0=U[:, cs], scalar=0.5, in1=t_[:, :],
                                                                            op0=ALU.mult, op1=ALU.mult), reads=[U, t_], writes=[yT2])
                    S.barrier()

                with ExitStack() as es2:
                    def sb2(name, shape, dt):
                        return Buf(es2.enter_context(nc.sbuf_tensor(name, list(shape), dt)))

                    dc = sb2("dc", [128, 2 + OWN], F32)
                    zz = sb2("zz", [128, 2 + OWN], F32)
                    acc = sb2("acc", [128, OWN], F32)
                    th2 = sb2("th2", [128, OWN], F32)
                    own = lambda tt: (lambda k: hnS[:, k, HALO + tt * 512:HALO + (tt + 1) * 512])
                    hal2 = lambda k: hnS[:, k, HALO - 2:HALO]
                    for c in range(8):
                        wb = next_w(o_wd[c], 512)
                        bk = proj(wb, 128, hal2, 2)
                        S.op(act, lambda: nc.scalar.activation(out=dc[:, 0:2], in_=bk[:, 0:2], func=AF.Copy), reads=[bk], writes=[dc])
                        for tt in range(4):
                            bk = proj(wb, 128, own(tt), 512)
                            S.op(act, lambda: nc.scalar.activation(out=dc[:, 2 + tt * 512:2 + (tt + 1) * 512], in_=bk[:, :], func=AF.Copy),
                                 reads=[bk], writes=[dc])
                        bk = proj(wb, 256, hal2, 2)
                        S.op(dve, lambda: nc.vector.tensor_tensor(out=zz[:, 0:2], in0=bk[:, 0:2], in1=dc[:, 0:2], op=ALU.mult),
                             reads=[bk, dc], writes=[zz])
                        for tt in range(4):
                            bk = proj(wb, 256, own(tt), 512)
                            S.op(dve, lambda: nc.vector.tensor_tensor(out=zz[:, 2 + tt * 512:2 + (tt + 1) * 512], in0=bk[:, :],
                                                                     in1=dc[:, 2 + tt * 512:2 + (tt + 1) * 512], op=ALU.mult),
                                 reads=[bk, dc], writes=[zz])
                        S.op(dve, lambda: nc.vector.tensor_scalar(out=acc[:, :], in0=zz[:, 0:OWN], scalar1=cw[:, 3 * c:3 * c + 1], scalar2=None,
                                                                 op0=ALU.mult), reads=[zz, cw], writes=[acc])
                        S.op(dve, lambda: nc.vector.scalar_tensor_tensor(out=acc[:, :], in0=zz[:, 1:OWN + 1], scalar=cw[:, 3 * c + 1:3 * c + 2],
                                                                        in1=acc[:, :], op0=ALU.mult, op1=ALU.add), reads=[zz, cw, acc], writes=[acc])
                        S.op(dve, lambda: nc.vector.scalar_tensor_tensor(out=acc[:, :], in0=zz[:, 2:OWN + 2], scalar=cw[:, 3 * c + 2:3 * c + 3],
                                                                        in1=acc[:, :], op0=ALU.mult, op1=ALU.add), reads=[zz, cw, acc], writes=[acc])
                        for tt in range(4):
                            bk = proj(wb, 384, own(tt), 512)
                            tsl = th2[:, tt * 512:(tt + 1) * 512]
                            S.op(act, lambda: nc.scalar.activation(out=tsl, in_=bk[:, :], func=AF.Tanh, scale=0.5), reads=[bk], writes=[th2])
                            S.op(dve, lambda: nc.vector.scalar_tensor_tensor(out=tsl, in0=tsl, scalar=1.0, in1=bk[:, :], op0=ALU.add, op1=ALU.mult),
                                 reads=[bk, th2], writes=[th2])
                        for tt in range(4):
                            bk = proj(wb, 0, own(tt), 512)
                            asl = acc[:, tt * 512:(tt + 1) * 512]
                            S.op(dve, lambda: nc.vector.tensor_tensor(out=asl, in0=bk[:, :], in1=asl, op=ALU.mult), reads=[bk, acc], writes=[acc])
                        S.op(dve, lambda: nc.vector.scalar_tensor_tensor(out=yT2[:, 8 + c, :], in0=acc[:, :], scalar=0.5, in1=th2[:, :],
                                                                        op0=ALU.mult, op1=ALU.mult), reads=[acc, th2], writes=[yT2])
                    S.barrier()

            with ExitStack() as es3:
                def sb3(name, shape, dt):
                    return Buf(es3.enter_context(nc.sbuf_tensor(name, list(shape), dt)))

                WO2 = [sb3(f"WO2_{i}", [128, 16, 128], BF16) for i in range(8)]
                h1t = [sb3(f"h1t{i}", [128, 8, 512], F32) for i in range(2)]
                sq3 = [sb3(f"sq3{i}", [128, 512], F32) for i in range(2)]
                rs3 = sb3("rs3", [128, 512], F32)
                acc3 = [sb3(f"acc3{i}", [128, 512], F32) for i in range(2)]
                wo_v = o_wout.rearrange("(k p) c -> p k c", p=128)
                for oc in range(8):
                    S.dma(pool, WO2[oc][:, :, :], wo_v[:, :, oc * 128:(oc + 1) * 128], writes=[WO2[oc]])

                def b3a(tt):
                    h_ = h1t[tt % 2]
                    S.dma(sp, h_[:, :, :], h1T_v[:, :, tt * 512:(tt + 1) * 512], writes=[h_])
                    for oc in range(8):
                        bk = S.bank()
                        for kc in range(16):
                            S.op(pe, lambda: nc.tensor.matmul(bk[:, :], lhsT=WO2[oc][:, kc, :],
                                                              rhs=yT2[:, kc, tt * 512:(tt + 1) * 512], start=(kc == 0), stop=(kc == 15)),
                                 reads=[WO2[oc], yT2], writes=[bk], mark=(kc == 15))
                        S.op(dve, lambda: nc.vector.tensor_tensor(out=h_[:, oc, :], in0=bk[:, :], in1=h_[:, oc, :], op=ALU.add),
                             reads=[bk, h_], writes=[h_])
                    a3 = acc3[tt % 2]
                    for k in range(8):
                        q_ = sq3[k % 2]
                        S.op(act, lambda: nc.scalar.activation(out=q_[:, :], in_=h_[:, k, :], func=AF.Square), reads=[h_], writes=[q_])
                        if k == 1:
                            S.op(pool, lambda: nc.gpsimd.tensor_tensor(out=a3[:, :], in0=sq3[0][:, :], in1=sq3[1][:, :], op=ALU.add),
                                 reads=[sq3[0], sq3[1]], writes=[a3])
                        elif k > 1:
                            S.op(pool, lambda: nc.gpsimd.tensor_tensor(out=a3[:, :], in0=a3[:, :], in1=q_[:, :], op=ALU.add),
                                 reads=[a3, q_], writes=[a3])

                def b3b(tt):
                    h_ = h1t[tt % 2]
                    a3 = acc3[tt % 2]
                    bk = S.bank()
                    S.op(pe, lambda: nc.tensor.matmul(bk[:, :], lhsT=ones_f2[:, :], rhs=a3[:, :], start=True, stop=True),
                         reads=[ones_f2, a3], writes=[bk], mark=True)
                    rstd_from(bk, rs3, 512)
                    for k in range(8):
                        S.op(dve, lambda: nc.vector.scalar_tensor_tensor(out=h_[:, k, :], in0=h_[:, k, :], scalar=gF(k), in1=rs3[:, :],
                                                                        op0=ALU.mult, op1=ALU.mult), reads=[h_, rs3, vecB], writes=[h_])
                    S.dma(sp, outT_v[:, :, tt * 512:(tt + 1) * 512], h_[:, :, :], reads=[h_])

                b3a(0)
                for tt in range(4):
                    if tt + 1 < 4:
                        b3a(tt + 1)
                    b3b(tt)

    S.barrier()
    return nc


def host_inputs_A(inp):
    x = np.asarray(inp["x"], dtype=np.float32)
    maps = []
    e_win = np.ascontiguousarray(inp["even_w_in"][0])
    pw = np.asarray(inp["even_pool_w"][0]).reshape(4, 2, 128, 256)
    wsT = np.ascontiguousarray(np.transpose(np.asarray(inp["even_ws"][0]), (2, 0, 1)))
    tril = np.ascontiguousarray(np.broadcast_to((np.arange(128)[:, None] <= np.arange(128)[None, :]).astype(np.float32)[:, None, :],
                                                (128, 4, 128)))
    bs = np.asarray(inp["even_bs"][0]).reshape(1, 512)
    e_wout = np.ascontiguousarray(inp["even_w_out"][0])
    vecs = np.zeros((128, 32), np.float32)
    vecs[:, 0:8] = np.asarray(inp["even_norm"][0]).reshape(8, 128).T
    vecs[:, 8:16] = np.asarray(inp["odd_norm"][0]).reshape(8, 128).T
    vecs[:, 16:24] = np.asarray(inp["even_pool_scale"][0]).reshape(8, 128).T
    vecs[:, 24:32] = np.asarray(inp["final_norm"]).reshape(8, 128).T
    for c in range(8):
        b, q = c // 4, c % 4
        t0 = OWN * q - HALO - PRE
        xt = np.zeros((TH, D), np.float32)
        lo = max(t0, 0)
        xt[lo - t0:, :] = x[b, lo:t0 + TH, :]
        ic = np.zeros((4, 2, 16), np.float32)
        for g, w in enumerate(POOLS):
            ic[g, :, :] = 1.0 / w
            cnt = np.minimum(np.arange(16) + 1, w).astype(np.float32)
            if q == 1:
                ic[g, 0, :] = 1.0 / cnt
            if q == 0:
                ic[g, 1, :] = 1.0 / cnt
        maps.append({
            "xT": np.ascontiguousarray(xt.T), "e_win": e_win, "e_pw": np.ascontiguousarray(pw), "e_wsT": wsT, "e_tril": tril,
            "e_bs": np.ascontiguousarray(bs), "e_wout": e_wout, "vecs": vecs,
            "invc": np.ascontiguousarray(np.broadcast_to(ic.reshape(1, 128), (128, 128))),
        })
    return maps


def host_inputs_B(inp):
    w = np.asarray(inp["odd_w_in"][0], dtype=np.float32)
    wqkv = np.zeros((24, D, 384), np.float32)
    for s_ in range(8):
        for g in range(3):
            for qi in range(3):
                c0 = ((qi * 3 + g) * 8 + s_) * 128
                wqkv[s_ * 3 + g, :, qi * 128:(qi + 1) * 128] = w[:, c0:c0 + 128]
    wgc = np.stack([w[:, 9216 + s_ * 128:9216 + (s_ + 1) * 128] for s_ in range(8)], 0)
    wd = np.stack([np.concatenate([w[:, 10240 + j * 1024 + c * 128:10240 + j * 1024 + (c + 1) * 128] for j in range(4)], 1)
                   for c in range(8)], 0)
    cwv = np.asarray(inp["odd_conv_w"][0], dtype=np.float32)
    cw = np.zeros((128, 24), np.float32)
    for c in range(8):
        for j in range(3):
            cw[:, c * 3 + j] = cwv[j, c * 128:(c + 1) * 128]
    vecs = np.zeros((128, 32), np.float32)
    vecs[:, 24:32] = np.asarray(inp["final_norm"]).reshape(8, 128).T
    kk = np.arange(128)[:, None].astype(np.float64)
    qq = np.arange(128)[None, :].astype(np.float64)
    slopes = 2.0 ** (-8.0 * (np.arange(24) + 1) / 24)
    masks = []
    for hv in (0.0, 1.0):
        m = np.zeros((24, 2, 128, 256), np.float32)
        for s_ in range(8):
            for g in range(3):
                sl = float(np.float32(slopes[g * 8 + s_])) * DILS[g]
                wprev = (kk >= qq) * np.exp(-sl * (qq + 128 - kk))
                wcur = (kk <= qq) * np.exp(-sl * (qq - kk))
                first = np.concatenate([wprev * hv, wcur], 1)
                rest = np.concatenate([wprev, wcur], 1)
                m[s_ * 3 + g, 0] = first
                m[s_ * 3 + g, 1] = rest
        masks.append(m)
    wout = np.ascontiguousarray(inp["odd_w_out"][0], dtype=np.float32)
    maps = []
    for c in range(8):
        q = c % 4
        maps.append({"o_wqkv": wqkv, "o_wgc": np.ascontiguousarray(wgc), "o_wd": np.ascontiguousarray(wd), "o_wout": wout,
                     "o_cw": cw, "o_mask": masks[0 if q == 0 else 1], "vecsB": vecs, "o_ident": np.eye(128, dtype=np.float32)})
    return maps


_NC_CACHE = {}


def kernel(**inputs):
    inp = {k: np.asarray(v) for k, v in inputs.items()}
    if "AB" not in _NC_CACHE:
        _NC_CACHE["AB"] = build_program("AB")
    nc = _NC_CACHE["AB"]
    mA = host_inputs_A(inp)
    mB = host_inputs_B(inp)
    maps = [dict(a, **b) for a, b in zip(mA, mB)]
    res = run_bass_kernel_spmd(nc, maps, core_ids=list(range(8)))
    out = np.zeros((NB, SEQ, D), np.float32)
    for c in range(8):
        b, q = c // 4, c % 4
        out[b, q * OWN:(q + 1) * OWN, :] = np.asarray(res.results[c]["outT"]).T
    return out
```

```python
import numpy as np
from contextlib import ExitStack
import concourse.bass as bass
import concourse.mybir as mybir
from concourse.bass_utils import run_bass_kernel_spmd

F32 = mybir.dt.float32
BF16 = mybir.dt.bfloat16
AF = mybir.ActivationFunctionType
ALU = mybir.AluOpType

D = 1024
SEQ = 8192
NB = 2
OWN = 2048
HALO = 2048
PRE = 16
TH = PRE + HALO + OWN
TA = 256
EPS = 1e-6
POOLS = (2, 4, 8, 16)
DILS = (1, 4, 16)
NDS = 16


class Sem:
    __slots__ = ("h", "i")

    def __init__(self, h, i):
        self.h = h
        self.i = i


class Buf:
    __slots__ = ("t", "writer", "readers")

    def __init__(self, t=None):
        self.t = t
        self.writer = None
        self.readers = []

    def __getitem__(self, k):
        return self.t[k]


class Eng:
    def __init__(self, S, name, e, is_pe=False):
        self.e = e
        self.name = name
        self.sem = S.newsem("s_" + name)
        self.cnt = 0
        self.known = {}
        self.is_pe = is_pe


class Sched:
    def __init__(self, nc):
        self.nc = nc
        self.nsem = 0
        self.pe = Eng(self, "pe", nc.tensor, True)
        self.act = Eng(self, "act", nc.scalar)
        self.dve = Eng(self, "dve", nc.vector)
        self.pool = Eng(self, "pool", nc.gpsimd)
        self.sp = Eng(self, "sp", nc.sync)
        self.engs = [self.pe, self.act, self.dve, self.pool, self.sp]
        self.dsems = [self.newsem(f"d{i}") for i in range(2 * NDS)]
        self.dcnt = [0] * (2 * NDS)
        self.dnext = [0, 0]
        self.nbank = 0
        self.banks = []

    def newsem(self, name):
        s = Sem(self.nc.alloc_semaphore(name), self.nsem)
        self.nsem += 1
        return s

    def _wait(self, eng, tok):
        sem, val = tok
        if eng.known.get(sem.i, 0) >= val:
            return
        eng.e.wait_ge(sem.h, val)
        eng.known[sem.i] = val

    def _deps(self, eng, reads, writes):
        for b in reads:
            if b.writer is not None:
                if b.writer[0] is eng.sem and eng.is_pe:
                    continue
                self._wait(eng, b.writer)
        for b in writes:
            if b.writer is not None and not (b.writer[0] is eng.sem and eng.is_pe):
                self._wait(eng, b.writer)
            for r in b.readers:
                if not (r[0] is eng.sem and eng.is_pe):
                    self._wait(eng, r)

    def _record(self, tok, reads, writes):
        for b in reads:
            b.readers.append(tok)
        for b in writes:
            b.writer = tok
            b.readers = []

    def op(self, eng, fn, reads=(), writes=(), mark=True):
        self._deps(eng, reads, writes)
        ins = fn()
        tok = (eng.sem, eng.cnt + 1)
        if mark:
            ins.then_inc(eng.sem.h, 1)
            eng.cnt += 1
        self._record(tok, reads, writes)
        return ins

    def dma(self, q, out, in_, reads=(), writes=()):
        self._deps(q, reads, writes)
        ring = 1 if q is self.pool else 0
        i = ring * NDS + self.dnext[ring]
        self.dnext[ring] = (self.dnext[ring] + 1) % NDS
        if self.dcnt[i] > 0:
            self._wait(q, (self.dsems[i], self.dcnt[i]))
        ins = q.e.dma_start(out=out, in_=in_)
        self.dcnt[i] += 16
        ins.then_inc(self.dsems[i].h, 16)
        self._record((self.dsems[i], self.dcnt[i]), reads, writes)

    def barrier(self):
        for e in self.engs:
            for f in self.engs:
                if f is not e and f.cnt > 0:
                    self._wait(e, (f.sem, f.cnt))
            for i in range(2 * NDS):
                if self.dcnt[i] > 0:
                    self._wait(e, (self.dsems[i], self.dcnt[i]))

    def bank(self):
        b = self.banks[self.nbank % len(self.banks)]
        self.nbank += 1
        return b

    def bank_pair(self):
        if self.nbank % 2:
            self.nbank += 1
        j = self.nbank % len(self.banks)
        self.nbank += 2
        return self.banks[j], self.banks[j + 1], self.psum_all[:, j:j + 2, :]


def build_program(phases="AB"):
    nc = bass.Bass("TRN2", target_bir_lowering=False)
    S = Sched(nc)
    pe, act, dve, pool, sp = S.pe, S.act, S.dve, S.pool, S.sp
    fused = phases == "AB"

    def dram(name, shape, dt, kind):
        return nc.dram_tensor(name, list(shape), dt, kind=kind).ap()

    mid_kind_out = "Internal" if fused else "ExternalOutput"
    mid_kind_in = "Internal" if fused else "ExternalInput"
    if "A" in phases:
        xT = dram("xT", [D, TH], F32, "ExternalInput")
        e_win = dram("e_win", [D, 5120], F32, "ExternalInput")
        e_pw = dram("e_pw", [4, 2, 128, 256], F32, "ExternalInput")
        e_wsT = dram("e_wsT", [128, 4, 128], F32, "ExternalInput")
        e_tril = dram("e_tril", [128, 4, 128], F32, "ExternalInput")
        e_bs = dram("e_bs", [1, 512], F32, "ExternalInput")
        e_wout = dram("e_wout", [2048, D], F32, "ExternalInput")
        vecs = dram("vecs", [128, 32], F32, "ExternalInput")
        invc = dram("invc", [128, 128], F32, "ExternalInput")
        h1T_d = dram("h1T", [D, OWN], F32, mid_kind_out)
        hnT_d = dram("hnT", [D, HALO + OWN], BF16, mid_kind_out)
    else:
        h1T_d = dram("h1T", [D, OWN], F32, mid_kind_in)
        hnT_d = dram("hnT", [D, HALO + OWN], BF16, mid_kind_in)
    if "B" in phases:
        o_wqkv = dram("o_wqkv", [24, D, 384], F32, "ExternalInput")
        o_wgc = dram("o_wgc", [8, D, 128], F32, "ExternalInput")
        o_wd = dram("o_wd", [8, D, 512], F32, "ExternalInput")
        o_wout = dram("o_wout", [2048, D], F32, "ExternalInput")
        o_cw = dram("o_cw", [128, 24], F32, "ExternalInput")
        o_mask = dram("o_mask", [24, 2, 128, 256], F32, "ExternalInput")
        o_ident = dram("o_ident", [128, 128], F32, "ExternalInput")
        vecsB = dram("vecsB", [128, 32], F32, "ExternalInput")
        outT = dram("outT", [D, OWN], F32, "ExternalOutput")

    psum_all = nc.alloc_psum_tensor("psum_all", [128, 8, 512], F32)
    S.psum_all = psum_all
    S.banks = [Buf(psum_all[:, i, :]) for i in range(8)]

    epsb = Buf(nc.alloc_sbuf_tensor("epsb", [128, 1], F32))
    S.op(dve, lambda: nc.vector.memset(epsb[:, :], EPS), writes=[epsb])

    def rstd_from(bk, rs_buf, w):
        S.op(act, lambda: nc.scalar.activation(out=rs_buf[:, 0:w], in_=bk[:, 0:w], func=AF.Sqrt, bias=epsb[:, 0:1],
                                               scale=1.0 / D), reads=[bk, epsb], writes=[rs_buf])
        S.op(dve, lambda: nc.vector.reciprocal(out=rs_buf[:, 0:w], in_=rs_buf[:, 0:w]), reads=[rs_buf], writes=[rs_buf])

    if "A" in phases:
        with ExitStack() as es:
            def sb(name, shape, dt):
                return Buf(es.enter_context(nc.sbuf_tensor(name, list(shape), dt)))

            WI = [sb(f"WI{i}", [128, 8, 1024], BF16) for i in range(5)]
            WO = sb("WO", [128, 16, D], BF16)
            PW = sb("PW", [128, 4, 2, 256], BF16)
            wsT_f = sb("wsT_f", [128, 4, 128], F32)
            tril_f = sb("tril_f", [128, 4, 128], F32)
            wsTm = sb("wsTm", [128, 4, 128], BF16)
            bsr = sb("bsr", [1, 512], BF16)
            ones_row = sb("ones_row", [1, 128], BF16)
            ones_f = sb("ones_f", [128, 128], F32)
            vec = sb("vec", [128, 32], F32)
            invc_s = sb("invc_s", [128, 128], F32)
            xt = [sb(f"xt{i}", [128, 8, TA], F32) for i in range(3)]
            sq = [sb(f"sq{i}", [128, TA], F32) for i in range(2)]
            rs = sb("rs", [128, TA], F32)
            rs2 = sb("rs2", [128, TA], F32)
            hT = [sb(f"hT{i}", [128, 8, TA], BF16) for i in range(2)]
            aT = [sb(f"aT{i}", [128, PRE + TA], F32) for i in range(8)]
            Tm = [sb(f"Tm{i}", [128, PRE + TA], F32) for i in range(2)]
            pooled = [sb(f"pooled{i}", [128, TA], BF16) for i in range(8)]
            accN = sb("accN", [128, TA], F32)
            accO = sb("accO", [128, TA], F32)
            tmp16 = sb("tmp16", [128, 16], F32)
            sga = [sb(f"sga{i}", [128, TA], F32) for i in range(2)]
            sgb = [sb(f"sgb{i}", [128, TA], F32) for i in range(2)]
            ugb = [sb(f"ugb{i}", [128, TA], F32) for i in range(2)]
            vtm = [sb(f"vtm{i}", [128, 1024], BF16) for i in range(TA // 128)]
            yT = [sb(f"yT{i}", [128, 16, TA], BF16) for i in range(2)]
            hn = [sb(f"hn{i}", [128, 8, TA], BF16) for i in range(1)]

            S.dma(sp, vec[:, :], vecs, writes=[vec])
            S.dma(sp, invc_s[:, :], invc, writes=[invc_s])
            S.dma(sp, wsT_f[:, :, :], e_wsT, writes=[wsT_f])
            S.dma(sp, tril_f[:, :, :], e_tril, writes=[tril_f])
            S.op(dve, lambda: nc.vector.memset(ones_f[:, :], 1.0), writes=[ones_f])
            S.op(dve, lambda: nc.vector.memset(ones_row[:, :], 1.0), writes=[ones_row])
            S.op(dve, lambda: nc.vector.tensor_tensor(out=wsTm[:, :, :], in0=wsT_f[:, :, :], in1=tril_f[:, :, :],
                                                     op=ALU.mult), reads=[wsT_f, tril_f], writes=[wsTm])
            S.dma(pool, bsr[:, :], e_bs, writes=[bsr])
            win_v = e_win.rearrange("(k p) c -> p k c", p=128)
            S.dma(pool, WI[0][:, :, :], win_v[:, :, 0:1024], writes=[WI[0]])
            S.dma(pool, PW[:, :, :, :], e_pw.rearrange("g cc p d -> p g cc d"), writes=[PW])
            for i in (1, 3, 2, 4):
                S.dma(pool, WI[i][:, :, :], win_v[:, :, i * 1024:(i + 1) * 1024], writes=[WI[i]])
            S.dma(pool, WO[:, :, :], e_wout.rearrange("(k p) c -> p k c", p=128), writes=[WO])

            xT_v = xT.rearrange("(k p) t -> p k t", p=128)
            h1T_v = h1T_d.rearrange("(k p) t -> p k t", p=128)
            hnT_v = hnT_d.rearrange("(k p) t -> p k t", p=128)
            gE = lambda k: vec[:, k:k + 1]
            gO = lambda k: vec[:, 8 + k:9 + k]
            psc = lambda k: vec[:, 16 + k:17 + k]

            ntile = 1 + (HALO + OWN) // TA

            def tile_geom(j):
                if j == 0:
                    return 0, PRE
                return PRE + (j - 1) * TA, TA

            def sumsq1(src_buf, w, accb):
                for k in range(8):
                    q_ = sq[k % 2]
                    S.op(act, lambda: nc.scalar.activation(out=q_[:, 0:w], in_=src_buf[:, k, 0:w], func=AF.Square),
                         reads=[src_buf], writes=[q_])
                    if k == 1:
                        S.op(pool, lambda: nc.gpsimd.tensor_tensor(out=accb[:, 0:w], in0=sq[0][:, 0:w], in1=sq[1][:, 0:w], op=ALU.add),
                             reads=[sq[0], sq[1]], writes=[accb])
                    elif k > 1:
                        S.op(pool, lambda: nc.gpsimd.tensor_tensor(out=accb[:, 0:w], in0=accb[:, 0:w], in1=q_[:, 0:w], op=ALU.add),
                             reads=[accb, q_], writes=[accb])

            def sumsq2(accb, w):
                bk = S.bank()
                S.op(pe, lambda: nc.tensor.matmul(bk[:, 0:w], lhsT=ones_f[:, :], rhs=accb[:, 0:w], start=True, stop=True),
                     reads=[ones_f, accb], writes=[bk], mark=True)
                return bk

            def stageN0(j):
                col0, w = tile_geom(j)
                x_ = xt[j % 3]
                S.dma(sp, x_[:, :, 0:w], xT_v[:, :, col0:col0 + w], writes=[x_])

            def stageN1(j):
                col0, w = tile_geom(j)
                x_ = xt[j % 3]
                sumsq1(x_, w, accN)

            def stageN2(j):
                col0, w = tile_geom(j)
                x_ = xt[j % 3]
                bk = sumsq2(accN, w)
                rstd_from(bk, rs, w)
                h_ = hT[j % 2]
                for k in range(8):
                    S.op(dve, lambda: nc.vector.scalar_tensor_tensor(out=h_[:, k, 0:w], in0=x_[:, k, 0:w], scalar=gE(k),
                                                                    in1=rs[:, 0:w], op0=ALU.mult, op1=ALU.mult),
                         reads=[x_, rs, vec], writes=[h_])

            def proj_fm(Wb, c, h_, w):
                bk = S.bank()
                for k in range(8):
                    S.op(pe, lambda: nc.tensor.matmul(bk[:, 0:w], lhsT=Wb[:, k, c * 128:(c + 1) * 128], rhs=h_[:, k, 0:w],
                                                      start=(k == 0), stop=(k == 7)),
                         reads=[Wb, h_], writes=[bk], mark=(k == 7))
                return bk

            def stagePa(j):
                col0, w = tile_geom(j)
                h_ = hT[j % 2]
                for c in range(8):
                    bk = proj_fm(WI[0], c, h_, w)
                    dst0 = 0 if j == 0 else PRE
                    S.op(act, lambda: nc.scalar.activation(out=aT[c][:, dst0:dst0 + w], in_=bk[:, 0:w], func=AF.Copy),
                         reads=[bk], writes=[aT[c]])

            def pool_group(j, gi):
                col0, w = tile_geom(j)
                W_ = PRE + w
                wpool = POOLS[gi]
                for cc in range(2):
                    c = 2 * gi + cc
                    A = aT[c]
                    cur = A
                    lo = 0
                    sh = 1
                    for step in range(gi + 1):
                        nxt = Tm[step % 2]
                        S.op(dve, lambda: nc.vector.tensor_tensor(out=nxt[:, lo + sh:W_], in0=cur[:, lo + sh:W_],
                                                                 in1=cur[:, lo:W_ - sh], op=ALU.add),
                             reads=[cur], writes=[nxt])
                        cur = nxt
                        lo += sh
                        sh *= 2
                    pl = pooled[c]
                    S.op(dve, lambda: nc.vector.scalar_tensor_tensor(out=pl[:, 0:w], in0=cur[:, PRE:W_], scalar=1.0 / wpool,
                                                                    in1=A[:, PRE:W_], op0=ALU.mult, op1=ALU.subtract),
                         reads=[cur, A], writes=[pl])
                    if j in (1, 1 + HALO // TA):
                        pos = 0 if j == 1 else 1
                        o0 = (gi * 2 + pos) * 16
                        S.op(dve, lambda: nc.vector.tensor_tensor(out=tmp16[:, :], in0=cur[:, PRE:PRE + 16],
                                                                 in1=invc_s[:, o0:o0 + 16], op=ALU.mult),
                             reads=[cur, invc_s], writes=[tmp16])
                        S.op(dve, lambda: nc.vector.tensor_tensor(out=pl[:, 0:16], in0=tmp16[:, :],
                                                                 in1=A[:, PRE:PRE + 16], op=ALU.subtract),
                             reads=[tmp16, A, pl], writes=[pl])
                    S.op(pool, lambda: nc.gpsimd.tensor_copy(out=A[:, 0:PRE], in_=A[:, w:w + PRE]),
                         reads=[A], writes=[A])

            def v_proj(j):
                col0, w = tile_geom(j)
                h_ = hT[j % 2]
                for tc in range(w // 128):
                    for half in range(2):
                        bk = S.bank()
                        for k in range(8):
                            S.op(pe, lambda: nc.tensor.matmul(bk[:, :], lhsT=h_[:, k, tc * 128:(tc + 1) * 128],
                                                              rhs=WI[3][:, k, half * 512:(half + 1) * 512],
                                                              start=(k == 0), stop=(k == 7)),
                                 reads=[WI[3], h_], writes=[bk], mark=(k == 7))
                        S.op(act, lambda: nc.scalar.activation(out=vtm[tc][:, half * 512:(half + 1) * 512], in_=bk[:, :],
                                                               func=AF.Copy), reads=[bk], writes=[vtm[tc]])

            def gmlp_cc(j, cc):
                col0, w = tile_geom(j)
                h_ = hT[j % 2]
                y_ = yT[j % 2]
                g = cc // 2
                bu = proj_fm(WI[2], cc, h_, w)
                bg = proj_fm(WI[4], cc, h_, w)
                bm = S.bank()
                for tc in range(w // 128):
                    S.op(pe, lambda: nc.tensor.matmul(bm[:, tc * 128:(tc + 1) * 128], lhsT=vtm[tc][:, cc * 128:(cc + 1) * 128],
                                                      rhs=wsTm[:, g, :], start=True, stop=False),
                         reads=[vtm[tc], wsTm], writes=[bm], mark=False)
                    S.op(pe, lambda: nc.tensor.matmul(bm[:, tc * 128:(tc + 1) * 128], lhsT=ones_row[0:1, :],
                                                      rhs=bsr[0:1, g * 128:(g + 1) * 128], start=False, stop=True),
                         reads=[ones_row, bsr], writes=[bm], mark=(tc == w // 128 - 1))
                sg = sgb[cc % 2]
                ug = ugb[cc % 2]
                S.op(act, lambda: nc.scalar.activation(out=sg[:, 0:w], in_=bg[:, 0:w], func=AF.Silu),
                     reads=[bg], writes=[sg])
                S.op(dve, lambda: nc.vector.tensor_tensor(out=ug[:, 0:w], in0=bu[:, 0:w], in1=sg[:, 0:w], op=ALU.mult),
                     reads=[bu, sg], writes=[ug])
                S.op(dve, lambda: nc.vector.tensor_tensor(out=y_[:, 8 + cc, 0:w], in0=bm[:, 0:w], in1=ug[:, 0:w], op=ALU.mult),
                     reads=[bm, ug], writes=[y_])

            def pool_mix(j, gi):
                col0, w = tile_geom(j)
                h_ = hT[j % 2]
                y_ = yT[j % 2]
                for dch in range(2):
                    co = 2 * gi + dch
                    bm = S.bank()
                    for cc in range(2):
                        S.op(pe, lambda: nc.tensor.matmul(bm[:, 0:w], lhsT=PW[:, gi, cc, dch * 128:(dch + 1) * 128],
                                                          rhs=pooled[2 * gi + cc][:, 0:w],
                                                          start=(cc == 0), stop=(cc == 1)),
                             reads=[PW, pooled[2 * gi + cc]], writes=[bm], mark=(cc == 1))
                    bg = proj_fm(WI[1], co, h_, w)
                    sg = sga[co % 2]
                    S.op(act, lambda: nc.scalar.activation(out=sg[:, 0:w], in_=bg[:, 0:w], func=AF.Silu),
                         reads=[bg], writes=[sg])
                    S.op(dve, lambda: nc.vector.scalar_tensor_tensor(out=y_[:, co, 0:w], in0=bm[:, 0:w], scalar=psc(co),
                                                                    in1=sg[:, 0:w], op0=ALU.mult, op1=ALU.mult),
                         reads=[bm, sg, vec], writes=[y_])

            def stagePb(j):
                v_proj(j)

            def stagePb2(j):
                order = [3, 2, 1, 0]
                for n_, gi in enumerate(order):
                    gmlp_cc(j, 2 * n_)
                    gmlp_cc(j, 2 * n_ + 1)
                    pool_group(j, gi)
                    if n_ >= 1:
                        pool_mix(j, order[n_ - 1])
                pool_mix(j, order[3])

            def stageO1(j):
                col0, w = tile_geom(j)
                x_ = xt[j % 3]
                y_ = yT[j % 2]
                for oc in range(8):
                    bk = S.bank()
                    for kc in range(16):
                        S.op(pe, lambda: nc.tensor.matmul(bk[:, 0:w], lhsT=WO[:, kc, oc * 128:(oc + 1) * 128], rhs=y_[:, kc, 0:w],
                                                          start=(kc == 0), stop=(kc == 15)),
                             reads=[WO, y_], writes=[bk], mark=(kc == 15))
                    S.op(dve, lambda: nc.vector.tensor_tensor(out=x_[:, oc, 0:w], in0=bk[:, 0:w], in1=x_[:, oc, 0:w], op=ALU.add),
                         reads=[bk, x_], writes=[x_])
                sumsq1(x_, w, accO)

            def stageO2(j):
                col0, w = tile_geom(j)
                x_ = xt[j % 3]
                bk = sumsq2(accO, w)
                rstd_from(bk, rs2, w)
                n_ = hn[0]
                for k in range(8):
                    S.op(dve, lambda: nc.vector.scalar_tensor_tensor(out=n_[:, k, 0:w], in0=x_[:, k, 0:w], scalar=gO(k),
                                                                    in1=rs2[:, 0:w], op0=ALU.mult, op1=ALU.mult),
                         reads=[x_, rs2, vec], writes=[n_])
                t0 = (j - 1) * TA
                S.dma(sp, hnT_v[:, :, t0:t0 + w], n_[:, :, 0:w], reads=[n_])
                if t0 >= HALO:
                    S.dma(sp, h1T_v[:, :, t0 - HALO:t0 - HALO + w], x_[:, :, 0:w], reads=[x_])

            stageN0(0)
            stageN0(1)
            stageN1(0)
            stageN2(0)
            stagePa(0)
            stageN1(1)
            stageN2(1)
            for j in range(1, ntile):
                if j + 1 < ntile:
                    stageN0(j + 1)
                stagePa(j)
                stagePb(j)
                if j > 1:
                    stageO2(j - 1)
                if j + 1 < ntile:
                    stageN1(j + 1)
                stagePb2(j)
                if j + 1 < ntile:
                    stageN2(j + 1)
                stageO1(j)
            stageO2(ntile - 1)
            S.barrier()

    if "B" in phases:
        SCALE = 128.0 ** -0.5
        h1T_v = h1T_d.rearrange("(k p) t -> p k t", p=128)
        hnT_v = hnT_d.rearrange("(k p) t -> p k t", p=128)
        outT_v = outT.rearrange("(k p) t -> p k t", p=128)
        with ExitStack() as esB:
            def sbB(name, shape, dt):
                return Buf(esB.enter_context(nc.sbuf_tensor(name, list(shape), dt)))

            yT2 = sbB("yT2", [128, 16, OWN], BF16)
            vecB = sbB("vecB", [128, 32], F32)
            cw = sbB("cw", [128, 24], F32)
            ones_b = sbB("ones_b", [128, 128], BF16)
            ones_f2 = sbB("ones_f2", [128, 128], F32)
            ident = sbB("ident", [128, 128], BF16)
            S.dma(pool, ident[:, :], o_ident, writes=[ident])
            S.dma(sp, vecB[:, :], vecsB, writes=[vecB])
            S.dma(sp, cw[:, :], o_cw, writes=[cw])
            S.op(dve, lambda: nc.vector.memset(ones_b[:, :], 1.0), writes=[ones_b])
            S.op(dve, lambda: nc.vector.memset(ones_f2[:, :], 1.0), writes=[ones_f2])
            gF = lambda k: vecB[:, 24 + k:25 + k]

            with ExitStack() as esH:
                def sbH(name, shape, dt):
                    return Buf(esH.enter_context(nc.sbuf_tensor(name, list(shape), dt)))

                hnS_b = sbH("hnS", [128, 8, HALO + OWN], BF16)
                hnS = hnS_b.t
                hnK = [Buf(hnS) for _ in range(8)]
                wsl = [sbH(f"wsl{i}", [128, 8, 512], BF16) for i in range(2)]
                widx = [0]
                for k in range(8):
                    S.dma(sp, hnS[:, k, :], hnT_v[:, k, :], writes=[hnK[k]])

                def next_w(src_ap, ncols):
                    wb = wsl[widx[0] % 2]
                    widx[0] += 1
                    S.dma(pool, wb[:, :, 0:ncols], src_ap.rearrange("(k p) c -> p k c", p=128), writes=[wb])
                    return wb

                def proj(wb, c0, rhs_fn, w, out_fn=None):
                    bk = S.bank()
                    for k in range(8):
                        o_ap = bk[:, 0:w] if out_fn is None else out_fn(bk)
                        S.op(pe, lambda: nc.tensor.matmul(o_ap, lhsT=wb[:, k, c0:c0 + 128], rhs=rhs_fn(k),
                                                          start=(k == 0), stop=(k == 7)),
                             reads=[wb, hnK[k]], writes=[bk], mark=(k == 7))
                    return bk

                with ExitStack() as es1:
                    def sb1(name, shape, dt):
                        return Buf(es1.enter_context(nc.sbuf_tensor(name, list(shape), dt)))

                    QT = sb1("QT", [128, OWN], BF16)
                    KT = sb1("KT", [128, 4096], BF16)
                    Vt = sb1("Vt", [128, 32, 128], BF16)
                    VT = sb1("VT", [128, 4096], BF16)
                    UR = sb1("UR", [128, 2, OWN], F32)
                    Eb = [sb1(f"Eb{i}", [128, 512], BF16) for i in range(3)]
                    PTb = [sb1(f"PTb{i}", [128, 512], BF16) for i in range(3)]
                    Wm = [sb1(f"Wm{i}", [128, 2, 256], BF16) for i in range(2)]
                    tht = [sb1(f"tht{i}", [128, 512], F32) for i in range(2)]
                    pcount = [0]

                    for s_ in range(8):
                        for g in range(3):
                            hidx = s_ * 3 + g
                            d = DILS[g]
                            wb = next_w(o_wqkv[hidx], 384)
                            wm = Wm[hidx % 2]
                            S.dma(pool, wm[:, :, :], o_mask[hidx].rearrange("f k c -> k f c"), writes=[wm])
                            base = HALO - 128 * d

                            ntok = (d + 16) * 128
                            ntile_k = (ntok + 511) // 512

                            def nat(k, m):
                                st = base + 512 * m
                                w_ = min(512, ntok - 512 * m)
                                return hnS[:, k, st:st + w_], w_

                            def perm_dst(T, m, w_):
                                if d == 1:
                                    return T[:, 512 * m:512 * m + w_]
                                if d == 4:
                                    return T[:, 512 * m:512 * (m + 1)].rearrange("p (r i) -> p i r", r=4)
                                sb_, mm = m // 4, m % 4
                                return T[:, sb_ * 2048:(sb_ + 1) * 2048].rearrange("p (r i) -> p i r", r=16)[:, 32 * mm:32 * mm + 32, :]

                            def nat_src(bk, w_):
                                if d == 1:
                                    return bk[:, 0:w_]
                                return bk[:, 0:w_].rearrange("p (i r) -> p i r", r=d)

                            m_own0 = (128 * d) // 512 if d > 1 else None
                            for tt in range(4):
                                bk = proj(wb, 0, lambda k: hnS[:, k, HALO + tt * 512:HALO + (tt + 1) * 512], 512)
                                if d == 1:
                                    dst = QT[:, tt * 512:(tt + 1) * 512]
                                elif d == 4:
                                    dst = QT[:, tt * 512:(tt + 1) * 512].rearrange("p (r i) -> p i r", r=4)
                                else:
                                    dst = QT[:, :].rearrange("p (r i) -> p i r", r=16)[:, 32 * tt:32 * tt + 32, :]
                                S.op(act, lambda: nc.scalar.activation(out=dst, in_=nat_src(bk, 512), func=AF.Copy),
                                     reads=[bk], writes=[QT])
                            for m in range(ntile_k):
                                w_ = min(512, ntok - 512 * m)
                                bk = proj(wb, 128, lambda k: nat(k, m)[0], w_)
                                S.op(act, lambda: nc.scalar.activation(out=perm_dst(KT, m, w_), in_=nat_src(bk, w_), func=AF.Copy),
                                     reads=[bk], writes=[KT])
                                bk = proj(wb, 256, lambda k: nat(k, m)[0], w_)
                                S.op(dve, lambda: nc.vector.tensor_copy(out=VT[:, 512 * m:512 * m + w_], in_=bk[:, 0:w_]),
                                     reads=[bk], writes=[VT])
                            nblk_all = d + 16
                            bi = 0
                            while bi < nblk_all:
                                nb_ = min(4, nblk_all - bi)
                                bk = S.bank()
                                bkb = bk[:, :].bitcast(BF16)
                                for j in range(nb_):
                                    b_ = bi + j
                                    if d == 1:
                                        src = VT[:, b_ * 128:(b_ + 1) * 128]
                                    else:
                                        sb_, r = b_ // d, b_ % d
                                        src = VT[:, sb_ * 128 * d:(sb_ + 1) * 128 * d].rearrange("p (i r) -> p r i", r=d)[:, r, :]
                                    S.op(pe, lambda: nc.tensor.transpose(out=bkb[:, j * 128:(j + 1) * 128], in_=src, identity=ident[:, :]),
                                         reads=[VT, ident], writes=[bk], mark=(j == nb_ - 1))
                                S.op(dve, lambda: nc.vector.tensor_copy(out=Vt[:, bi:bi + nb_, :],
                                                                        in_=bkb[:, 0:nb_ * 128].rearrange("p (j e) -> p j e", j=nb_)),
                                     reads=[bk], writes=[Vt])
                                bi += nb_
                            pairs = [(quad, pair) for quad in range(4) for pair in range(2)]
                            st = {}

                            def att1(pi):
                                quad, pair = pairs[pi]
                                bS = S.bank()
                                blks = []
                                for j in range(2):
                                    qi = quad * 4 + pair * 2 + j
                                    sb_ = qi // d + 1
                                    r = qi % d
                                    bp = (sb_ - 1) * d + r
                                    bc = sb_ * d + r
                                    blks.append((qi, sb_, bp, bc))
                                    S.op(pe, lambda: nc.tensor.matmul(bS[:, j * 256:j * 256 + 128], lhsT=KT[:, bp * 128:(bp + 1) * 128],
                                                                      rhs=QT[:, qi * 128:(qi + 1) * 128], start=True, stop=True),
                                         reads=[KT, QT], writes=[bS], mark=False)
                                    S.op(pe, lambda: nc.tensor.matmul(bS[:, j * 256 + 128:j * 256 + 256], lhsT=KT[:, bc * 128:(bc + 1) * 128],
                                                                      rhs=QT[:, qi * 128:(qi + 1) * 128], start=True, stop=True),
                                         reads=[KT, QT], writes=[bS], mark=(j == 1))
                                E = Eb[pcount[0] % 3]
                                PT = PTb[pcount[0] % 3]
                                pcount[0] += 1
                                S.op(act, lambda: nc.scalar.activation(out=E[:, :], in_=bS[:, :], func=AF.Exp, scale=SCALE),
                                     reads=[bS], writes=[E])
                                f0 = 0 if blks[0][1] == 1 else 1
                                f1 = 0 if blks[1][1] == 1 else 1
                                if f0 == f1:
                                    S.op(dve, lambda: nc.vector.tensor_tensor(
                                        out=PT[:, :].rearrange("p (j c) -> p j c", j=2), in0=E[:, :].rearrange("p (j c) -> p j c", j=2),
                                        in1=wm[:, f0, :].unsqueeze(1).broadcast_to([128, 2, 256]), op=ALU.mult),
                                         reads=[E, wm], writes=[PT])
                                else:
                                    S.op(dve, lambda: nc.vector.tensor_tensor(out=PT[:, 0:256], in0=E[:, 0:256], in1=wm[:, f0, :],
                                                                             op=ALU.mult), reads=[E, wm], writes=[PT])
                                    S.op(dve, lambda: nc.vector.tensor_tensor(out=PT[:, 256:512], in0=E[:, 256:512], in1=wm[:, f1, :],
                                                                             op=ALU.mult), reads=[E, wm], writes=[PT])
                                st[pi] = (blks, PT)

                            def att2(pi):
                                quad, pair = pairs[pi]
                                blks, PT = st.pop(pi)
                                if pair == 0:
                                    st["bOR"] = S.bank_pair()
                                bO, bR, bOR = st["bOR"]
                                for j in range(2):
                                    qi, sb_, bp, bc = blks[j]
                                    qc = (pair * 2 + j) * 128
                                    S.op(pe, lambda: nc.tensor.matmul(bO[:, qc:qc + 128], lhsT=Vt[:, bp, :], rhs=PT[:, j * 256:j * 256 + 128],
                                                                      start=True, stop=False), reads=[Vt, PT], writes=[bO], mark=False)
                                    S.op(pe, lambda: nc.tensor.matmul(bO[:, qc:qc + 128], lhsT=Vt[:, bc, :], rhs=PT[:, j * 256 + 128:j * 256 + 256],
                                                                      start=False, stop=True), reads=[Vt, PT], writes=[bO], mark=False)
                                    S.op(pe, lambda: nc.tensor.matmul(bR[:, qc:qc + 128], lhsT=ones_b[:, :], rhs=PT[:, j * 256:j * 256 + 128],
                                                                      start=True, stop=False), reads=[ones_b, PT], writes=[bR], mark=False)
                                    S.op(pe, lambda: nc.tensor.matmul(bR[:, qc:qc + 128], lhsT=ones_b[:, :], rhs=PT[:, j * 256 + 128:j * 256 + 256],
                                                                      start=False, stop=True), reads=[ones_b, PT], writes=[bR], mark=(j == 1))
                                if pair == 1:
                                    if d == 1:
                                        dst = UR[:, :, quad * 512:(quad + 1) * 512]
                                        src = bOR
                                    elif d == 4:
                                        dst = UR[:, :, quad * 512:(quad + 1) * 512].rearrange("p u (i r) -> p u r i", r=4)
                                        src = bOR.rearrange("p u (r i) -> p u r i", r=4)
                                    else:
                                        dst = UR[:, :, :].rearrange("p u (i r) -> p u r i", r=16)[:, :, 4 * quad:4 * quad + 4, :]
                                        src = bOR.rearrange("p u (r i) -> p u r i", r=4)
                                    if g == 0:
                                        S.op(act, lambda: nc.scalar.activation(out=dst, in_=src, func=AF.Copy), reads=[bO, bR], writes=[UR])
                                    else:
                                        S.op(dve, lambda: nc.vector.tensor_tensor(out=dst, in0=src, in1=dst, op=ALU.add),
                                             reads=[bO, bR, UR], writes=[UR])

                            att1(0)
                            att1(1)
                            for pi in range(8):
                                if pi + 2 < 8:
                                    att1(pi + 2)
                                att2(pi)
                        wb = next_w(o_wgc[s_], 128)
                        for tt in range(4):
                            bk = proj(wb, 0, lambda k: hnS[:, k, HALO + tt * 512:HALO + (tt + 1) * 512], 512)
                            t_ = tht[tt % 2]
                            cs = slice(tt * 512, (tt + 1) * 512)
                            S.op(act, lambda: nc.scalar.activation(out=t_[:, :], in_=bk[:, :], func=AF.Tanh, scale=0.5), reads=[bk], writes=[t_])
                            S.op(dve, lambda: nc.vector.scalar_tensor_tensor(out=t_[:, :], in0=t_[:, :], scalar=1.0, in1=bk[:, :],
                                                                            op0=ALU.add, op1=ALU.mult), reads=[bk, t_], writes=[t_])
                            S.op(dve, lambda: nc.vector.reciprocal(out=UR[:, 1, cs], in_=UR[:, 1, cs]), reads=[UR], writes=[UR])
                            S.op(dve, lambda: nc.vector.tensor_tensor(out=UR[:, 0, cs], in0=UR[:, 0, cs], in1=UR[:, 1, cs], op=ALU.mult),
                                 reads=[UR], writes=[UR])
                            S.op(dve, lambda: nc.vector.scalar_tensor_tensor(out=yT2[:, s_, cs], in0=UR[:, 0, cs], scalar=0.5, in1=t_[:, :],
                                                                            op0=ALU.mult, op1=ALU.mult), reads=[UR, t_], writes=[yT2])
                    S.barrier()

                with ExitStack() as es2:
                    def sb2(name, shape, dt):
                        return Buf(es2.enter_context(nc.sbuf_tensor(name, list(shape), dt)))

                    dc = sb2("dc", [128, 2 + OWN], F32)
                    zz = sb2("zz", [128, 2 + OWN], F32)
                    acc = sb2("acc", [128, OWN], F32)
                    th2 = sb2("th2", [128, OWN], F32)
                    own = lambda tt: (lambda k: hnS[:, k, HALO + tt * 512:HALO + (tt + 1) * 512])
                    hal2 = lambda k: hnS[:, k, HALO - 2:HALO]
                    for c in range(8):
                        wb = next_w(o_wd[c], 512)
                        bk = proj(wb, 128, hal2, 2)
                        S.op(act, lambda: nc.scalar.activation(out=dc[:, 0:2], in_=bk[:, 0:2], func=AF.Copy), reads=[bk], writes=[dc])
                        for tt in range(4):
                            bk = proj(wb, 128, own(tt), 512)
                            S.op(act, lambda: nc.scalar.activation(out=dc[:, 2 + tt * 512:2 + (tt + 1) * 512], in_=bk[:, :], func=AF.Copy),
                                 reads=[bk], writes=[dc])
                        bk = proj(wb, 256, hal2, 2)
                        S.op(dve, lambda: nc.vector.tensor_tensor(out=zz[:, 0:2], in0=bk[:, 0:2], in1=dc[:, 0:2], op=ALU.mult),
                             reads=[bk, dc], writes=[zz])
                        for tt in range(4):
                            bk = proj(wb, 256, own(tt), 512)
                            S.op(dve, lambda: nc.vector.tensor_tensor(out=zz[:, 2 + tt * 512:2 + (tt + 1) * 512], in0=bk[:, :],
                                                                     in1=dc[:, 2 + tt * 512:2 + (tt + 1) * 512], op=ALU.mult),
                                 reads=[bk, dc], writes=[zz])
                        S.op(dve, lambda: nc.vector.tensor_scalar(out=acc[:, :], in0=zz[:, 0:OWN], scalar1=cw[:, 3 * c:3 * c + 1], scalar2=None,
                                                                 op0=ALU.mult), reads=[zz, cw], writes=[acc])
                        S.op(dve, lambda: nc.vector.scalar_tensor_tensor(out=acc[:, :], in0=zz[:, 1:OWN + 1], scalar=cw[:, 3 * c + 1:3 * c + 2],
                                                                        in1=acc[:, :], op0=ALU.mult, op1=ALU.add), reads=[zz, cw, acc], writes=[acc])
                        S.op(dve, lambda: nc.vector.scalar_tensor_tensor(out=acc[:, :], in0=zz[:, 2:OWN + 2], scalar=cw[:, 3 * c + 2:3 * c + 3],
                                                                        in1=acc[:, :], op0=ALU.mult, op1=ALU.add), reads=[zz, cw, acc], writes=[acc])
                        for tt in range(4):
                            bk = proj(wb, 384, own(tt), 512)
                            tsl = th2[:, tt * 512:(tt + 1) * 512]
                            S.op(act, lambda: nc.scalar.activation(out=tsl, in_=bk[:, :], func=AF.Tanh, scale=0.5), reads=[bk], writes=[th2])
                            S.op(dve, lambda: nc.vector.scalar_tensor_tensor(out=tsl, in0=tsl, scalar=1.0, in1=bk[:, :], op0=ALU.add, op1=ALU.mult),
                                 reads=[bk, th2], writes=[th2])
                        for tt in range(4):
                            bk = proj(wb, 0, own(tt), 512)
                            asl = acc[:, tt * 512:(tt + 1) * 512]
                            S.op(dve, lambda: nc.vector.tensor_tensor(out=asl, in0=bk[:, :], in1=asl, op=ALU.mult), reads=[bk, acc], writes=[acc])
                        S.op(dve, lambda: nc.vector.scalar_tensor_tensor(out=yT2[:, 8 + c, :], in0=acc[:, :], scalar=0.5, in1=th2[:, :],
                                                                        op0=ALU.mult, op1=ALU.mult), reads=[acc, th2], writes=[yT2])
                    S.barrier()

            with ExitStack() as es3:
                def sb3(name, shape, dt):
                    return Buf(es3.enter_context(nc.sbuf_tensor(name, list(shape), dt)))

                WO2 = [sb3(f"WO2_{i}", [128, 16, 128], BF16) for i in range(8)]
                h1t = [sb3(f"h1t{i}", [128, 8, 512], F32) for i in range(2)]
                sq3 = [sb3(f"sq3{i}", [128, 512], F32) for i in range(2)]
                rs3 = sb3("rs3", [128, 512], F32)
                acc3 = [sb3(f"acc3{i}", [128, 512], F32) for i in range(2)]
                wo_v = o_wout.rearrange("(k p) c -> p k c", p=128)
                for oc in range(8):
                    S.dma(pool, WO2[oc][:, :, :], wo_v[:, :, oc * 128:(oc + 1) * 128], writes=[WO2[oc]])

                def b3a(tt):
                    h_ = h1t[tt % 2]
                    S.dma(sp, h_[:, :, :], h1T_v[:, :, tt * 512:(tt + 1) * 512], writes=[h_])
                    for oc in range(8):
                        bk = S.bank()
                        for kc in range(16):
                            S.op(pe, lambda: nc.tensor.matmul(bk[:, :], lhsT=WO2[oc][:, kc, :],
                                                              rhs=yT2[:, kc, tt * 512:(tt + 1) * 512], start=(kc == 0), stop=(kc == 15)),
                                 reads=[WO2[oc], yT2], writes=[bk], mark=(kc == 15))
                        S.op(dve, lambda: nc.vector.tensor_tensor(out=h_[:, oc, :], in0=bk[:, :], in1=h_[:, oc, :], op=ALU.add),
                             reads=[bk, h_], writes=[h_])
                    a3 = acc3[tt % 2]
                    for k in range(8):
                        q_ = sq3[k % 2]
                        S.op(act, lambda: nc.scalar.activation(out=q_[:, :], in_=h_[:, k, :], func=AF.Square), reads=[h_], writes=[q_])
                        if k == 1:
                            S.op(dve, lambda: nc.vector.tensor_tensor(out=a3[:, :], in0=sq3[0][:, :], in1=sq3[1][:, :], op=ALU.add),
                                 reads=[sq3[0], sq3[1]], writes=[a3])
                        elif k > 1:
                            S.op(dve, lambda: nc.vector.tensor_tensor(out=a3[:, :], in0=a3[:, :], in1=q_[:, :], op=ALU.add),
                                 reads=[a3, q_], writes=[a3])

                def b3b(tt):
                    h_ = h1t[tt % 2]
                    a3 = acc3[tt % 2]
                    bk = S.bank()
                    S.op(pe, lambda: nc.tensor.matmul(bk[:, :], lhsT=ones_f2[:, :], rhs=a3[:, :], start=True, stop=True),
                         reads=[ones_f2, a3], writes=[bk], mark=True)
                    rstd_from(bk, rs3, 512)
                    for k in range(8):
                        S.op(dve, lambda: nc.vector.scalar_tensor_tensor(out=h_[:, k, :], in0=h_[:, k, :], scalar=gF(k), in1=rs3[:, :],
                                                                        op0=ALU.mult, op1=ALU.mult), reads=[h_, rs3, vecB], writes=[h_])
                    S.dma(sp, outT_v[:, :, tt * 512:(tt + 1) * 512], h_[:, :, :], reads=[h_])

                b3a(0)
                for tt in range(4):
                    if tt + 1 < 4:
                        b3a(tt + 1)
                    b3b(tt)

    S.barrier()
    return nc


def host_inputs_A(inp):
    x = np.asarray(inp["x"], dtype=np.float32)
    maps = []
    e_win = np.ascontiguousarray(inp["even_w_in"][0])
    pw = np.asarray(inp["even_pool_w"][0]).reshape(4, 2, 128, 256)
    wsT = np.ascontiguousarray(np.transpose(np.asarray(inp["even_ws"][0]), (2, 0, 1)))
    tril = np.ascontiguousarray(np.broadcast_to((np.arange(128)[:, None] <= np.arange(128)[None, :]).astype(np.float32)[:, None, :],
                                                (128, 4, 128)))
    bs = np.asarray(inp["even_bs"][0]).reshape(1, 512)
    e_wout = np.ascontiguousarray(inp["even_w_out"][0])
    vecs = np.zeros((128, 32), np.float32)
    vecs[:, 0:8] = np.asarray(inp["even_norm"][0]).reshape(8, 128).T
    vecs[:, 8:16] = np.asarray(inp["odd_norm"][0]).reshape(8, 128).T
    vecs[:, 16:24] = np.asarray(inp["even_pool_scale"][0]).reshape(8, 128).T
    vecs[:, 24:32] = np.asarray(inp["final_norm"]).reshape(8, 128).T
    for c in range(8):
        b, q = c // 4, c % 4
        t0 = OWN * q - HALO - PRE
        xt = np.zeros((TH, D), np.float32)
        lo = max(t0, 0)
        xt[lo - t0:, :] = x[b, lo:t0 + TH, :]
        ic = np.zeros((4, 2, 16), np.float32)
        for g, w in enumerate(POOLS):
            ic[g, :, :] = 1.0 / w
            cnt = np.minimum(np.arange(16) + 1, w).astype(np.float32)
            if q == 1:
                ic[g, 0, :] = 1.0 / cnt
            if q == 0:
                ic[g, 1, :] = 1.0 / cnt
        maps.append({
            "xT": np.ascontiguousarray(xt.T), "e_win": e_win, "e_pw": np.ascontiguousarray(pw), "e_wsT": wsT, "e_tril": tril,
            "e_bs": np.ascontiguousarray(bs), "e_wout": e_wout, "vecs": vecs,
            "invc": np.ascontiguousarray(np.broadcast_to(ic.reshape(1, 128), (128, 128))),
        })
    return maps


def host_inputs_B(inp):
    w = np.asarray(inp["odd_w_in"][0], dtype=np.float32)
    wqkv = np.zeros((24, D, 384), np.float32)
    for s_ in range(8):
        for g in range(3):
            for qi in range(3):
                c0 = ((qi * 3 + g) * 8 + s_) * 128
                wqkv[s_ * 3 + g, :, qi * 128:(qi + 1) * 128] = w[:, c0:c0 + 128]
    wgc = np.stack([w[:, 9216 + s_ * 128:9216 + (s_ + 1) * 128] for s_ in range(8)], 0)
    wd = np.stack([np.concatenate([w[:, 10240 + j * 1024 + c * 128:10240 + j * 1024 + (c + 1) * 128] for j in range(4)], 1)
                   for c in range(8)], 0)
    cwv = np.asarray(inp["odd_conv_w"][0], dtype=np.float32)
    cw = np.zeros((128, 24), np.float32)
    for c in range(8):
        for j in range(3):
            cw[:, c * 3 + j] = cwv[j, c * 128:(c + 1) * 128]
    vecs = np.zeros((128, 32), np.float32)
    vecs[:, 24:32] = np.asarray(inp["final_norm"]).reshape(8, 128).T
    kk = np.arange(128)[:, None].astype(np.float64)
    qq = np.arange(128)[None, :].astype(np.float64)
    slopes = 2.0 ** (-8.0 * (np.arange(24) + 1) / 24)
    masks = []
    for hv in (0.0, 1.0):
        m = np.zeros((24, 2, 128, 256), np.float32)
        for s_ in range(8):
            for g in range(3):
                sl = float(np.float32(slopes[g * 8 + s_])) * DILS[g]
                wprev = (kk >= qq) * np.exp(-sl * (qq + 128 - kk))
                wcur = (kk <= qq) * np.exp(-sl * (qq - kk))
                first = np.concatenate([wprev * hv, wcur], 1)
                rest = np.concatenate([wprev, wcur], 1)
                m[s_ * 3 + g, 0] = first
                m[s_ * 3 + g, 1] = rest
        masks.append(m)
    wout = np.ascontiguousarray(inp["odd_w_out"][0], dtype=np.float32)
    maps = []
    for c in range(8):
        q = c % 4
        maps.append({"o_wqkv": wqkv, "o_wgc": np.ascontiguousarray(wgc), "o_wd": np.ascontiguousarray(wd), "o_wout": wout,
                     "o_cw": cw, "o_mask": masks[0 if q == 0 else 1], "vecsB": vecs, "o_ident": np.eye(128, dtype=np.float32)})
    return maps


_NC_CACHE = {}


def kernel(**inputs):
    inp = {k: np.asarray(v) for k, v in inputs.items()}
    if "AB" not in _NC_CACHE:
        _NC_CACHE["AB"] = build_program("AB")
    nc = _NC_CACHE["AB"]
    mA = host_inputs_A(inp)
    mB = host_inputs_B(inp)
    maps = [dict(a, **b) for a, b in zip(mA, mB)]
    res = run_bass_kernel_spmd(nc, maps, core_ids=list(range(8)))
    out = np.zeros((NB, SEQ, D), np.float32)
    for c in range(8):
        b, q = c // 4, c % 4
        out[b, q * OWN:(q + 1) * OWN, :] = np.asarray(res.results[c]["outT"]).T
    return out
```
